# Optimizing a Trainium2 kernel written in Bass

```python
import math
import jax, jax.numpy as jnp
from jax import lax
import numpy as np

D_MODEL = 1024
BATCH = 16
SEQ = 2048
DEPTH = 2

CTX_LEN = 256
GRID_W = 64
HEAD_DIM = 64
ROPE_THETA = 10000.0
NORM_EPS = 1e-6
Q_BLOCK = 128

DA_HEADS = 4
DA_QK = DA_HEADS * 2 * HEAD_DIM
DA_V = DA_HEADS * 2 * HEAD_DIM
DA_SCALE = HEAD_DIM ** -0.5
GQA_Q_HEADS = 8
GQA_KV_HEADS = 2
GQA_GROUP = GQA_Q_HEADS // GQA_KV_HEADS
GQA_Q = GQA_Q_HEADS * HEAD_DIM
GQA_KV = GQA_KV_HEADS * HEAD_DIM
GQA_SCALE = HEAD_DIM ** -0.5
CONV_DIM = 512
CONV_WIDTH = 3
N_BRANCH = 3
N_MOD = 6
D_FF = -(-(8 * D_MODEL) // (3 * 256)) * 256
IN_SIZES = (DA_QK, DA_QK, DA_V, GQA_Q, GQA_KV, GQA_KV, CONV_DIM, CONV_DIM, CONV_DIM, N_BRANCH * D_MODEL)
D_IN = sum(IN_SIZES)

kernel_name = "hybrid_diffattn_gqa_shortconv_adaln_prefix_block"


def rmsnorm(x, g):
    xf = x.astype(jnp.float32)
    y = xf * lax.rsqrt(jnp.mean(xf * xf, axis=-1, keepdims=True) + NORM_EPS)
    return (y * g.astype(jnp.float32)).astype(x.dtype)


def modulate(h, shift, scale):
    return h * (1.0 + scale) + shift


def split_cols(z, sizes):
    out = []
    start = 0
    for n in sizes:
        out.append(z[..., start:start + n])
        start += n
    return out


def axial_rope_tables(seq, dtype):
    rows = seq // GRID_W
    row = jnp.repeat(jnp.arange(rows, dtype=jnp.float32), GRID_W)
    col = jnp.tile(jnp.arange(GRID_W, dtype=jnp.float32), rows)
    n_freq = HEAD_DIM // 4
    inv_freq = ROPE_THETA ** (-jnp.arange(n_freq, dtype=jnp.float32) / n_freq)
    ang = jnp.stack([row[:, None] * inv_freq, col[:, None] * inv_freq], axis=1)
    return jnp.cos(ang).astype(dtype), jnp.sin(ang).astype(dtype)


def apply_axial_rope(x, rope):
    cos, sin = rope
    b, s, h, d = x.shape
    xa = x.reshape(b, s, h, 2, 2, d // 4)
    x1, x2 = xa[:, :, :, :, 0], xa[:, :, :, :, 1]
    cs, sn = cos[None, :, None], sin[None, :, None]
    out = jnp.stack([x1 * cs - x2 * sn, x2 * cs + x1 * sn], axis=4)
    return out.reshape(b, s, h, d)


def block_attention(q, k, v, scale):
    b, s, hk, g, d = q.shape
    nb = s // Q_BLOCK
    qb = jnp.moveaxis(q.reshape(b, nb, Q_BLOCK, hk, g, d), 1, 0)

    def one_block(qi):
        sc = jnp.einsum('bqhgd,bkhd->bhgqk', qi, k, preferred_element_type=jnp.float32) * scale
        p = jax.nn.softmax(sc, axis=-1)
        return jnp.einsum('bhgqk,bkhe->bqhge', p.astype(v.dtype), v)

    ob = lax.map(one_block, qb)
    return jnp.moveaxis(ob, 0, 1).reshape(b, s, hk, g, v.shape[-1])


def short_conv(u, w):
    n = u.shape[1]
    up = jnp.pad(u, ((0, 0), (1, 1), (0, 0)))
    return up[:, :n] * w[0] + up[:, 1:n + 1] * w[1] + up[:, 2:] * w[2]


def mixer_inputs(h, w_in, q_norm_g, k_norm_g, rope):
    b, s, _ = h.shape
    a_q, a_k, a_v, b_q, b_k, b_v, c_b, c_c, c_u, gates = split_cols(h @ w_in, IN_SIZES)
    a_q1, a_q2 = jnp.split(a_q.reshape(b, s, DA_HEADS, 2 * HEAD_DIM), 2, axis=-1)
    a_k1, a_k2 = jnp.split(a_k.reshape(b, s, DA_HEADS, 2 * HEAD_DIM), 2, axis=-1)
    a_v = a_v.reshape(b, s, DA_HEADS, 2 * HEAD_DIM)
    b_q = rmsnorm(b_q.reshape(b, s, GQA_Q_HEADS, HEAD_DIM), q_norm_g)
    b_k = rmsnorm(b_k.reshape(b, s, GQA_KV_HEADS, HEAD_DIM), k_norm_g)
    b_v = b_v.reshape(b, s, GQA_KV_HEADS, HEAD_DIM)
    if rope is not None:
        a_q1, a_q2, a_k1, a_k2 = (apply_axial_rope(t, rope) for t in (a_q1, a_q2, a_k1, a_k2))
        b_q, b_k = apply_axial_rope(b_q, rope), apply_axial_rope(b_k, rope)
    return (a_q1, a_q2, b_q), (a_k1, a_k2, a_v, b_k, b_v), (c_b, c_c, c_u, gates)


def mixer_outputs(queries, kv, local, conv_w, lam, lam_init, subln_g, w_a, w_b, w_c, w_out):
    a_q1, a_q2, b_q = queries
    a_k1, a_k2, a_v, b_k, b_v = kv
    c_b, c_c, c_u, gates = local
    b, s = b_q.shape[:2]
    o1 = block_attention(a_q1[:, :, :, None], a_k1, a_v, DA_SCALE)[:, :, :, 0]
    o2 = block_attention(a_q2[:, :, :, None], a_k2, a_v, DA_SCALE)[:, :, :, 0]
    y_a = (rmsnorm(o1 - lam.astype(o1.dtype) * o2, subln_g) * (1.0 - lam_init)).reshape(b, s, DA_V)
    y_b = block_attention(b_q.reshape(b, s, GQA_KV_HEADS, GQA_GROUP, HEAD_DIM), b_k, b_v, GQA_SCALE)
    y_b = y_b.reshape(b, s, GQA_Q)
    y_c = c_b * short_conv(c_c * c_u, conv_w)
    g = jax.nn.sigmoid(gates.astype(jnp.float32)).astype(gates.dtype)
    g_a, g_b, g_c = jnp.split(g, N_BRANCH, axis=-1)
    merged = g_a * (y_a @ w_a) + g_b * (y_b @ w_b) + g_c * (y_c @ w_c)
    return merged @ w_out


def swiglu(h, w_gu, w_down):
    gt, up = jnp.split(h @ w_gu, 2, axis=-1)
    return (jax.nn.silu(gt) * up) @ w_down


def setup_inputs(seed: int = 0) -> dict:
    key = jax.random.key(seed)
    ks = jax.random.split(key, 32)
    nrm = lambda k, shape, s: jax.random.normal(k, shape, jnp.float32) * s
    gain = lambda k, shape: 1.0 + 0.02 * jax.random.normal(k, shape, jnp.float32)
    L, D = DEPTH, D_MODEL
    return {
        "x": nrm(ks[0], (BATCH, SEQ, D), 1.0),
        "c": nrm(ks[1], (BATCH, D), 1.0),
        "ctx": nrm(ks[2], (BATCH, CTX_LEN, D), 1.0),
        "c_ctx": nrm(ks[3], (D,), 1.0),
        "w_mod": nrm(ks[4], (L, D, N_MOD * D), 0.3 * D ** -0.5),
        "b_mod": nrm(ks[5], (L, N_MOD * D), 0.02),
        "norm1_g": gain(ks[6], (L, D)),
        "norm2_g": gain(ks[7], (L, D)),
        "w_in": nrm(ks[8], (L, D, D_IN), D ** -0.5),
        "lam_q1": nrm(ks[9], (L, HEAD_DIM), 0.1),
        "lam_k1": nrm(ks[10], (L, HEAD_DIM), 0.1),
        "lam_q2": nrm(ks[11], (L, HEAD_DIM), 0.1),
        "lam_k2": nrm(ks[12], (L, HEAD_DIM), 0.1),
        "diff_subln_g": gain(ks[13], (L, 2 * HEAD_DIM)),
        "q_norm_g": gain(ks[14], (L, HEAD_DIM)),
        "k_norm_g": gain(ks[15], (L, HEAD_DIM)),
        "conv_w": nrm(ks[16], (L, CONV_WIDTH, CONV_DIM), CONV_WIDTH ** -0.5),
        "w_branch_a": nrm(ks[17], (L, DA_V, D), DA_V ** -0.5),
        "w_branch_b": nrm(ks[18], (L, GQA_Q, D), GQA_Q ** -0.5),
        "w_branch_c": nrm(ks[19], (L, CONV_DIM, D), CONV_DIM ** -0.5),
        "w_out": nrm(ks[20], (L, D, D), D ** -0.5),
        "w_ffn_gu": nrm(ks[21], (L, D, 2 * D_FF), D ** -0.5),
        "w_ffn_down": nrm(ks[22], (L, D_FF, D), D_FF ** -0.5),
        "final_g": gain(ks[23], (D,)),
    }


def reference(x, c, ctx, c_ctx, w_mod, b_mod, norm1_g, norm2_g, w_in, lam_q1, lam_k1, lam_q2, lam_k2,
              diff_subln_g, q_norm_g, k_norm_g, conv_w, w_branch_a, w_branch_b, w_branch_c, w_out,
              w_ffn_gu, w_ffn_down, final_g):
    seq = x.shape[1]
    rope = axial_rope_tables(seq, x.dtype)
    silu_c = jax.nn.silu(c)
    silu_cc = jax.nn.silu(c_ctx)
    for l in range(DEPTH):
        lam_init = 0.8 - 0.6 * math.exp(-0.3 * l)
        lam = (jnp.exp(jnp.sum(lam_q1[l].astype(jnp.float32) * lam_k1[l].astype(jnp.float32)))
               - jnp.exp(jnp.sum(lam_q2[l].astype(jnp.float32) * lam_k2[l].astype(jnp.float32)))
               + lam_init)
        mx = jnp.split(silu_c @ w_mod[l] + b_mod[l], N_MOD, axis=-1)
        mx = [m[:, None, :] for m in mx]
        mc = jnp.split(silu_cc @ w_mod[l] + b_mod[l], N_MOD, axis=-1)
        out_args = (conv_w[l], lam, lam_init, diff_subln_g[l], w_branch_a[l], w_branch_b[l],
                    w_branch_c[l], w_out[l])
        hc = modulate(rmsnorm(ctx, norm1_g[l]), mc[0], mc[1])
        qc, kvc, locc = mixer_inputs(hc, w_in[l], q_norm_g[l], k_norm_g[l], None)
        hx = modulate(rmsnorm(x, norm1_g[l]), mx[0], mx[1])
        qx, kvx, locx = mixer_inputs(hx, w_in[l], q_norm_g[l], k_norm_g[l], rope)
        kv_all = tuple(jnp.concatenate([kc_, kx_], axis=1) for kc_, kx_ in zip(kvc, kvx))
        x = x + mx[2] * mixer_outputs(qx, kv_all, locx, *out_args)
        x = x + mx[5] * swiglu(modulate(rmsnorm(x, norm2_g[l]), mx[3], mx[4]), w_ffn_gu[l], w_ffn_down[l])
        if l < DEPTH - 1:
            ctx = ctx + mc[2] * mixer_outputs(qc, kvc, locc, *out_args)
            ctx = ctx + mc[5] * swiglu(modulate(rmsnorm(ctx, norm2_g[l]), mc[3], mc[4]),
                                       w_ffn_gu[l], w_ffn_down[l])
    return rmsnorm(x, final_g)
```

```python
import math
import contextlib
import numpy as np
import concourse.bass as bass
import concourse.mybir as mybir
from concourse.bass_utils import run_bass_kernel_spmd

F32 = mybir.dt.float32
BF16 = mybir.dt.bfloat16
AF = mybir.ActivationFunctionType
ALU = mybir.AluOpType
AX = mybir.AxisListType

D = 1024
KC = 8
SEQ = 2048
CTX = 256
TOK = CTX + SEQ
NL = 2
NCORES = 8
NSEQ = 2
EPS = 1e-6
DFF = 2816
NJ = 22
GRID_W = 64

CH_AK, CH_BK, CH_AQ, CH_BQ, CH_CC, CH_CU, CH_CB, CH_G, CH_WAB, CH_WC, CH_WOUT, CH_GU, CH_WD = (
    0, 4, 5, 9, 13, 17, 21, 25, 49, 57, 65, 73, 117)
NCHUNK = 141
V_N1G, V_N2G, V_FG, V_CONV, V_SUB, V_QG, V_KG, V_LAM = 0, 16, 32, 40, 64, 66, 70, 74
NV = 74
NVL = 512


class Buf:
    __slots__ = ("w", "r", "excl")

    def __init__(self, excl=False):
        self.w = None
        self.r = []
        self.excl = excl


class Prog:
    ENGS = ("pe", "act", "dve", "pool", "sp")

    def __init__(self, nc, n_dma_sems=32):
        self.nc = nc
        self.ops = {e: [] for e in self.ENGS}
        self.cnt = {e: 0 for e in self.ENGS}
        self.waited = {e: {} for e in self.ENGS}
        self.n_dma_sems = n_dma_sems
        self.dcum = [0] * n_dma_sems
        self.dnext = 0
        self.dnext2 = [0, 0]
        self.self_wait = {"pe": False, "act": True, "dve": True, "pool": True, "sp": False}

    def _need(self, eng, tok):
        if tok is None:
            return
        sid, val = tok
        if sid == eng and not self.self_wait[eng]:
            return
        if self.waited[eng].get(sid, 0) >= val:
            return
        if sid in self.cnt and val > self.cnt[sid]:
            raise RuntimeError("wait on pending (future) token %s %d > %d from %s" % (sid, val, self.cnt[sid], eng))
        self.waited[eng][sid] = val
        self.ops[eng].append(("wait", sid, val))

    def _deps(self, eng, reads, writes):
        for b in reads:
            self._need(eng, b.w)
            if b.excl:
                for t in b.r:
                    if t[0] != eng:
                        self._need(eng, t)
        for b in writes:
            self._need(eng, b.w)
            for t in b.r:
                self._need(eng, t)

    def _upd(self, tok, reads, writes):
        for b in writes:
            b.w = tok
            b.r = []
        for b in reads:
            r = b.r
            for i in range(len(r)):
                if r[i][0] == tok[0]:
                    r[i] = tok
                    break
            else:
                r.append(tok)

    def emit(self, eng, fn, reads=(), writes=(), inc=True):
        self._deps(eng, reads, writes)
        tok = (eng, self.cnt[eng] + 1)
        if inc:
            self.cnt[eng] += 1
        self.ops[eng].append(("op", fn, inc))
        self._upd(tok, reads, writes)
        return tok

    def dma(self, q, out, in_, reads=(), writes=()):
        half = self.n_dma_sems // 2
        qi = 1 if q == "pool" else 0
        k = qi * half + self.dnext2[qi]
        self.dnext2[qi] = (self.dnext2[qi] + 1) % half
        sid = ("d", k)
        if self.dcum[k] > 0:
            self._need(q, (sid, self.dcum[k]))
        self._deps(q, reads, writes)
        self.dcum[k] += 16
        tok = (sid, self.dcum[k])
        self.ops[q].append(("dma", out, in_, k))
        self._upd(tok, reads, writes)
        return tok

    def barrier(self, engs=("pe", "act", "dve")):
        for e in engs:
            for f in engs:
                if f != e and self.cnt[f] > 0:
                    self._need(e, (f, self.cnt[f]))

    def finish(self, q="sp"):
        for k in range(self.n_dma_sems):
            if self.dcum[k] > 0:
                self._need(q, (("d", k), self.dcum[k]))

    def build(self):
        nc = self.nc
        with contextlib.ExitStack() as st:
            esem = {e: st.enter_context(nc.semaphore("s_" + e)) for e in self.ENGS}
            dsem = [st.enter_context(nc.semaphore("d_%d" % i)) for i in range(self.n_dma_sems)]

            def getsem(sid):
                return dsem[sid[1]] if isinstance(sid, tuple) else esem[sid]

            block = st.enter_context(nc.Block())

            def run(ename):
                def f(e):
                    for op in self.ops[ename]:
                        if op[0] == "wait":
                            e.wait_ge(getsem(op[1]), op[2])
                        elif op[0] == "op":
                            ins = op[1](e)
                            if op[2]:
                                ins.then_inc(esem[ename], 1)
                        else:
                            e.dma_start(out=op[1], in_=op[2]).then_inc(dsem[op[3]], 16)
                return f

            block.tensor(run("pe"))
            block.scalar(run("act"))
            block.vector(run("dve"))
            block.gpsimd(run("pool"))
            block.sync(run("sp"))


def lam_init_of(l):
    return 0.8 - 0.6 * math.exp(-0.3 * l)


def build_program(layers, final_norm, out_ctx, nseq=NSEQ, dbg=False):
    nc = bass.Bass("TRN2", target_bir_lowering=False, dynamic_dma_scratch_size=8192)
    nl = len(layers)
    x_d = nc.dram_tensor("x", [nseq, SEQ, D], F32, kind="ExternalInput").ap()
    ctx_d = nc.dram_tensor("ctx", [nseq, CTX, D], F32, kind="ExternalInput").ap()
    cT_d = nc.dram_tensor("cT", [128, KC * 4], F32, kind="ExternalInput").ap()
    wmod_d = nc.dram_tensor("wmod", [NL, 48, 128, 1024], F32, kind="ExternalInput").ap()
    bmod_d = nc.dram_tensor("bmod", [128, NL * 48], F32, kind="ExternalInput").ap()
    vecs_d = nc.dram_tensor("vecs", [128, NV + NVL], F32, kind="ExternalInput").ap()
    rope_d = nc.dram_tensor("rope", [128, 2 * SEQ], F32, kind="ExternalInput").ap()
    cst_d = nc.dram_tensor("cst", [128, 5 * 128], F32, kind="ExternalInput").ap()
    wfm_d = nc.dram_tensor("wfm", [NL, NCHUNK, 128, 1024], F32, kind="ExternalInput").ap()
    wv_d = nc.dram_tensor("wv", [NL, 128, KC * 640], F32, kind="ExternalInput").ap()
    out_d = nc.dram_tensor("out", [nseq, SEQ, D], F32, kind="ExternalOutput").ap()
    if out_ctx:
        octx_d = nc.dram_tensor("ctx_out", [nseq, CTX, D], F32, kind="ExternalOutput").ap()

    P = Prog(nc)
    dbg_list = []

    def dump(key, ap, bufs):
        if not dbg or key in dbg_list:
            return
        dbg_list.append(key)
        shp = list(ap.shape)
        d = nc.dram_tensor("dbg_" + key, shp, ap.dtype, kind="ExternalOutput").ap()
        P.dma("sp", d, ap, reads=bufs)

    def sb(name, shape, dt):
        return nc.alloc_sbuf_tensor("sb_" + name, shape, dt).ap()

    xT = sb("xT", [128, KC, TOK], F32)
    kvreg = sb("kvreg", [128, 24192], BF16)
    gscr = sb("gscr", [128, 8208 + 8192 + 4096], BF16)
    mreg = sb("mreg", [128, 8192], BF16)
    NWS = 6
    wsl = [sb("wsl%d" % i, [128, 1024], BF16) for i in range(NWS)]
    tf = sb("tf", [128, 6, 512], F32)
    tb = sb("tb", [128, 6, 512], BF16)
    sqd = sb("sqd", [128, 4, 512], BF16)
    identF = sb("identF", [128, 128], F32)
    cstb = sb("cstb", [128, 4, 128], BF16)
    onesb = sb("onesb", [128, 128], BF16)
    modt = sb("modt", [128, NL, 48, 4], F32)
    drv = sb("drv", [128, NL, 3, 2, KC], F32)
    vecs = sb("vecs", [128, NV], F32)
    lamt = sb("lamt", [128, NL, 4], F32)
    subs = sb("subs", [128, NL], F32)
    scT = sb("scT", [128, KC, 4], F32)
    bmod = sb("bmod", [128, NL, 48], F32)

    ak = kvreg[:, 0:9216].rearrange("p (c t) -> p c t", c=4)
    bk = kvreg[:, 9216:11520]
    av = kvreg[:, 11520:20736].rearrange("p (k c) -> p k c", k=18)
    bv = kvreg[:, 20736:24192].rearrange("p (k c) -> p k c", k=18)
    act = kvreg[:, 0:22528].rearrange("p (j t) -> p j t", j=NJ)
    hg = gscr[:, 0:8200].rearrange("p (c t) -> p c t", c=8)
    hs = [gscr[:, 0:4096].rearrange("p (c t) -> p c t", c=8), gscr[:, 4096:8192].rearrange("p (c t) -> p c t", c=8)]
    qy = gscr[:, 8208:16400].rearrange("p (c t) -> p c t", c=8)
    yc = gscr[:, 16400:20496].rearrange("p (c t) -> p c t", c=4)
    kw = gscr[:, 8208:13328].rearrange("p (i c) -> p i c", i=5)
    vw = gscr[:, 13328:18448].rearrange("p (k c) -> p k c", k=8)
    merged = mreg.rearrange("p (c t) -> p c t", c=8)
    ropeT = mreg.bitcast(F32).rearrange("p (a t) -> p a t", a=2)
    gF = gscr[:, 0:16384].bitcast(F32)
    Pm, Blk, OnesD, OnesV = cstb[:, 0, :], cstb[:, 1, :], cstb[:, 2, :], cstb[:, 3, :]

    psall = nc.alloc_psum_tensor("psall", [128, 8, 512], F32).ap()
    ps = [psall[:, i, :] for i in range(8)]
    B_ps = [Buf(True) for _ in range(8)]

    B_x = [Buf() for _ in range(5)]
    B_ak, B_bk, B_av, B_bv = Buf(), Buf(), Buf(), Buf()
    B_hs = [Buf(), Buf()]
    B_hg = Buf()
    B_qy = [[Buf(), Buf()] for _ in range(8)]
    B_yc = [Buf() for _ in range(4)]
    B_kw = [Buf() for _ in range(5)]
    B_vw = Buf()
    B_mg = [[Buf(), Buf()] for _ in range(8)]
    B_rope = Buf()
    B_ws = [Buf() for _ in range(NWS)]
    B_tf = [Buf() for _ in range(6)]
    B_tb = [Buf() for _ in range(6)]
    B_sqd = [Buf() for _ in range(4)]
    B_act = [[Buf(), Buf()] for _ in range(NJ)]
    B_c = Buf()
    B_gF = Buf()
    NWM = 7
    B_wm = [Buf() for _ in range(NWM)]
    st = {"ws": 0, "tf": 0, "tb": 0, "ps": 0, "tbp": 0}

    def all_qy():
        return [b for r in B_qy for b in r]

    def all_mg():
        return [b for r in B_mg for b in r]

    def ntf():
        i = 2 + st["tf"] % 4
        st["tf"] = (st["tf"] + 1) % 4
        return tf[:, i, :], B_tf[i]

    def ntb():
        i = st["tb"]
        st["tb"] = (i + 1) % 6
        return tb[:, i, :], B_tb[i]

    def nps():
        nb = 7 if st.get("rsv7") else 8
        i = st["ps"] % nb
        st["ps"] = (i + 1) % nb
        return ps[i], B_ps[i]

    def wload(l, ch, ncols=1024):
        i = st["ws"]
        st["ws"] = (i + 1) % NWS
        P.dma("pool", wsl[i][:, 0:ncols], wfm_d[l, ch, :, 0:ncols], writes=[B_ws[i]])
        return wsl[i], B_ws[i]

    def mm(out, pairs, reads, wbuf):
        n = len(pairs)
        for i, (l, r) in enumerate(pairs):
            P.emit("pe", lambda e, l=l, r=r, i=i: e.matmul(out, lhsT=l, rhs=r, start=(i == 0), stop=(i == n - 1)),
                   reads=reads, writes=[wbuf], inc=(i == n - 1))

    def V(col):
        return vecs[:, col:col + 1]

    def act_(out, in_, func, reads, writes, **kw_):
        P.emit("act", lambda e: e.activation(out=out, in_=in_, func=func, **kw_), reads=reads, writes=writes)

    def tt(out, in0, in1, op, reads, writes):
        P.emit("dve", lambda e: e.tensor_tensor(out=out, in0=in0, in1=in1, op=op), reads=reads, writes=writes)

    def stt(out, in0, scalar, in1, op0, op1, reads, writes):
        P.emit("dve", lambda e: e.scalar_tensor_tensor(out=out, in0=in0, scalar=scalar, in1=in1, op0=op0, op1=op1),
               reads=reads, writes=writes)

    def recip(out, in_, reads, writes):
        P.emit("dve", lambda e: e.reciprocal(out=out, in_=in_), reads=reads, writes=writes)

    def rstd_from(pss, B_pss, n, reads_extra=()):
        sd, B_sd = tf[:, 0, :], B_tf[0]
        act_(sd[:, 0:n], pss[:, 0:n], AF.Ln, [B_pss] + list(reads_extra), [B_sd], bias=epsc[:, 0:1])
        rs, B_rs = tf[:, 1, :], B_tf[1]
        act_(rs[:, 0:n], sd[:, 0:n], AF.Exp, [B_sd], [B_rs], scale=-0.5)
        return rs, B_rs

    if dbg:
        P.emit("dve", lambda e: e.memset(modt.rearrange("p l j s -> p (l j s)"), 0.0), writes=[B_c, B_gF] + B_wm)
        P.emit("dve", lambda e: e.memset(drv.rearrange("p l s a k -> p (l s a k)"), 0.0), writes=[B_c, B_gF] + B_wm)
        P.emit("dve", lambda e: e.memset(lamt.rearrange("p l a -> p (l a)"), 0.0), writes=[B_c, B_gF] + B_wm)
        P.emit("dve", lambda e: e.memset(gscr, 0.0), writes=[B_c, B_gF] + B_wm)
        P.emit("dve", lambda e: e.memset(mreg, 0.0), writes=[B_c, B_gF] + B_wm)
    P.dma("sp", vecs, vecs_d[:, 0:NV], writes=[B_c])
    lamv = tf[:, 0, :]
    P.dma("sp", bmod.rearrange("p l j -> p (l j)"), bmod_d, writes=[B_c])
    P.dma("sp", scT.rearrange("p k s -> p (k s)"), cT_d, writes=[B_c])
    cstF = gF[:, 0:640]
    P.dma("sp", cstF, cst_d, writes=[B_gF])
    P.emit("dve", lambda e: e.tensor_copy(out=identF, in_=cstF[:, 0:128]), reads=[B_gF], writes=[B_c])
    P.emit("dve", lambda e: e.tensor_copy(out=cstb.rearrange("p a c -> p (a c)"), in_=cstF[:, 128:640]), reads=[B_gF], writes=[B_c])
    P.emit("dve", lambda e: e.memset(onesb, 1.0), writes=[B_c])
    epsc = sb("epsc", [128, 1], F32)
    P.emit("dve", lambda e: e.memset(epsc, EPS), writes=[B_c])
    act_(scT.rearrange("p k s -> p (k s)"), scT.rearrange("p k s -> p (k s)"), AF.Silu, [B_c], [B_c])
    def mod_prologue():
      if True:
        P.dma("sp", lamv, vecs_d[:, NV:NV + NVL], writes=[B_tf[0]])
        wmF = [gF[:, 1024 * (1 + i):1024 * (2 + i)] for i in range(NWM)]
        for l in layers:
            pm, B_pm = nps()
            for j in range(48):
                s_ = (l * 48 + j) % NWM
                P.dma("sp" if j % 2 == 0 else "pool", wmF[s_], wmod_d[l, j], writes=[B_wm[s_]])
                mm(pm[:, j * 4:(j + 1) * 4], [(wmF[s_][:, kc * 128:(kc + 1) * 128], scT[:, kc, :]) for kc in range(KC)],
                   [B_wm[s_], B_c], B_pm)
            for s_ in range(4):
                tt(modt[:, l, :, s_], pm[:, 0:192].rearrange("p (j s) -> p j s", s=4)[:, :, s_], bmod[:, l, :], ALU.add,
                   [B_pm, B_c], [B_c])
            for s_ in range(3):
                for a, (msc, vg) in enumerate(((8, V_N1G), (32, V_N2G))):
                    stt(drv[:, l, s_, a, :], modt[:, l, msc:msc + 8, s_], 1.0, vecs[:, vg + l * 8:vg + l * 8 + 8], ALU.add, ALU.mult,
                        [B_c], [B_c])
            for a in range(2):
                t_, B_t = ntf()
                base = l * 256 + a * 128
                tt(t_[:, 0:64], lamv[:, base:base + 64], lamv[:, base + 64:base + 128], ALU.mult, [B_tf[0]], [B_t])
                P.emit("dve", lambda e, t_=t_, a=a, l=l: e.tensor_reduce(out=lamt[:, l, 2 + a:3 + a], in_=t_[:, 0:64], axis=AX.X, op=ALU.add),
                       reads=[B_t], writes=[B_c])
            act_(lamt[:, l, 2:4], lamt[:, l, 2:4], AF.Exp, [B_c], [B_c])
            li = lam_init_of(l)
            stt(lamt[:, l, 1:2], lamt[:, l, 3:4], -li, lamt[:, l, 2:3], ALU.add, ALU.subtract, [B_c], [B_c])
            P.emit("dve", lambda e, l=l, li=li: e.tensor_scalar(out=subs[:, l:l + 1], in0=vecs[:, V_SUB + l:V_SUB + l + 1], scalar1=1.0 - li,
                                                        scalar2=None, op0=ALU.mult), reads=[B_c], writes=[B_c])


    def mod_ap(l, m, kc, s_):
        return modt[:, l, m * 8 + kc, s_:s_ + 1]

    def norm_parts(l, s_, which, xoff, n, dst, B_dst, B_xs, sqbufs=None, bank=None):
        state = {}

        def get_bank():
            if "pss" not in state:
                state["pss"] = (ps[bank], B_ps[bank]) if bank is not None else nps()
            return state["pss"]

        def sq(k):
            if sqbufs is None:
                b, B_b = ntb()
            else:
                b, B_b = sqbufs[k % len(sqbufs)]
            state[k] = (b, B_b)
            act_(b[:, 0:n], xT[:, k, xoff:xoff + n], AF.Square, B_xs, [B_b])

        def mmk(k):
            pss, B_pss = get_bank()
            b, B_b = state[k]
            P.emit("pe", lambda e: e.matmul(pss[:, 0:n], lhsT=OnesD, rhs=b[:, 0:n], start=(k == 0), stop=(k == KC - 1)),
                   reads=[B_b, B_c], writes=[B_pss], inc=True)

        def fin():
            pss, B_pss = get_bank()
            rs, B_rs = rstd_from(pss, B_pss, n)
            msh = 0 if which == 0 else 3
            for kc in range(KC):
                t_, B_t = ntf()
                stt(t_[:, 0:n], xT[:, kc, xoff:xoff + n], drv[:, l, s_, which, kc:kc + 1], rs[:, 0:n], ALU.mult, ALU.mult,
                    B_xs + [B_rs, B_c], [B_t])
                act_(dst[:, kc, 0:n], t_[:, 0:n], AF.Identity, [B_t, B_c], [B_dst], bias=mod_ap(l, msh, kc, s_), scale=1.0)
        return sq, mmk, fin

    def norm_tile(l, s_, which, xoff, n, dst, B_dst, B_xs):
        sq, mmk, fin = norm_parts(l, s_, which, xoff, n, dst, B_dst, B_xs)
        for kc in range(KC):
            sq(kc)
            mmk(kc)
        fin()

    def proj_fm(w, B_w, src, B_src, n, ntiles=8):
        pz, B_pz = nps()
        mm(pz[:, 0:n], [(w[:, kc * 128:(kc + 1) * 128], src(kc)) for kc in range(ntiles)], [B_w] + B_src, B_pz)
        return pz, B_pz

    def qk_post(pz, B_pz, n, lat_off, gcol, normed, dst, B_dst):
        rs = None
        roped = lat_off is not None
        if normed:
            z2, B_z2 = ntb()
            act_(z2[:, 0:n], pz[:, 0:n], AF.Square, [B_pz], [B_z2])
        if roped:
            zb, B_zb = ntb()
            act_(zb[:, 0:n], pz[:, 0:n], AF.Copy, [B_pz], [B_zb])
        if normed:
            pss, B_pss = nps()
            mm(pss[:, 0:n], [(Blk, z2[:, 0:n])], [B_z2, B_c], B_pss)
        if roped:
            psw, B_psw = nps()
            mm(psw[:, 0:n], [(Pm, zb[:, 0:n])], [B_zb, B_c], B_psw)
        if normed:
            rs, B_rs = rstd_from(pss, B_pss, n)
        if not roped:
            if normed:
                stt(dst, pz[:, 0:n], V(gcol), rs[:, 0:n], ALU.mult, ALU.mult, [B_pz, B_rs, B_c], [B_dst])
            else:
                act_(dst, pz[:, 0:n], AF.Copy, [B_pz], [B_dst])
            return
        C = ropeT[:, 0, lat_off:lat_off + n]
        S = ropeT[:, 1, lat_off:lat_off + n]
        t1, B_t1 = ntf()
        t2, B_t2 = ntf()
        if normed:
            stt(t1[:, 0:n], pz[:, 0:n], V(gcol), C, ALU.mult, ALU.mult, [B_pz, B_rope, B_c], [B_t1])
            stt(t2[:, 0:n], psw[:, 0:n], V(gcol + 1), S, ALU.mult, ALU.mult, [B_psw, B_rope, B_c], [B_t2])
            tt(t1[:, 0:n], t1[:, 0:n], t2[:, 0:n], ALU.add, [B_t1, B_t2], [B_t1])
            tt(dst, t1[:, 0:n], rs[:, 0:n], ALU.mult, [B_t1, B_rs], [B_dst])
        else:
            tt(t1[:, 0:n], pz[:, 0:n], C, ALU.mult, [B_pz, B_rope], [B_t1])
            tt(t2[:, 0:n], psw[:, 0:n], S, ALU.mult, [B_psw, B_rope], [B_t2])
            tt(dst, t1[:, 0:n], t2[:, 0:n], ALU.add, [B_t1, B_t2], [B_dst])

    def run_items(items):
        pend = None
        for (A, Bf) in items:
            r = A()
            if pend is not None:
                pend[0](*pend[1])
            pend = (Bf, r)
        if pend is not None:
            pend[0](*pend[1])

    def load_rope(extra_writes, lo=0, nn=SEQ):
        P.dma("sp", ropeT[:, :, lo:lo + nn], rope_d.rearrange("p (a t) -> p a t", a=2)[:, :, lo:lo + nn],
              writes=[B_rope] + all_mg() + extra_writes)

    TILES = [(0, 256), (256, 512), (768, 512), (1280, 512), (1792, 512)]

    def kv_phase(l, q, s_lat):
        for i in range(5):
            P.dma("pool", kw[:, i, :], wfm_d[l, CH_AK + i], writes=[B_kw[i]] + (all_qy() + B_yc + [B_gF] + B_wm if i == 0 else []))
        P.dma("pool", vw, wv_d[l].rearrange("p (k c) -> p k c", k=8), writes=[B_vw])
        load_rope([])
        P.emit("dve", lambda e: e.memset(bv[:, :, 64:128], 1.0), writes=[B_bv])
        st["rsv7"] = True

        def kv_norm(ti):
            off, n = TILES[ti]
            norm_tile(l, 2 if ti == 0 else s_lat, 0, off, n, hs[ti % 2], B_hs[ti % 2], [B_x[ti]])
        kv_norm(0)
        for ti, (off, n) in enumerate(TILES):
            h, B_h = hs[ti % 2], B_hs[ti % 2]
            lat_off = None if ti == 0 else off - CTX
            items = []
            for c in range(5):
                def A(c=c, h=h, B_h=B_h, n=n):
                    return proj_fm(kw[:, c, :], B_kw[c], lambda kc: h[:, kc, 0:n], [B_h], n)

                def Bf(pz, B_pz, c=c, n=n, off=off, lat_off=lat_off):
                    if c < 4:
                        qk_post(pz, B_pz, n, lat_off, 0, False, ak[:, c, off:off + n], B_ak)
                    else:
                        qk_post(pz, B_pz, n, lat_off, V_KG + l * 2, True, bk[:, off:off + n], B_bk)
                items.append((A, Bf))
            for sub in range(n // 128):
                kci = off // 128 + sub

                def A(sub=sub, h=h, B_h=B_h):
                    pv, B_pv = nps()
                    mm(pv, [(h[:, kc, sub * 128:(sub + 1) * 128], vw[:, kc, 0:512]) for kc in range(KC)], [B_h, B_vw], B_pv)
                    pv2, B_pv2 = nps()
                    mm(pv2[:, 0:128], [(h[:, kc, sub * 128:(sub + 1) * 128], vw[:, kc, 512:640]) for kc in range(KC)], [B_h, B_vw], B_pv2)
                    return (pv, pv2), (B_pv, B_pv2)

                def Bf(pvs, Bs, kci=kci):
                    pv, pv2 = pvs
                    B_pv, B_pv2 = Bs
                    act_(av[:, kci, :], pv, AF.Copy, [B_pv], [B_av])
                    P.emit("dve", lambda e: e.tensor_copy(out=bv[:, kci, 0:64], in_=pv2[:, 0:64]), reads=[B_pv2], writes=[B_bv])
                    P.emit("dve", lambda e: e.tensor_copy(out=bv[:, kci, 128:192], in_=pv2[:, 64:128]), reads=[B_pv2], writes=[B_bv])
                items.append((A, Bf))
            if ti + 1 < len(TILES):
                off2, n2 = TILES[ti + 1]
                nsq, nmm, nfin = norm_parts(l, s_lat, 0, off2, n2, hs[(ti + 1) % 2], B_hs[(ti + 1) % 2], [B_x[ti + 1]],
                                            sqbufs=[(sqd[:, i, :], B_sqd[i]) for i in range(4)], bank=7)
                wrapped = []
                for j, (A, Bf) in enumerate(items):
                    def A2(A=A, j=j):
                        if 1 <= j <= 4:
                            nmm(2 * j - 2)
                            nmm(2 * j - 1)
                        if j <= 3:
                            nsq(2 * j)
                            nsq(2 * j + 1)
                        if j == 5:
                            nfin()
                        return A()
                    wrapped.append((A2, Bf))
                items = wrapped
            run_items(items)
        st["rsv7"] = False
        P.barrier()
        dump("xT", xT.rearrange("p c t -> p (c t)"), B_x)
        dump("kv", kvreg, [B_ak, B_bk, B_av, B_bv])

    def attention(l, main, nkc):
        for ti, (qo, n) in enumerate(main):
            for h in range(8):
                diff = h < 4
                SA = [(ps[0], B_ps[0]), (ps[2], B_ps[2])]
                SB = [(ps[1], B_ps[1]), (ps[3], B_ps[3])]
                kT = (lambda kc, h=h: ak[:, h, kc * 128:(kc + 1) * 128]) if diff else (lambda kc: bk[:, kc * 128:(kc + 1) * 128])
                B_k = B_ak if diff else B_bk
                qT = qy[:, h, qo:qo + n]
                B_q = B_qy[h][ti]
                gA, gB = (4, 5) if h % 2 == 0 else (6, 7)

                def smm(kc):
                    a, B_a = SA[kc % 2]
                    b, B_b = SB[kc % 2]
                    kt = kT(kc)
                    mm(a[:, 0:n], [(kt[0:64, :], qT[0:64, :])], [B_k, B_q], B_a)
                    mm(b[:, 0:n], [(kt[64:128, :], qT[64:128, :])], [B_k, B_q], B_b)

                smm(0)
                for kc in range(nkc):
                    if kc + 1 < nkc:
                        smm(kc + 1)
                    a, B_a = SA[kc % 2]
                    b, B_b = SB[kc % 2]
                    pi = 2 * (st["tbp"] % 3)
                    st["tbp"] += 1
                    p1, B_p1, p2, B_p2 = tb[:, pi, :], B_tb[pi], tb[:, pi + 1, :], B_tb[pi + 1]
                    act_(p1[:, 0:n], a[:, 0:n], AF.Exp, [B_a], [B_p1], scale=0.125)
                    act_(p2[:, 0:n], b[:, 0:n], AF.Exp, [B_b], [B_p2], scale=0.125)
                    f, la = (kc == 0), (kc == nkc - 1)

                    def acc(bank, lhsT, rhs, rd, f=f, la=la):
                        P.emit("pe", lambda e: e.matmul(ps[bank][:, 0:n], lhsT=lhsT, rhs=rhs, start=f, stop=la),
                               reads=rd, writes=[B_ps[bank]], inc=la)
                    if diff:
                        vt = av[:, kc, h * 128:(h + 1) * 128]
                        acc(4, vt, p1[:, 0:n], [B_av, B_p1])
                        acc(5, onesb, p1[:, 0:n], [B_c, B_p1])
                        acc(6, vt, p2[:, 0:n], [B_av, B_p2])
                        acc(7, onesb, p2[:, 0:n], [B_c, B_p2])
                    else:
                        acc(gA, bv[:, kc, 0:128], p1[:, 0:n], [B_bv, B_p1])
                        acc(gB, bv[:, kc, 64:192], p2[:, 0:n], [B_bv, B_p2])
                if diff:
                    sa, B_sa = ntf()
                    sb2, B_sb2 = ntf()
                    t1, B_t1 = ntf()
                    t2, B_t2 = ntf()
                    P.emit("dve", lambda e, sa=sa: e.tensor_copy(out=sa[:, 0:n], in_=ps[5][:, 0:n]), reads=[B_ps[5]], writes=[B_sa])
                    P.emit("dve", lambda e, sb2=sb2: e.tensor_copy(out=sb2[:, 0:n], in_=ps[7][:, 0:n]), reads=[B_ps[7]], writes=[B_sb2])
                    tt(t1[:, 0:n], ps[4][:, 0:n], sb2[:, 0:n], ALU.mult, [B_ps[4], B_sb2], [B_t1])
                    tt(t2[:, 0:n], ps[6][:, 0:n], sa[:, 0:n], ALU.mult, [B_ps[6], B_sa], [B_t2])
                    stt(t1[:, 0:n], t2[:, 0:n], lamt[:, l, 1:2], t1[:, 0:n], ALU.mult, ALU.add, [B_t1, B_t2, B_c], [B_t1])
                    tt(sa[:, 0:n], sa[:, 0:n], sb2[:, 0:n], ALU.mult, [B_sa, B_sb2], [B_sa])
                    pi = 2 * (st["tbp"] % 3)
                    st["tbp"] += 1
                    d2, B_d2 = tb[:, pi, :], B_tb[pi]
                    tt(d2[:, 0:n], t1[:, 0:n], t1[:, 0:n], ALU.mult, [B_t1], [B_d2])
                    mm(ps[5][:, 0:n], [(OnesV, d2[:, 0:n])], [B_d2, B_c], B_ps[5])
                    stt(sb2[:, 0:n], sa[:, 0:n], EPS, sa[:, 0:n], ALU.mult, ALU.mult, [B_sa], [B_sb2])
                    tt(sb2[:, 0:n], ps[5][:, 0:n], sb2[:, 0:n], ALU.add, [B_ps[5], B_sb2], [B_sb2])
                    act_(sb2[:, 0:n], sb2[:, 0:n], AF.Ln, [B_sb2], [B_sb2])
                    act_(sb2[:, 0:n], sb2[:, 0:n], AF.Exp, [B_sb2], [B_sb2], scale=-0.5)
                    stt(qT, t1[:, 0:n], subs[:, l:l + 1], sb2[:, 0:n], ALU.mult, ALU.mult, [B_t1, B_sb2, B_c], [B_q])
                else:
                    r1, B_r1 = ntf()
                    recip(r1[0:64, 0:n], ps[gA][64:128, 0:n], [B_ps[gA]], [B_r1])
                    recip(r1[64:128, 0:n], ps[gB][0:64, 0:n], [B_ps[gB]], [B_r1])
                    tt(qT[0:64, :], ps[gA][0:64, 0:n], r1[0:64, 0:n], ALU.mult, [B_ps[gA], B_r1], [B_q])
                    tt(qT[64:128, :], ps[gB][64:128, 0:n], r1[64:128, 0:n], ALU.mult, [B_ps[gB], B_r1], [B_q])

    def group_norm(l, s_, pieces):
        for (off, n, hc) in pieces:
            xt = [B_x[i] for i, (o2, n2) in enumerate(TILES) if o2 < off + n and off < o2 + n2]
            norm_tile(l, s_, 0, off, n, hg[:, :, hc:hc + n], B_hg, xt)

    def group_phase(l, s_, pieces, main, halo_side, lat0, nkc):
        is_lat = lat0 is not None
        if is_lat:
            load_rope([], lat0, 1024)
        items = []
        for c in range(8):
            for ti, (hc, n, xi) in enumerate(main):
                def A(c=c, ti=ti, hc=hc, n=n):
                    if ti == 0:
                        qw[c] = wload(l, CH_AQ + c)
                    w, B_w = qw[c]
                    return proj_fm(w, B_w, lambda kc: hg[:, kc, hc:hc + n], [B_hg], n)

                def Bf(pz, B_pz, c=c, ti=ti, hc=hc, n=n):
                    lo = (lat0 + hc) if is_lat else None
                    qk_post(pz, B_pz, n, lo, V_QG + l * 2, c >= 4, qy[:, c, hc:hc + n], B_qy[c][ti])
                items.append((A, Bf))
        qw = {}
        run_items(items)
        dump("q_%s" % halo_side, gscr[:, 8208:16400], all_qy())
        attention(l, [(hc, n) for (hc, n, xi) in main], nkc)
        dump("y_%s" % halo_side, gscr[:, 8208:16400], all_qy())
        st["ps"] = 0
        ntot = sum(n for (hc, n, xi) in main)
        for j in range(4):
            wc_, B_wc = wload(l, CH_CC + j)
            wu_, B_wu = wload(l, CH_CU + j)
            pext = tf[:, 0:3, :].rearrange("p a t -> p (a t)")
            B_pe = B_tf[0:3]
            for (off, n, hc) in pieces:
                pc, B_pc = proj_fm(wc_, B_wc, lambda kc: hg[:, kc, hc:hc + n], [B_hg], n)
                pu, B_pu = proj_fm(wu_, B_wu, lambda kc: hg[:, kc, hc:hc + n], [B_hg], n)
                if hc >= ntot:
                    dcol = 0 if halo_side == "left" else ntot + 1
                else:
                    dcol = 1 + hc
                cc_, B_cc = tf[:, 3, :], B_tf[3]
                act_(cc_[:, 0:n], pc[:, 0:n], AF.Copy, [B_pc], [B_cc])
                tt(pext[:, dcol:dcol + n], cc_[:, 0:n], pu[:, 0:n], ALU.mult, [B_cc, B_pu], B_pe)
            zc = []
            if halo_side != "left":
                zc.append(0)
            if halo_side != "right":
                zc.append(ntot + 1)
            for z in zc:
                P.emit("dve", lambda e, z=z: e.memset(pext[:, z:z + 1], 0.0), writes=B_pe)
            wb_, B_wb = wload(l, CH_CB + j)
            for ti, (hc, n, xi) in enumerate(main):
                pb, B_pb = proj_fm(wb_, B_wb, lambda kc: hg[:, kc, hc:hc + n], [B_hg], n)
                q_, B_q_ = tf[:, 4 + ti % 2, :], B_tf[4 + ti % 2]
                cw = V_CONV + l * 12 + j
                act_(q_[:, 0:n], pext[:, 1 + hc:1 + hc + n], AF.Identity, B_pe + [B_c], [B_q_], scale=V(cw + 4))
                stt(q_[:, 0:n], pext[:, hc:hc + n], V(cw), q_[:, 0:n], ALU.mult, ALU.add, B_pe + [B_c, B_q_], [B_q_])
                stt(q_[:, 0:n], pext[:, 2 + hc:2 + hc + n], V(cw + 8), q_[:, 0:n], ALU.mult, ALU.add, B_pe + [B_c, B_q_], [B_q_])
                tt(yc[:, j, hc:hc + n], pb[:, 0:n], q_[:, 0:n], ALU.mult, [B_pb, B_q_], [B_yc[j]])
        dump("yc_%s" % halo_side, gscr[:, 16400:20496], B_yc)
        st["tf"] = 0
        for m in range(8):
            wg = [wload(l, CH_G + 3 * m + b) for b in range(3)]
            wab, B_wab = wload(l, CH_WAB + m)
            for ti, (hc, n, xi) in enumerate(main):
                sg = []
                for b in range(3):
                    pg, B_pg = proj_fm(wg[b][0], wg[b][1], lambda kc: hg[:, kc, hc:hc + n], [B_hg], n)
                    s__, B_s = tf[:, b, :], B_tf[b]
                    act_(s__[:, 0:n], pg[:, 0:n], AF.Sigmoid, [B_pg], [B_s])
                    sg.append((s__, B_s))
                if ti == 0:
                    wcm, B_wcm = wload(l, CH_WC + m, 512)
                pa, B_pa = nps()
                mm(pa[:, 0:n], [(wab[:, kc * 128:(kc + 1) * 128], qy[:, kc, hc:hc + n]) for kc in range(4)],
                   [B_wab] + [B_qy[kc][ti] for kc in range(4)], B_pa)
                pb2, B_pb2 = nps()
                mm(pb2[:, 0:n], [(wab[:, (4 + kc) * 128:(5 + kc) * 128], qy[:, 4 + kc, hc:hc + n]) for kc in range(4)],
                   [B_wab] + [B_qy[4 + kc][ti] for kc in range(4)], B_pb2)
                pc2, B_pc2 = nps()
                mm(pc2[:, 0:n], [(wcm[:, kc * 128:(kc + 1) * 128], yc[:, kc, hc:hc + n]) for kc in range(4)],
                   [B_wcm] + B_yc, B_pc2)
                t1, B_t1 = tf[:, 3, :], B_tf[3]
                t2, B_t2 = tf[:, 4, :], B_tf[4]
                tt(t1[:, 0:n], pa[:, 0:n], sg[0][0][:, 0:n], ALU.mult, [B_pa, sg[0][1]], [B_t1])
                tt(t2[:, 0:n], pb2[:, 0:n], sg[1][0][:, 0:n], ALU.mult, [B_pb2, sg[1][1]], [B_t2])
                tt(t1[:, 0:n], t1[:, 0:n], t2[:, 0:n], ALU.add, [B_t1, B_t2], [B_t1])
                tt(t2[:, 0:n], pc2[:, 0:n], sg[2][0][:, 0:n], ALU.mult, [B_pc2, sg[2][1]], [B_t2])
                tt(merged[:, m, hc:hc + n], t1[:, 0:n], t2[:, 0:n], ALU.add, [B_t1, B_t2], [B_mg[m][ti], B_rope])
        dump("mg_%s" % halo_side, mreg, all_mg())

    def spread(items, norms):
        sched = {}
        pos = 0
        for (nsq, nmm, nfin) in norms:
            for j in range(4):
                sched.setdefault(pos + j, []).append((nsq, 2 * j))
                sched.setdefault(pos + j, []).append((nsq, 2 * j + 1))
                sched.setdefault(pos + j + 1, []).append((nmm, 2 * j))
                sched.setdefault(pos + j + 1, []).append((nmm, 2 * j + 1))
            sched.setdefault(pos + 5, []).append((nfin, None))
            pos += 6
        assert pos <= len(items), (pos, len(items))
        out = []
        for j, (A, Bf) in enumerate(items):
            def A2(A=A, j=j):
                for f, k in sorted(sched.get(j, []), key=lambda fk: 0 if fk[0].__name__ == "mmk" else 1):
                    f(k) if k is not None else f()
                return A()
            out.append((A2, Bf))
        return out

    def tile_norms(l, s_, which, main):
        return [norm_parts(l, s_, which, TILES[xi][0], n, hg[:, :, hc:hc + n], B_hg, [B_x[xi]],
                           sqbufs=[(sqd[:, i, :], B_sqd[i]) for i in range(4)], bank=7) for (hc, n, xi) in main]

    def group_wout(l, s_, main, norms=()):
        items = []
        wq = {}
        for nn in range(8):
            for ti, (hc, n, xi) in enumerate(main):
                def A(nn=nn, ti=ti, hc=hc, n=n):
                    if ti == 0:
                        wq[nn] = wload(l, CH_WOUT + nn)
                    w, B_w = wq[nn]
                    po, B_po = nps()
                    mm(po[:, 0:n], [(w[:, m * 128:(m + 1) * 128], merged[:, m, hc:hc + n]) for m in range(8)],
                       [B_w] + [B_mg[m][ti] for m in range(8)], B_po)
                    return po, B_po

                def Bf(po, B_po, nn=nn, n=n, xi=xi):
                    xo = TILES[xi][0]
                    stt(xT[:, nn, xo:xo + n], po[:, 0:n], mod_ap(l, 2, nn, s_), xT[:, nn, xo:xo + n], ALU.mult, ALU.add,
                        [B_po, B_c, B_x[xi]], [B_x[xi]])
                items.append((A, Bf))
        st["rsv7"] = True
        run_items(spread(items, norms) if norms and len(items) >= 6 * len(norms) else items)
        st["rsv7"] = False

    def group_wout_old(l, s_, main):
        for nn in range(8):
            w, B_w = wload(l, CH_WOUT + nn)
            for ti, (hc, n, xi) in enumerate(main):
                po, B_po = nps()
                mm(po[:, 0:n], [(w[:, m * 128:(m + 1) * 128], merged[:, m, hc:hc + n]) for m in range(8)],
                   [B_w] + [B_mg[m][ti] for m in range(8)], B_po)
                xo = TILES[xi][0]
                stt(xT[:, nn, xo:xo + n], po[:, 0:n], mod_ap(l, 2, nn, s_), xT[:, nn, xo:xo + n], ALU.mult, ALU.add,
                    [B_po, B_c, B_x[xi]], [B_x[xi]])

    def ffn_norm(l, s_, main):
        for ti, (hc, n, xi) in enumerate(main):
            norm_tile(l, s_, 1, TILES[xi][0], n, hg[:, :, hc:hc + n], B_hg, [B_x[xi]])

    def ffn_gu(l, s_, main):
        for j in range(NJ):
            wg_, B_wg = wload(l, CH_GU + 2 * j)
            wu_, B_wu = wload(l, CH_GU + 2 * j + 1)
            for ti, (hc, n, xi) in enumerate(main):
                pg, B_pg = proj_fm(wg_, B_wg, lambda kc: hg[:, kc, hc:hc + n], [B_hg], n)
                pu, B_pu = proj_fm(wu_, B_wu, lambda kc: hg[:, kc, hc:hc + n], [B_hg], n)
                sl, B_sl = ntf()
                act_(sl[:, 0:n], pg[:, 0:n], AF.Silu, [B_pg], [B_sl])
                tt(act[:, j, hc:hc + n], sl[:, 0:n], pu[:, 0:n], ALU.mult, [B_sl, B_pu], [B_act[j][ti]])

    def ffn_down(l, s_, main, norms=()):
        items = []
        wq = {}
        for nn in range(8):
            for ti, (hc, n, xi) in enumerate(main):
                def A(nn=nn, ti=ti, hc=hc, n=n):
                    if ti == 0:
                        wq[nn] = [wload(l, CH_WD + 3 * nn + i, 1024 if i < 2 else 768) for i in range(3)]
                    ws_ = wq[nn]
                    po, B_po = nps()
                    mm(po[:, 0:n], [(ws_[j // 8][0][:, (j % 8) * 128:(j % 8 + 1) * 128], act[:, j, hc:hc + n]) for j in range(NJ)],
                       [w_[1] for w_ in ws_] + [B_act[j][ti] for j in range(NJ)], B_po)
                    return po, B_po

                def Bf(po, B_po, nn=nn, n=n, xi=xi):
                    xo = TILES[xi][0]
                    stt(xT[:, nn, xo:xo + n], po[:, 0:n], mod_ap(l, 5, nn, s_), xT[:, nn, xo:xo + n], ALU.mult, ALU.add,
                        [B_po, B_c, B_x[xi]], [B_x[xi]])
                items.append((A, Bf))
        st["rsv7"] = True
        run_items(spread(items, norms) if norms and len(items) >= 6 * len(norms) else items)
        st["rsv7"] = False

    def ffn_down_old(l, s_, main):
        for nn in range(8):
            ws_ = [wload(l, CH_WD + 3 * nn + i, 1024 if i < 2 else 768) for i in range(3)]
            for ti, (hc, n, xi) in enumerate(main):
                po, B_po = nps()
                mm(po[:, 0:n], [(ws_[j // 8][0][:, (j % 8) * 128:(j % 8 + 1) * 128], act[:, j, hc:hc + n]) for j in range(NJ)],
                   [w_[1] for w_ in ws_] + [B_act[j][ti] for j in range(NJ)], B_po)
                xo = TILES[xi][0]
                stt(xT[:, nn, xo:xo + n], po[:, 0:n], mod_ap(l, 5, nn, s_), xT[:, nn, xo:xo + n], ALU.mult, ALU.add,
                    [B_po, B_c, B_x[xi]], [B_x[xi]])

    def load_x(q):
        for i in range(TOK // 128):
            src = ctx_d[q, i * 128:(i + 1) * 128, :] if i < 2 else x_d[q, (i - 2) * 128:(i - 1) * 128, :]
            s_ = i % 2
            stg = tf[:, 2 * s_:2 * s_ + 2, :].rearrange("p a t -> p (a t)")
            B_s = B_tf[2 * s_:2 * s_ + 2]
            P.dma("sp" if s_ == 0 else "pool", stg, src, writes=B_s)
            xi = [j for j, (o, n) in enumerate(TILES) if o <= i * 128 < o + n][0]
            for half in range(2):
                pt, B_pt = nps()
                for k4 in range(4):
                    kc = half * 4 + k4
                    P.emit("pe", lambda e, pt=pt, k4=k4, kc=kc, stg=stg: e.transpose(out=pt[:, k4 * 128:(k4 + 1) * 128], in_=stg[:, kc * 128:(kc + 1) * 128],
                                                                                    identity=identF), reads=B_s + [B_c], writes=[B_pt], inc=(k4 == 3))
                dst = xT[:, half * 4:half * 4 + 4, i * 128:(i + 1) * 128]
                srcp = pt.rearrange("p (a t) -> p a t", a=4)
                if half == 0:
                    P.emit("act", lambda e, dst=dst, srcp=srcp: e.activation(out=dst, in_=srcp, func=AF.Copy), reads=[B_pt], writes=[B_x[xi]])
                else:
                    P.emit("dve", lambda e, dst=dst, srcp=srcp: e.tensor_copy(out=dst, in_=srcp), reads=[B_pt], writes=[B_x[xi]])

    def store_tokens(q, tok0, ntok, dst_d, dst_row0, norm):
        yT = gF.rearrange("p (c t) -> p c t", c=8)
        for t0 in range(0, ntok, 512):
            n = min(512, ntok - t0)
            xo = tok0 + t0
            xi = [j for j, (o, nn) in enumerate(TILES) if o <= xo < o + nn][0]
            if norm:
                pss, B_pss = nps()
                for kc in range(KC):
                    sq, B_sq = ntb()
                    act_(sq[:, 0:n], xT[:, kc, xo:xo + n], AF.Square, [B_x[xi]], [B_sq])
                    P.emit("pe", lambda e, sq=sq, kc=kc, pss=pss, n=n: e.matmul(pss[:, 0:n], lhsT=OnesD, rhs=sq[:, 0:n], start=(kc == 0), stop=(kc == KC - 1)),
                           reads=[B_sq, B_c], writes=[B_pss], inc=True)
                rs, B_rs = rstd_from(pss, B_pss, n)
                for kc in range(KC):
                    stt(yT[:, kc, 0:n], xT[:, kc, xo:xo + n], V(V_FG + kc), rs[:, 0:n], ALU.mult, ALU.mult, [B_x[xi], B_rs, B_c], [B_gF])
                srcT, B_src = (lambda kc, a: yT[:, kc, a * 128:(a + 1) * 128]), [B_gF]
            else:
                srcT, B_src = (lambda kc, a, xo=xo: xT[:, kc, xo + a * 128:xo + (a + 1) * 128]), [B_x[xi]]
            for a in range(n // 128):
                s_ = a % 2
                stg = tf[:, 2 * s_:2 * s_ + 2, :].rearrange("p a t -> p (a t)")
                B_s = B_tf[2 * s_:2 * s_ + 2]
                for half in range(2):
                    pt, B_pt = nps()
                    for k4 in range(4):
                        kc = half * 4 + k4
                        P.emit("pe", lambda e, pt=pt, k4=k4, kc=kc, a=a, srcT=srcT: e.transpose(
                            out=pt[:, k4 * 128:(k4 + 1) * 128], in_=srcT(kc, a), identity=identF),
                            reads=B_src + [B_c], writes=[B_pt], inc=(k4 == 3))
                    if half == 0:
                        P.emit("act", lambda e, pt=pt, stg=stg: e.activation(out=stg[:, 0:512], in_=pt, func=AF.Copy), reads=[B_pt], writes=B_s)
                    else:
                        P.emit("dve", lambda e, pt=pt, stg=stg: e.tensor_copy(out=stg[:, 512:1024], in_=pt), reads=[B_pt], writes=B_s)
                r0 = dst_row0 + t0 + a * 128
                P.dma("sp" if s_ == 0 else "pool", dst_d[q, r0:r0 + 128, :], stg, reads=B_s)

    for q in range(nseq):
        load_x(q)
        if q == 0:
            mod_prologue()
            P.barrier()
        for li, l in enumerate(layers):
            last = (l == NL - 1)
            kv_phase(l, q, q)
            groups = []
            if not last:
                groups.append((2, [(0, 256, 0)], [(0, 256, 0)], "none", None, 2))
            groups.append((q, [(256, 512, 0), (768, 512, 512), (1280, 1, 1024)], [(0, 512, 1), (512, 512, 2)], "right", 0, 18))
            groups.append((q, [(1280, 512, 0), (1792, 512, 512), (1279, 1, 1024)], [(0, 512, 3), (512, 512, 4)], "left", 1024, 18))
            ng = len(groups)
            group_norm(l, groups[0][0], groups[0][1])
            for gi, (s_, pieces, main, hside, lat0, nkc) in enumerate(groups):
                group_phase(l, s_, pieces, main, hside, lat0, nkc)
                if gi + 1 < ng:
                    ns_, npieces, nmain = groups[gi + 1][0], groups[gi + 1][1], groups[gi + 1][2]
                    halo = [p_ for p_ in npieces if p_[1] == 1]
                    if halo:
                        group_norm(l, ns_, halo)
                    if len(main) * 8 >= 6 * len(nmain):
                        group_wout(l, s_, main, tile_norms(l, ns_, 0, nmain))
                    else:
                        group_norm(l, ns_, [p_ for p_ in npieces if p_[1] != 1])
                        group_wout(l, s_, main)
                else:
                    group_wout(l, s_, main, tile_norms(l, groups[0][0], 1, groups[0][2]))
            P.barrier()
            dump("xT_mid", xT.rearrange("p c t -> p (c t)"), B_x)
            for gi, (s_, pieces, main, hside, lat0, nkc) in enumerate(groups):
                ffn_gu(l, s_, main)
                if gi + 1 < ng and len(main) * 8 >= 6 * len(groups[gi + 1][2]):
                    ffn_down(l, s_, main, tile_norms(l, groups[gi + 1][0], 1, groups[gi + 1][2]))
                else:
                    if gi + 1 < ng:
                        ffn_norm(l, groups[gi + 1][0], groups[gi + 1][2])
                    ffn_down(l, s_, main)
            P.barrier()
        store_tokens(q, CTX, SEQ, out_d, 0, final_norm)
        if out_ctx:
            store_tokens(q, 0, CTX, octx_d, 0, False)
        P.barrier()
    P.finish("sp")
    P.build()
    return nc


def _fm_chunk(w, rows, cols):
    raise NotImplementedError


def _prep_weights(inp):
    w_in = np.asarray(inp["w_in"], np.float32)
    wfm = np.zeros((NL, NCHUNK, 128, 8, 128), np.float32)
    wv = np.zeros((NL, 128, 8, 640), np.float32)

    def fm(wcols):
        return wcols.reshape(8, 128, 128).transpose(1, 0, 2)

    oq, okk, ov, obq, obk, obv, ocb, occ, ocu, og = 0, 512, 1024, 1536, 2048, 2176, 2304, 2816, 3328, 3840
    for l in range(NL):
        W = w_in[l]
        for c in range(4):
            wfm[l, CH_AK + c] = fm(W[:, okk + c * 128: okk + (c + 1) * 128])
            wfm[l, CH_AQ + c] = fm(W[:, oq + c * 128: oq + (c + 1) * 128])
            cols = np.concatenate([np.arange(obq + c * 64, obq + (c + 1) * 64), np.arange(obq + (c + 4) * 64, obq + (c + 5) * 64)])
            wfm[l, CH_BQ + c] = fm(W[:, cols])
            wfm[l, CH_CC + c] = fm(W[:, occ + c * 128: occ + (c + 1) * 128])
            wfm[l, CH_CU + c] = fm(W[:, ocu + c * 128: ocu + (c + 1) * 128])
            wfm[l, CH_CB + c] = fm(W[:, ocb + c * 128: ocb + (c + 1) * 128])
        wfm[l, CH_BK] = fm(W[:, obk: obk + 128])
        for m in range(8):
            for b in range(3):
                wfm[l, CH_G + 3 * m + b] = fm(W[:, og + b * 1024 + m * 128: og + b * 1024 + (m + 1) * 128])
        wv[l, :, :, 0:512] = W[:, ov:ov + 512].reshape(8, 128, 512).transpose(1, 0, 2)
        wv[l, :, :, 512:640] = W[:, obv:obv + 128].reshape(8, 128, 128).transpose(1, 0, 2)
        wa = np.asarray(inp["w_branch_a"], np.float32)[l]
        wb = np.asarray(inp["w_branch_b"], np.float32)[l]
        wc = np.asarray(inp["w_branch_c"], np.float32)[l]
        perm = np.concatenate([np.concatenate([np.arange(c * 64, (c + 1) * 64), np.arange((c + 4) * 64, (c + 5) * 64)]) for c in range(4)])
        wbp = wb[perm]
        for m in range(8):
            wfm[l, CH_WAB + m, :, 0:4, :] = wa[:, m * 128:(m + 1) * 128].reshape(4, 128, 128).transpose(1, 0, 2)
            wfm[l, CH_WAB + m, :, 4:8, :] = wbp[:, m * 128:(m + 1) * 128].reshape(4, 128, 128).transpose(1, 0, 2)
            wfm[l, CH_WC + m, :, 0:4, :] = wc[:, m * 128:(m + 1) * 128].reshape(4, 128, 128).transpose(1, 0, 2)
            wfm[l, CH_WOUT + m] = fm(np.asarray(inp["w_out"], np.float32)[l][:, m * 128:(m + 1) * 128])
        gu = np.asarray(inp["w_ffn_gu"], np.float32)[l]
        for j in range(NJ):
            wfm[l, CH_GU + 2 * j] = fm(gu[:, j * 128:(j + 1) * 128])
            wfm[l, CH_GU + 2 * j + 1] = fm(gu[:, DFF + j * 128: DFF + (j + 1) * 128])
        wd = np.asarray(inp["w_ffn_down"], np.float32)[l]
        for nn in range(8):
            blk = wd[:, nn * 128:(nn + 1) * 128].reshape(NJ, 128, 128).transpose(1, 0, 2)
            for i in range(3):
                nt = 8 if i < 2 else 6
                wfm[l, CH_WD + 3 * nn + i, :, 0:nt, :] = blk[:, i * 8:i * 8 + nt, :]
    wmod = np.asarray(inp["w_mod"], np.float32).reshape(NL, 8, 128, 48, 128).transpose(0, 3, 2, 1, 4)
    bmod = np.asarray(inp["b_mod"], np.float32).reshape(NL, 48, 128).transpose(2, 0, 1)
    vecs = np.zeros((128, NV + NVL), np.float32)
    for l in range(NL):
        vecs[:, V_N1G + l * 8:V_N1G + l * 8 + 8] = np.asarray(inp["norm1_g"], np.float32)[l].reshape(8, 128).T
        vecs[:, V_N2G + l * 8:V_N2G + l * 8 + 8] = np.asarray(inp["norm2_g"], np.float32)[l].reshape(8, 128).T
        cw = np.asarray(inp["conv_w"], np.float32)[l]
        for k in range(3):
            vecs[:, V_CONV + l * 12 + k * 4: V_CONV + l * 12 + k * 4 + 4] = cw[k].reshape(4, 128).T
        vecs[:, V_SUB + l] = np.asarray(inp["diff_subln_g"], np.float32)[l]
        pidx = np.arange(128) % 64
        for col, name in ((V_QG, "q_norm_g"), (V_KG, "k_norm_g")):
            g = np.asarray(inp[name], np.float32)[l]
            vecs[:, col + l * 2] = g[pidx]
            vecs[:, col + l * 2 + 1] = g[pidx ^ 16]
        for a, (n1, n2) in enumerate((("lam_q1", "lam_k1"), ("lam_q2", "lam_k2"))):
            base = V_LAM + l * 256 + a * 128
            vecs[:, base:base + 64] = np.asarray(inp[n1], np.float32)[l][None, :]
            vecs[:, base + 64:base + 128] = np.asarray(inp[n2], np.float32)[l][None, :]
    vecs[:, V_FG:V_FG + 8] = np.asarray(inp["final_g"], np.float32).reshape(8, 128).T
    rows = SEQ // GRID_W
    row = np.repeat(np.arange(rows, dtype=np.float32), GRID_W)
    col = np.tile(np.arange(GRID_W, dtype=np.float32), rows)
    nf = 16
    inv_freq = (np.float32(10000.0) ** (-np.arange(nf, dtype=np.float32) / np.float32(nf))).astype(np.float32)
    rope = np.zeros((128, 2, SEQ), np.float32)
    for p in range(128):
        d = p % 64
        axis, half, f = d // 32, (d // 16) % 2, d % 16
        ang = ((row if axis == 0 else col) * inv_freq[f]).astype(np.float32)
        rope[p, 0] = np.cos(ang)
        rope[p, 1] = np.sin(ang) * (-1.0 if half == 0 else 1.0)
    cst = np.zeros((128, 5, 128), np.float32)
    cst[:, 0] = np.eye(128, dtype=np.float32)
    for p in range(128):
        cst[p ^ 16, 1, p] = 1.0
    cst[0:64, 2, 0:64] = 1.0 / 64
    cst[64:128, 2, 64:128] = 1.0 / 64
    cst[:, 3] = 1.0 / 1024
    cst[:, 4] = 1.0 / 128
    return dict(wfm=np.ascontiguousarray(wfm.reshape(NL, NCHUNK, 128, 1024)),
                wv=np.ascontiguousarray(wv.reshape(NL, 128, 5120)),
                wmod=np.ascontiguousarray(wmod.reshape(NL, 48, 128, 1024)),
                bmod=np.ascontiguousarray(bmod.reshape(128, NL * 48)),
                vecs=vecs, rope=np.ascontiguousarray(rope.reshape(128, 2 * SEQ)),
                cst=np.ascontiguousarray(cst.reshape(128, 640)))


def _core_maps(shared, x, ctx, c, c_ctx, nseq, ncores):
    maps = []
    for i in range(ncores):
        cT = np.zeros((128, 8, 4), np.float32)
        for s_ in range(nseq):
            cT[:, :, s_] = c[i * nseq + s_].reshape(8, 128).T
        cT[:, :, 2] = c_ctx.reshape(8, 128).T
        m = dict(shared)
        m["x"] = np.ascontiguousarray(x[i * nseq:(i + 1) * nseq])
        m["ctx"] = np.ascontiguousarray(ctx[i * nseq:(i + 1) * nseq])
        m["cT"] = np.ascontiguousarray(cT.reshape(128, 32))
        maps.append(m)
    return maps


FUSED = True


def kernel(**inp):
    x = np.asarray(inp["x"], np.float32)
    ctx = np.asarray(inp["ctx"], np.float32)
    c = np.asarray(inp["c"], np.float32)
    c_ctx = np.asarray(inp["c_ctx"], np.float32)
    shared = _prep_weights(inp)
    cores = list(range(NCORES))
    if FUSED:
        nc = build_program([0, 1], True, False)
        res = run_bass_kernel_spmd(nc, _core_maps(shared, x, ctx, c, c_ctx, NSEQ, NCORES), core_ids=cores)
        return np.concatenate([np.asarray(r["out"], np.float32) for r in res.results], axis=0)
    nc0 = build_program([0], False, True)
    res = run_bass_kernel_spmd(nc0, _core_maps(shared, x, ctx, c, c_ctx, NSEQ, NCORES), core_ids=cores)
    x1 = np.concatenate([np.asarray(r["out"], np.float32) for r in res.results], axis=0)
    ctx1 = np.concatenate([np.asarray(r["ctx_out"], np.float32) for r in res.results], axis=0)
    nc1 = build_program([1], True, False)
    res = run_bass_kernel_spmd(nc1, _core_maps(shared, x1, ctx1, c, c_ctx, NSEQ, NCORES), core_ids=cores)
    return np.concatenate([np.asarray(r["out"], np.float32) for r in res.results], axis=0)
```

```python
import math
import contextlib
import numpy as np
import concourse.bass as bass
import concourse.mybir as mybir
from concourse.bass_utils import run_bass_kernel_spmd

F32 = mybir.dt.float32
BF16 = mybir.dt.bfloat16
AF = mybir.ActivationFunctionType
ALU = mybir.AluOpType
AX = mybir.AxisListType

D = 1024
KC = 8
SEQ = 2048
CTX = 256
TOK = CTX + SEQ
NL = 2
NCORES = 8
NSEQ = 2
EPS = 1e-6
DFF = 2816
NJ = 22
GRID_W = 64

CH_AK, CH_BK, CH_AQ, CH_BQ, CH_CC, CH_CU, CH_CB, CH_G, CH_WAB, CH_WC, CH_WOUT, CH_GU, CH_WD = (
    0, 4, 5, 9, 13, 17, 21, 25, 49, 57, 65, 73, 117)
NCHUNK = 141
V_N1G, V_N2G, V_FG, V_CONV, V_SUB, V_QG, V_KG, V_LAM = 0, 16, 32, 40, 64, 66, 70, 74
NV = 74
NVL = 512


class Buf:
    __slots__ = ("w", "r", "excl")

    def __init__(self, excl=False):
        self.w = None
        self.r = []
        self.excl = excl


class Prog:
    ENGS = ("pe", "act", "dve", "pool", "sp")

    def __init__(self, nc, n_dma_sems=32):
        self.nc = nc
        self.ops = {e: [] for e in self.ENGS}
        self.cnt = {e: 0 for e in self.ENGS}
        self.waited = {e: {} for e in self.ENGS}
        self.n_dma_sems = n_dma_sems
        self.dcum = [0] * n_dma_sems
        self.dnext = 0
        self.dnext2 = [0, 0]
        self.self_wait = {"pe": False, "act": True, "dve": True, "pool": True, "sp": False}

    def _need(self, eng, tok):
        if tok is None:
            return
        sid, val = tok
        if sid == eng and not self.self_wait[eng]:
            return
        if self.waited[eng].get(sid, 0) >= val:
            return
        if sid in self.cnt and val > self.cnt[sid]:
            raise RuntimeError("wait on pending (future) token %s %d > %d from %s" % (sid, val, self.cnt[sid], eng))
        self.waited[eng][sid] = val
        self.ops[eng].append(("wait", sid, val))

    def _deps(self, eng, reads, writes):
        for b in reads:
            self._need(eng, b.w)
            if b.excl:
                for t in b.r:
                    if t[0] != eng:
                        self._need(eng, t)
        for b in writes:
            self._need(eng, b.w)
            for t in b.r:
                self._need(eng, t)

    def _upd(self, tok, reads, writes):
        for b in writes:
            b.w = tok
            b.r = []
        for b in reads:
            r = b.r
            for i in range(len(r)):
                if r[i][0] == tok[0]:
                    r[i] = tok
                    break
            else:
                r.append(tok)

    def emit(self, eng, fn, reads=(), writes=(), inc=True):
        self._deps(eng, reads, writes)
        tok = (eng, self.cnt[eng] + 1)
        if inc:
            self.cnt[eng] += 1
        self.ops[eng].append(("op", fn, inc))
        self._upd(tok, reads, writes)
        return tok

    def dma(self, q, out, in_, reads=(), writes=()):
        half = self.n_dma_sems // 2
        qi = 1 if q == "pool" else 0
        k = qi * half + self.dnext2[qi]
        self.dnext2[qi] = (self.dnext2[qi] + 1) % half
        sid = ("d", k)
        if self.dcum[k] > 0:
            self._need(q, (sid, self.dcum[k]))
        self._deps(q, reads, writes)
        self.dcum[k] += 16
        tok = (sid, self.dcum[k])
        self.ops[q].append(("dma", out, in_, k))
        self._upd(tok, reads, writes)
        return tok

    def barrier(self, engs=("pe", "act", "dve")):
        for e in engs:
            for f in engs:
                if f != e and self.cnt[f] > 0:
                    self._need(e, (f, self.cnt[f]))

    def finish(self, q="sp"):
        for k in range(self.n_dma_sems):
            if self.dcum[k] > 0:
                self._need(q, (("d", k), self.dcum[k]))

    def build(self):
        nc = self.nc
        with contextlib.ExitStack() as st:
            esem = {e: st.enter_context(nc.semaphore("s_" + e)) for e in self.ENGS}
            dsem = [st.enter_context(nc.semaphore("d_%d" % i)) for i in range(self.n_dma_sems)]

            def getsem(sid):
                return dsem[sid[1]] if isinstance(sid, tuple) else esem[sid]

            block = st.enter_context(nc.Block())

            def run(ename):
                def f(e):
                    for op in self.ops[ename]:
                        if op[0] == "wait":
                            e.wait_ge(getsem(op[1]), op[2])
                        elif op[0] == "op":
                            ins = op[1](e)
                            if op[2]:
                                ins.then_inc(esem[ename], 1)
                        else:
                            e.dma_start(out=op[1], in_=op[2]).then_inc(dsem[op[3]], 16)
                return f

            block.tensor(run("pe"))
            block.scalar(run("act"))
            block.vector(run("dve"))
            block.gpsimd(run("pool"))
            block.sync(run("sp"))


def lam_init_of(l):
    return 0.8 - 0.6 * math.exp(-0.3 * l)


def build_program(layers, final_norm, out_ctx, nseq=NSEQ, dbg=False):
    nc = bass.Bass("TRN2", target_bir_lowering=False, dynamic_dma_scratch_size=8192)
    nl = len(layers)
    x_d = nc.dram_tensor("x", [nseq, SEQ, D], F32, kind="ExternalInput").ap()
    ctx_d = nc.dram_tensor("ctx", [nseq, CTX, D], F32, kind="ExternalInput").ap()
    cT_d = nc.dram_tensor("cT", [128, KC * 4], F32, kind="ExternalInput").ap()
    wmod_d = nc.dram_tensor("wmod", [NL, 48, 128, 1024], F32, kind="ExternalInput").ap()
    bmod_d = nc.dram_tensor("bmod", [128, NL * 48], F32, kind="ExternalInput").ap()
    vecs_d = nc.dram_tensor("vecs", [128, NV + NVL], F32, kind="ExternalInput").ap()
    rope_d = nc.dram_tensor("rope", [128, 2 * SEQ], F32, kind="ExternalInput").ap()
    cst_d = nc.dram_tensor("cst", [128, 5 * 128], F32, kind="ExternalInput").ap()
    wfm_d = nc.dram_tensor("wfm", [NL, NCHUNK, 128, 1024], F32, kind="ExternalInput").ap()
    wv_d = nc.dram_tensor("wv", [NL, 128, KC * 640], F32, kind="ExternalInput").ap()
    out_d = nc.dram_tensor("out", [nseq, SEQ, D], F32, kind="ExternalOutput").ap()
    if out_ctx:
        octx_d = nc.dram_tensor("ctx_out", [nseq, CTX, D], F32, kind="ExternalOutput").ap()

    P = Prog(nc)
    dbg_list = []

    def dump(key, ap, bufs):
        if not dbg or key in dbg_list:
            return
        dbg_list.append(key)
        shp = list(ap.shape)
        d = nc.dram_tensor("dbg_" + key, shp, ap.dtype, kind="ExternalOutput").ap()
        P.dma("sp", d, ap, reads=bufs)

    def sb(name, shape, dt):
        return nc.alloc_sbuf_tensor("sb_" + name, shape, dt).ap()

    xT = sb("xT", [128, KC, TOK], F32)
    kvreg = sb("kvreg", [128, 24192], BF16)
    gscr = sb("gscr", [128, 8208 + 8192 + 4096], BF16)
    mreg = sb("mreg", [128, 8192], BF16)
    NWS = 7
    wsl = [sb("wsl%d" % i, [128, 1024], BF16) for i in range(NWS)]
    tf = sb("tf", [128, 6, 512], F32)
    tb = sb("tb", [128, 6, 512], BF16)
    sqd = sb("sqd", [128, 4, 512], BF16)
    identF = sb("identF", [128, 128], F32)
    cstb = sb("cstb", [128, 4, 128], BF16)
    onesb = sb("onesb", [128, 128], BF16)
    modt = sb("modt", [128, NL, 48, 4], F32)
    drv = sb("drv", [128, NL, 3, 2, KC], F32)
    vecs = sb("vecs", [128, NV], F32)
    lamt = sb("lamt", [128, NL, 4], F32)
    subs = sb("subs", [128, NL], F32)
    scT = sb("scT", [128, KC, 4], F32)
    bmod = sb("bmod", [128, NL, 48], F32)

    ak = kvreg[:, 0:9216].rearrange("p (c t) -> p c t", c=4)
    bk = kvreg[:, 9216:11520]
    av = kvreg[:, 11520:20736].rearrange("p (k c) -> p k c", k=18)
    bv = kvreg[:, 20736:24192].rearrange("p (k c) -> p k c", k=18)
    act = kvreg[:, 0:22528].rearrange("p (j t) -> p j t", j=NJ)
    hg = gscr[:, 0:8200].rearrange("p (c t) -> p c t", c=8)
    hs = [gscr[:, 0:4096].rearrange("p (c t) -> p c t", c=8), gscr[:, 4096:8192].rearrange("p (c t) -> p c t", c=8)]
    qy = gscr[:, 8208:16400].rearrange("p (c t) -> p c t", c=8)
    yc = gscr[:, 16400:20496].rearrange("p (c t) -> p c t", c=4)
    kw = gscr[:, 8208:13328].rearrange("p (i c) -> p i c", i=5)
    vw = gscr[:, 13328:18448].rearrange("p (k c) -> p k c", k=8)
    merged = mreg.rearrange("p (c t) -> p c t", c=8)
    ropeT = mreg.bitcast(F32).rearrange("p (a t) -> p a t", a=2)
    gF = gscr[:, 0:16384].bitcast(F32)
    Pm, Blk, OnesD, OnesV = cstb[:, 0, :], cstb[:, 1, :], cstb[:, 2, :], cstb[:, 3, :]

    psall = nc.alloc_psum_tensor("psall", [128, 8, 512], F32).ap()
    ps = [psall[:, i, :] for i in range(8)]
    B_ps = [Buf(True) for _ in range(8)]

    B_x = [Buf() for _ in range(5)]
    B_ak, B_bk, B_av, B_bv = Buf(), Buf(), Buf(), Buf()
    B_hs = [Buf(), Buf()]
    B_hg = Buf()
    B_qy = [[Buf(), Buf()] for _ in range(8)]
    B_yc = [Buf() for _ in range(4)]
    B_kw = [Buf() for _ in range(5)]
    B_vw = Buf()
    B_mg = [[Buf(), Buf()] for _ in range(8)]
    B_rope = Buf()
    B_ws = [Buf() for _ in range(NWS)]
    B_tf = [Buf() for _ in range(6)]
    B_tb = [Buf() for _ in range(6)]
    B_sqd = [Buf() for _ in range(4)]
    B_act = [[Buf(), Buf()] for _ in range(NJ)]
    B_c = Buf()
    B_gF = Buf()
    NWM = 7
    B_wm = [Buf() for _ in range(NWM)]
    st = {"ws": 0, "tf": 0, "tb": 0, "ps": 0, "tbp": 0}

    def all_qy():
        return [b for r in B_qy for b in r]

    def all_mg():
        return [b for r in B_mg for b in r]

    def ntf():
        i = 2 + st["tf"] % 4
        st["tf"] = (st["tf"] + 1) % 4
        return tf[:, i, :], B_tf[i]

    def ntb():
        i = st["tb"]
        st["tb"] = (i + 1) % 6
        return tb[:, i, :], B_tb[i]

    def nps():
        nb = 7 if st.get("rsv7") else 8
        i = st["ps"] % nb
        st["ps"] = (i + 1) % nb
        return ps[i], B_ps[i]

    def wload(l, ch, ncols=1024):
        i = st["ws"]
        st["ws"] = (i + 1) % NWS
        P.dma("pool", wsl[i][:, 0:ncols], wfm_d[l, ch, :, 0:ncols], writes=[B_ws[i]])
        return wsl[i], B_ws[i]

    def mm(out, pairs, reads, wbuf):
        n = len(pairs)
        for i, (l, r) in enumerate(pairs):
            P.emit("pe", lambda e, l=l, r=r, i=i: e.matmul(out, lhsT=l, rhs=r, start=(i == 0), stop=(i == n - 1)),
                   reads=reads, writes=[wbuf], inc=(i == n - 1))

    def V(col):
        return vecs[:, col:col + 1]

    def act_(out, in_, func, reads, writes, **kw_):
        P.emit("act", lambda e: e.activation(out=out, in_=in_, func=func, **kw_), reads=reads, writes=writes)

    def tt(out, in0, in1, op, reads, writes):
        P.emit("dve", lambda e: e.tensor_tensor(out=out, in0=in0, in1=in1, op=op), reads=reads, writes=writes)

    def stt(out, in0, scalar, in1, op0, op1, reads, writes):
        P.emit("dve", lambda e: e.scalar_tensor_tensor(out=out, in0=in0, scalar=scalar, in1=in1, op0=op0, op1=op1),
               reads=reads, writes=writes)

    def recip(out, in_, reads, writes):
        P.emit("dve", lambda e: e.reciprocal(out=out, in_=in_), reads=reads, writes=writes)

    def rstd_from(pss, B_pss, n, reads_extra=()):
        sd, B_sd = tf[:, 0, :], B_tf[0]
        act_(sd[:, 0:n], pss[:, 0:n], AF.Ln, [B_pss] + list(reads_extra), [B_sd], bias=epsc[:, 0:1])
        rs, B_rs = tf[:, 1, :], B_tf[1]
        act_(rs[:, 0:n], sd[:, 0:n], AF.Exp, [B_sd], [B_rs], scale=-0.5)
        return rs, B_rs

    if dbg:
        P.emit("dve", lambda e: e.memset(modt.rearrange("p l j s -> p (l j s)"), 0.0), writes=[B_c, B_gF] + B_wm)
        P.emit("dve", lambda e: e.memset(drv.rearrange("p l s a k -> p (l s a k)"), 0.0), writes=[B_c, B_gF] + B_wm)
        P.emit("dve", lambda e: e.memset(lamt.rearrange("p l a -> p (l a)"), 0.0), writes=[B_c, B_gF] + B_wm)
        P.emit("dve", lambda e: e.memset(gscr, 0.0), writes=[B_c, B_gF] + B_wm)
        P.emit("dve", lambda e: e.memset(mreg, 0.0), writes=[B_c, B_gF] + B_wm)
    P.dma("sp", vecs, vecs_d[:, 0:NV], writes=[B_c])
    lamv = tf[:, 0, :]
    P.dma("sp", bmod.rearrange("p l j -> p (l j)"), bmod_d, writes=[B_c])
    P.dma("sp", scT.rearrange("p k s -> p (k s)"), cT_d, writes=[B_c])
    cstF = gF[:, 0:640]
    P.dma("sp", cstF, cst_d, writes=[B_gF])
    P.emit("dve", lambda e: e.tensor_copy(out=identF, in_=cstF[:, 0:128]), reads=[B_gF], writes=[B_c])
    P.emit("dve", lambda e: e.tensor_copy(out=cstb.rearrange("p a c -> p (a c)"), in_=cstF[:, 128:640]), reads=[B_gF], writes=[B_c])
    P.emit("dve", lambda e: e.memset(onesb, 1.0), writes=[B_c])
    epsc = sb("epsc", [128, 1], F32)
    P.emit("dve", lambda e: e.memset(epsc, EPS), writes=[B_c])
    act_(scT.rearrange("p k s -> p (k s)"), scT.rearrange("p k s -> p (k s)"), AF.Silu, [B_c], [B_c])
    def mod_prologue():
      if True:
        P.dma("sp", lamv, vecs_d[:, NV:NV + NVL], writes=[B_tf[0]])
        wmF = [gF[:, 1024 * (1 + i):1024 * (2 + i)] for i in range(NWM)]
        for l in layers:
            pm, B_pm = nps()
            for j in range(48):
                s_ = (l * 48 + j) % NWM
                P.dma("sp" if j % 2 == 0 else "pool", wmF[s_], wmod_d[l, j], writes=[B_wm[s_]])
                mm(pm[:, j * 4:(j + 1) * 4], [(wmF[s_][:, kc * 128:(kc + 1) * 128], scT[:, kc, :]) for kc in range(KC)],
                   [B_wm[s_], B_c], B_pm)
            for s_ in range(4):
                tt(modt[:, l, :, s_], pm[:, 0:192].rearrange("p (j s) -> p j s", s=4)[:, :, s_], bmod[:, l, :], ALU.add,
                   [B_pm, B_c], [B_c])
            for s_ in range(3):
                for a, (msc, vg) in enumerate(((8, V_N1G), (32, V_N2G))):
                    stt(drv[:, l, s_, a, :], modt[:, l, msc:msc + 8, s_], 1.0, vecs[:, vg + l * 8:vg + l * 8 + 8], ALU.add, ALU.mult,
                        [B_c], [B_c])
            for a in range(2):
                t_, B_t = ntf()
                base = l * 256 + a * 128
                tt(t_[:, 0:64], lamv[:, base:base + 64], lamv[:, base + 64:base + 128], ALU.mult, [B_tf[0]], [B_t])
                P.emit("dve", lambda e, t_=t_, a=a, l=l: e.tensor_reduce(out=lamt[:, l, 2 + a:3 + a], in_=t_[:, 0:64], axis=AX.X, op=ALU.add),
                       reads=[B_t], writes=[B_c])
            act_(lamt[:, l, 2:4], lamt[:, l, 2:4], AF.Exp, [B_c], [B_c])
            li = lam_init_of(l)
            stt(lamt[:, l, 1:2], lamt[:, l, 3:4], -li, lamt[:, l, 2:3], ALU.add, ALU.subtract, [B_c], [B_c])
            P.emit("dve", lambda e, l=l, li=li: e.tensor_scalar(out=subs[:, l:l + 1], in0=vecs[:, V_SUB + l:V_SUB + l + 1], scalar1=1.0 - li,
                                                        scalar2=None, op0=ALU.mult), reads=[B_c], writes=[B_c])


    def mod_ap(l, m, kc, s_):
        return modt[:, l, m * 8 + kc, s_:s_ + 1]

    def norm_parts(l, s_, which, xoff, n, dst, B_dst, B_xs, sqbufs=None, bank=None):
        state = {}

        def get_bank():
            if "pss" not in state:
                state["pss"] = (ps[bank], B_ps[bank]) if bank is not None else nps()
            return state["pss"]

        def sq(k):
            if sqbufs is None:
                b, B_b = ntb()
            else:
                b, B_b = sqbufs[k % len(sqbufs)]
            state[k] = (b, B_b)
            act_(b[:, 0:n], xT[:, k, xoff:xoff + n], AF.Square, B_xs, [B_b])

        def mmk(k):
            pss, B_pss = get_bank()
            b, B_b = state[k]
            P.emit("pe", lambda e: e.matmul(pss[:, 0:n], lhsT=OnesD, rhs=b[:, 0:n], start=(k == 0), stop=(k == KC - 1)),
                   reads=[B_b, B_c], writes=[B_pss], inc=True)

        def fin():
            pss, B_pss = get_bank()
            rs, B_rs = rstd_from(pss, B_pss, n)
            msh = 0 if which == 0 else 3
            for kc in range(KC):
                t_, B_t = ntf()
                stt(t_[:, 0:n], xT[:, kc, xoff:xoff + n], drv[:, l, s_, which, kc:kc + 1], rs[:, 0:n], ALU.mult, ALU.mult,
                    B_xs + [B_rs, B_c], [B_t])
                act_(dst[:, kc, 0:n], t_[:, 0:n], AF.Identity, [B_t, B_c], [B_dst], bias=mod_ap(l, msh, kc, s_), scale=1.0)
        return sq, mmk, fin

    def norm_tile(l, s_, which, xoff, n, dst, B_dst, B_xs):
        sq, mmk, fin = norm_parts(l, s_, which, xoff, n, dst, B_dst, B_xs)
        for kc in range(KC):
            sq(kc)
            mmk(kc)
        fin()

    def proj_fm(w, B_w, src, B_src, n, ntiles=8):
        pz, B_pz = nps()
        mm(pz[:, 0:n], [(w[:, kc * 128:(kc + 1) * 128], src(kc)) for kc in range(ntiles)], [B_w] + B_src, B_pz)
        return pz, B_pz

    def qk_post(pz, B_pz, n, lat_off, gcol, normed, dst, B_dst):
        rs = None
        roped = lat_off is not None
        if normed:
            z2, B_z2 = ntb()
            act_(z2[:, 0:n], pz[:, 0:n], AF.Square, [B_pz], [B_z2])
        if roped:
            zb, B_zb = ntb()
            act_(zb[:, 0:n], pz[:, 0:n], AF.Copy, [B_pz], [B_zb])
        if normed:
            pss, B_pss = nps()
            mm(pss[:, 0:n], [(Blk, z2[:, 0:n])], [B_z2, B_c], B_pss)
        if roped:
            psw, B_psw = nps()
            mm(psw[:, 0:n], [(Pm, zb[:, 0:n])], [B_zb, B_c], B_psw)
        if normed:
            rs, B_rs = rstd_from(pss, B_pss, n)
        if not roped:
            if normed:
                stt(dst, pz[:, 0:n], V(gcol), rs[:, 0:n], ALU.mult, ALU.mult, [B_pz, B_rs, B_c], [B_dst])
            else:
                act_(dst, pz[:, 0:n], AF.Copy, [B_pz], [B_dst])
            return
        C = ropeT[:, 0, lat_off:lat_off + n]
        S = ropeT[:, 1, lat_off:lat_off + n]
        t1, B_t1 = ntf()
        t2, B_t2 = ntf()
        if normed:
            stt(t1[:, 0:n], pz[:, 0:n], V(gcol), C, ALU.mult, ALU.mult, [B_pz, B_rope, B_c], [B_t1])
            stt(t2[:, 0:n], psw[:, 0:n], V(gcol + 1), S, ALU.mult, ALU.mult, [B_psw, B_rope, B_c], [B_t2])
            tt(t1[:, 0:n], t1[:, 0:n], t2[:, 0:n], ALU.add, [B_t1, B_t2], [B_t1])
            tt(dst, t1[:, 0:n], rs[:, 0:n], ALU.mult, [B_t1, B_rs], [B_dst])
        else:
            tt(t1[:, 0:n], pz[:, 0:n], C, ALU.mult, [B_pz, B_rope], [B_t1])
            tt(t2[:, 0:n], psw[:, 0:n], S, ALU.mult, [B_psw, B_rope], [B_t2])
            tt(dst, t1[:, 0:n], t2[:, 0:n], ALU.add, [B_t1, B_t2], [B_dst])

    def run_items(items):
        pend = None
        for (A, Bf) in items:
            r = A()
            if pend is not None:
                pend[0](*pend[1])
            pend = (Bf, r)
        if pend is not None:
            pend[0](*pend[1])

    def load_rope(extra_writes, lo=0, nn=SEQ):
        P.dma("sp", ropeT[:, :, lo:lo + nn], rope_d.rearrange("p (a t) -> p a t", a=2)[:, :, lo:lo + nn],
              writes=[B_rope] + all_mg() + extra_writes)

    TILES = [(0, 256), (256, 512), (768, 512), (1280, 512), (1792, 512)]

    def kv_phase(l, q, s_lat):
        for i in range(5):
            P.dma("pool", kw[:, i, :], wfm_d[l, CH_AK + i], writes=[B_kw[i]] + (all_qy() + B_yc + [B_gF] + B_wm if i == 0 else []))
        P.dma("pool", vw, wv_d[l].rearrange("p (k c) -> p k c", k=8), writes=[B_vw])
        load_rope([])
        P.emit("dve", lambda e: e.memset(bv[:, :, 64:128], 1.0), writes=[B_bv])
        st["rsv7"] = True

        def kv_norm(ti):
            off, n = TILES[ti]
            norm_tile(l, 2 if ti == 0 else s_lat, 0, off, n, hs[ti % 2], B_hs[ti % 2], [B_x[ti]])
        kv_norm(0)
        for ti, (off, n) in enumerate(TILES):
            h, B_h = hs[ti % 2], B_hs[ti % 2]
            lat_off = None if ti == 0 else off - CTX
            items = []
            for c in range(5):
                def A(c=c, h=h, B_h=B_h, n=n):
                    return proj_fm(kw[:, c, :], B_kw[c], lambda kc: h[:, kc, 0:n], [B_h], n)

                def Bf(pz, B_pz, c=c, n=n, off=off, lat_off=lat_off):
                    if c < 4:
                        qk_post(pz, B_pz, n, lat_off, 0, False, ak[:, c, off:off + n], B_ak)
                    else:
                        qk_post(pz, B_pz, n, lat_off, V_KG + l * 2, True, bk[:, off:off + n], B_bk)
                items.append((A, Bf))
            for sub in range(n // 128):
                kci = off // 128 + sub

                def A(sub=sub, h=h, B_h=B_h):
                    pv, B_pv = nps()
                    mm(pv, [(h[:, kc, sub * 128:(sub + 1) * 128], vw[:, kc, 0:512]) for kc in range(KC)], [B_h, B_vw], B_pv)
                    pv2, B_pv2 = nps()
                    mm(pv2[:, 0:128], [(h[:, kc, sub * 128:(sub + 1) * 128], vw[:, kc, 512:640]) for kc in range(KC)], [B_h, B_vw], B_pv2)
                    return (pv, pv2), (B_pv, B_pv2)

                def Bf(pvs, Bs, kci=kci):
                    pv, pv2 = pvs
                    B_pv, B_pv2 = Bs
                    act_(av[:, kci, :], pv, AF.Copy, [B_pv], [B_av])
                    P.emit("dve", lambda e: e.tensor_copy(out=bv[:, kci, 0:64], in_=pv2[:, 0:64]), reads=[B_pv2], writes=[B_bv])
                    P.emit("dve", lambda e: e.tensor_copy(out=bv[:, kci, 128:192], in_=pv2[:, 64:128]), reads=[B_pv2], writes=[B_bv])
                items.append((A, Bf))
            if ti + 1 < len(TILES):
                off2, n2 = TILES[ti + 1]
                nsq, nmm, nfin = norm_parts(l, s_lat, 0, off2, n2, hs[(ti + 1) % 2], B_hs[(ti + 1) % 2], [B_x[ti + 1]],
                                            sqbufs=[(sqd[:, i, :], B_sqd[i]) for i in range(4)], bank=7)
                wrapped = []
                for j, (A, Bf) in enumerate(items):
                    def A2(A=A, j=j):
                        if 1 <= j <= 4:
                            nmm(2 * j - 2)
                            nmm(2 * j - 1)
                        if j <= 3:
                            nsq(2 * j)
                            nsq(2 * j + 1)
                        if j == 5:
                            nfin()
                        return A()
                    wrapped.append((A2, Bf))
                items = wrapped
            run_items(items)
        st["rsv7"] = False
        P.barrier()
        dump("xT", xT.rearrange("p c t -> p (c t)"), B_x)
        dump("kv", kvreg, [B_ak, B_bk, B_av, B_bv])

    def attention(l, main, nkc):
        for ti, (qo, n) in enumerate(main):
            for h in range(8):
                diff = h < 4
                SA = [(ps[0], B_ps[0]), (ps[2], B_ps[2])]
                SB = [(ps[1], B_ps[1]), (ps[3], B_ps[3])]
                kT = (lambda kc, h=h: ak[:, h, kc * 128:(kc + 1) * 128]) if diff else (lambda kc: bk[:, kc * 128:(kc + 1) * 128])
                B_k = B_ak if diff else B_bk
                qT = qy[:, h, qo:qo + n]
                B_q = B_qy[h][ti]
                gA, gB = (4, 5) if h % 2 == 0 else (6, 7)

                def smm(kc):
                    a, B_a = SA[kc % 2]
                    b, B_b = SB[kc % 2]
                    kt = kT(kc)
                    mm(a[:, 0:n], [(kt[0:64, :], qT[0:64, :])], [B_k, B_q], B_a)
                    mm(b[:, 0:n], [(kt[64:128, :], qT[64:128, :])], [B_k, B_q], B_b)

                smm(0)
                for kc in range(nkc):
                    if kc + 1 < nkc:
                        smm(kc + 1)
                    a, B_a = SA[kc % 2]
                    b, B_b = SB[kc % 2]
                    pi = 2 * (st["tbp"] % 3)
                    st["tbp"] += 1
                    p1, B_p1, p2, B_p2 = tb[:, pi, :], B_tb[pi], tb[:, pi + 1, :], B_tb[pi + 1]
                    act_(p1[:, 0:n], a[:, 0:n], AF.Exp, [B_a], [B_p1], scale=0.125)
                    act_(p2[:, 0:n], b[:, 0:n], AF.Exp, [B_b], [B_p2], scale=0.125)
                    f, la = (kc == 0), (kc == nkc - 1)

                    def acc(bank, lhsT, rhs, rd, f=f, la=la):
                        P.emit("pe", lambda e: e.matmul(ps[bank][:, 0:n], lhsT=lhsT, rhs=rhs, start=f, stop=la),
                               reads=rd, writes=[B_ps[bank]], inc=la)
                    if diff:
                        vt = av[:, kc, h * 128:(h + 1) * 128]
                        acc(4, vt, p1[:, 0:n], [B_av, B_p1])
                        acc(5, onesb, p1[:, 0:n], [B_c, B_p1])
                        acc(6, vt, p2[:, 0:n], [B_av, B_p2])
                        acc(7, onesb, p2[:, 0:n], [B_c, B_p2])
                    else:
                        acc(gA, bv[:, kc, 0:128], p1[:, 0:n], [B_bv, B_p1])
                        acc(gB, bv[:, kc, 64:192], p2[:, 0:n], [B_bv, B_p2])
                if diff:
                    sa, B_sa = ntf()
                    sb2, B_sb2 = ntf()
                    t1, B_t1 = ntf()
                    t2, B_t2 = ntf()
                    P.emit("dve", lambda e, sa=sa: e.tensor_copy(out=sa[:, 0:n], in_=ps[5][:, 0:n]), reads=[B_ps[5]], writes=[B_sa])
                    P.emit("dve", lambda e, sb2=sb2: e.tensor_copy(out=sb2[:, 0:n], in_=ps[7][:, 0:n]), reads=[B_ps[7]], writes=[B_sb2])
                    tt(t1[:, 0:n], ps[4][:, 0:n], sb2[:, 0:n], ALU.mult, [B_ps[4], B_sb2], [B_t1])
                    tt(t2[:, 0:n], ps[6][:, 0:n], sa[:, 0:n], ALU.mult, [B_ps[6], B_sa], [B_t2])
                    stt(t1[:, 0:n], t2[:, 0:n], lamt[:, l, 1:2], t1[:, 0:n], ALU.mult, ALU.add, [B_t1, B_t2, B_c], [B_t1])
                    tt(sa[:, 0:n], sa[:, 0:n], sb2[:, 0:n], ALU.mult, [B_sa, B_sb2], [B_sa])
                    pi = 2 * (st["tbp"] % 3)
                    st["tbp"] += 1
                    d2, B_d2 = tb[:, pi, :], B_tb[pi]
                    tt(d2[:, 0:n], t1[:, 0:n], t1[:, 0:n], ALU.mult, [B_t1], [B_d2])
                    mm(ps[5][:, 0:n], [(OnesV, d2[:, 0:n])], [B_d2, B_c], B_ps[5])
                    stt(sb2[:, 0:n], sa[:, 0:n], EPS, sa[:, 0:n], ALU.mult, ALU.mult, [B_sa], [B_sb2])
                    tt(sb2[:, 0:n], ps[5][:, 0:n], sb2[:, 0:n], ALU.add, [B_ps[5], B_sb2], [B_sb2])
                    act_(sb2[:, 0:n], sb2[:, 0:n], AF.Ln, [B_sb2], [B_sb2])
                    act_(sb2[:, 0:n], sb2[:, 0:n], AF.Exp, [B_sb2], [B_sb2], scale=-0.5)
                    stt(qT, t1[:, 0:n], subs[:, l:l + 1], sb2[:, 0:n], ALU.mult, ALU.mult, [B_t1, B_sb2, B_c], [B_q])
                else:
                    r1, B_r1 = ntf()
                    recip(r1[0:64, 0:n], ps[gA][64:128, 0:n], [B_ps[gA]], [B_r1])
                    recip(r1[64:128, 0:n], ps[gB][0:64, 0:n], [B_ps[gB]], [B_r1])
                    tt(qT[0:64, :], ps[gA][0:64, 0:n], r1[0:64, 0:n], ALU.mult, [B_ps[gA], B_r1], [B_q])
                    tt(qT[64:128, :], ps[gB][64:128, 0:n], r1[64:128, 0:n], ALU.mult, [B_ps[gB], B_r1], [B_q])

    def group_norm(l, s_, pieces):
        for (off, n, hc) in pieces:
            xt = [B_x[i] for i, (o2, n2) in enumerate(TILES) if o2 < off + n and off < o2 + n2]
            norm_tile(l, s_, 0, off, n, hg[:, :, hc:hc + n], B_hg, xt)

    def group_phase(l, s_, pieces, main, halo_side, lat0, nkc):
        is_lat = lat0 is not None
        if is_lat:
            load_rope([], lat0, 1024)
        items = []
        for c in range(8):
            for ti, (hc, n, xi) in enumerate(main):
                def A(c=c, ti=ti, hc=hc, n=n):
                    if ti == 0:
                        qw[c] = wload(l, CH_AQ + c)
                    w, B_w = qw[c]
                    return proj_fm(w, B_w, lambda kc: hg[:, kc, hc:hc + n], [B_hg], n)

                def Bf(pz, B_pz, c=c, ti=ti, hc=hc, n=n):
                    lo = (lat0 + hc) if is_lat else None
                    qk_post(pz, B_pz, n, lo, V_QG + l * 2, c >= 4, qy[:, c, hc:hc + n], B_qy[c][ti])
                items.append((A, Bf))
        qw = {}
        run_items(items)
        dump("q_%s" % halo_side, gscr[:, 8208:16400], all_qy())
        attention(l, [(hc, n) for (hc, n, xi) in main], nkc)
        dump("y_%s" % halo_side, gscr[:, 8208:16400], all_qy())
        st["ps"] = 0
        ntot = sum(n for (hc, n, xi) in main)
        for j in range(4):
            wc_, B_wc = wload(l, CH_CC + j)
            wu_, B_wu = wload(l, CH_CU + j)
            pext = tf[:, 0:3, :].rearrange("p a t -> p (a t)")
            B_pe = B_tf[0:3]
            for (off, n, hc) in pieces:
                pc, B_pc = proj_fm(wc_, B_wc, lambda kc: hg[:, kc, hc:hc + n], [B_hg], n)
                pu, B_pu = proj_fm(wu_, B_wu, lambda kc: hg[:, kc, hc:hc + n], [B_hg], n)
                if hc >= ntot:
                    dcol = 0 if halo_side == "left" else ntot + 1
                else:
                    dcol = 1 + hc
                cc_, B_cc = tf[:, 3, :], B_tf[3]
                act_(cc_[:, 0:n], pc[:, 0:n], AF.Copy, [B_pc], [B_cc])
                tt(pext[:, dcol:dcol + n], cc_[:, 0:n], pu[:, 0:n], ALU.mult, [B_cc, B_pu], B_pe)
            zc = []
            if halo_side != "left":
                zc.append(0)
            if halo_side != "right":
                zc.append(ntot + 1)
            for z in zc:
                P.emit("dve", lambda e, z=z: e.memset(pext[:, z:z + 1], 0.0), writes=B_pe)
            wb_, B_wb = wload(l, CH_CB + j)
            for ti, (hc, n, xi) in enumerate(main):
                pb, B_pb = proj_fm(wb_, B_wb, lambda kc: hg[:, kc, hc:hc + n], [B_hg], n)
                q_, B_q_ = tf[:, 4 + ti % 2, :], B_tf[4 + ti % 2]
                cw = V_CONV + l * 12 + j
                act_(q_[:, 0:n], pext[:, 1 + hc:1 + hc + n], AF.Identity, B_pe + [B_c], [B_q_], scale=V(cw + 4))
                stt(q_[:, 0:n], pext[:, hc:hc + n], V(cw), q_[:, 0:n], ALU.mult, ALU.add, B_pe + [B_c, B_q_], [B_q_])
                stt(q_[:, 0:n], pext[:, 2 + hc:2 + hc + n], V(cw + 8), q_[:, 0:n], ALU.mult, ALU.add, B_pe + [B_c, B_q_], [B_q_])
                tt(yc[:, j, hc:hc + n], pb[:, 0:n], q_[:, 0:n], ALU.mult, [B_pb, B_q_], [B_yc[j]])
        dump("yc_%s" % halo_side, gscr[:, 16400:20496], B_yc)
        st["tf"] = 0
        for m in range(8):
            wg = [wload(l, CH_G + 3 * m + b) for b in range(3)]
            wab, B_wab = wload(l, CH_WAB + m)
            for ti, (hc, n, xi) in enumerate(main):
                sg = []
                for b in range(3):
                    pg, B_pg = proj_fm(wg[b][0], wg[b][1], lambda kc: hg[:, kc, hc:hc + n], [B_hg], n)
                    s__, B_s = tf[:, b, :], B_tf[b]
                    act_(s__[:, 0:n], pg[:, 0:n], AF.Sigmoid, [B_pg], [B_s])
                    sg.append((s__, B_s))
                if ti == 0:
                    wcm, B_wcm = wload(l, CH_WC + m, 512)
                pa, B_pa = nps()
                mm(pa[:, 0:n], [(wab[:, kc * 128:(kc + 1) * 128], qy[:, kc, hc:hc + n]) for kc in range(4)],
                   [B_wab] + [B_qy[kc][ti] for kc in range(4)], B_pa)
                pb2, B_pb2 = nps()
                mm(pb2[:, 0:n], [(wab[:, (4 + kc) * 128:(5 + kc) * 128], qy[:, 4 + kc, hc:hc + n]) for kc in range(4)],
                   [B_wab] + [B_qy[4 + kc][ti] for kc in range(4)], B_pb2)
                pc2, B_pc2 = nps()
                mm(pc2[:, 0:n], [(wcm[:, kc * 128:(kc + 1) * 128], yc[:, kc, hc:hc + n]) for kc in range(4)],
                   [B_wcm] + B_yc, B_pc2)
                t1, B_t1 = tf[:, 3, :], B_tf[3]
                t2, B_t2 = tf[:, 4, :], B_tf[4]
                tt(t1[:, 0:n], pa[:, 0:n], sg[0][0][:, 0:n], ALU.mult, [B_pa, sg[0][1]], [B_t1])
                tt(t2[:, 0:n], pb2[:, 0:n], sg[1][0][:, 0:n], ALU.mult, [B_pb2, sg[1][1]], [B_t2])
                tt(t1[:, 0:n], t1[:, 0:n], t2[:, 0:n], ALU.add, [B_t1, B_t2], [B_t1])
                tt(t2[:, 0:n], pc2[:, 0:n], sg[2][0][:, 0:n], ALU.mult, [B_pc2, sg[2][1]], [B_t2])
                tt(merged[:, m, hc:hc + n], t1[:, 0:n], t2[:, 0:n], ALU.add, [B_t1, B_t2], [B_mg[m][ti], B_rope])
        dump("mg_%s" % halo_side, mreg, all_mg())

    def spread(items, norms):
        sched = {}
        pos = 0
        for (nsq, nmm, nfin) in norms:
            for j in range(4):
                sched.setdefault(pos + j, []).append((nsq, 2 * j))
                sched.setdefault(pos + j, []).append((nsq, 2 * j + 1))
                sched.setdefault(pos + j + 1, []).append((nmm, 2 * j))
                sched.setdefault(pos + j + 1, []).append((nmm, 2 * j + 1))
            sched.setdefault(pos + 5, []).append((nfin, None))
            pos += 6
        assert pos <= len(items), (pos, len(items))
        out = []
        for j, (A, Bf) in enumerate(items):
            def A2(A=A, j=j):
                for f, k in sorted(sched.get(j, []), key=lambda fk: 0 if fk[0].__name__ == "mmk" else 1):
                    f(k) if k is not None else f()
                return A()
            out.append((A2, Bf))
        return out

    def tile_norms(l, s_, which, main):
        return [norm_parts(l, s_, which, TILES[xi][0], n, hg[:, :, hc:hc + n], B_hg, [B_x[xi]],
                           sqbufs=[(sqd[:, i, :], B_sqd[i]) for i in range(4)], bank=7) for (hc, n, xi) in main]

    def group_wout(l, s_, main, norms=()):
        items = []
        wq = {}
        for nn in range(8):
            for ti, (hc, n, xi) in enumerate(main):
                def A(nn=nn, ti=ti, hc=hc, n=n):
                    if ti == 0:
                        wq[nn] = wload(l, CH_WOUT + nn)
                    w, B_w = wq[nn]
                    po, B_po = nps()
                    mm(po[:, 0:n], [(w[:, m * 128:(m + 1) * 128], merged[:, m, hc:hc + n]) for m in range(8)],
                       [B_w] + [B_mg[m][ti] for m in range(8)], B_po)
                    return po, B_po

                def Bf(po, B_po, nn=nn, n=n, xi=xi):
                    xo = TILES[xi][0]
                    stt(xT[:, nn, xo:xo + n], po[:, 0:n], mod_ap(l, 2, nn, s_), xT[:, nn, xo:xo + n], ALU.mult, ALU.add,
                        [B_po, B_c, B_x[xi]], [B_x[xi]])
                items.append((A, Bf))
        st["rsv7"] = True
        run_items(spread(items, norms) if norms and len(items) >= 6 * len(norms) else items)
        st["rsv7"] = False

    def group_wout_old(l, s_, main):
        for nn in range(8):
            w, B_w = wload(l, CH_WOUT + nn)
            for ti, (hc, n, xi) in enumerate(main):
                po, B_po = nps()
                mm(po[:, 0:n], [(w[:, m * 128:(m + 1) * 128], merged[:, m, hc:hc + n]) for m in range(8)],
                   [B_w] + [B_mg[m][ti] for m in range(8)], B_po)
                xo = TILES[xi][0]
                stt(xT[:, nn, xo:xo + n], po[:, 0:n], mod_ap(l, 2, nn, s_), xT[:, nn, xo:xo + n], ALU.mult, ALU.add,
                    [B_po, B_c, B_x[xi]], [B_x[xi]])

    def ffn_norm(l, s_, main):
        for ti, (hc, n, xi) in enumerate(main):
            norm_tile(l, s_, 1, TILES[xi][0], n, hg[:, :, hc:hc + n], B_hg, [B_x[xi]])

    def ffn_gu(l, s_, main):
        for j in range(NJ):
            wg_, B_wg = wload(l, CH_GU + 2 * j)
            wu_, B_wu = wload(l, CH_GU + 2 * j + 1)
            for ti, (hc, n, xi) in enumerate(main):
                pg, B_pg = proj_fm(wg_, B_wg, lambda kc: hg[:, kc, hc:hc + n], [B_hg], n)
                pu, B_pu = proj_fm(wu_, B_wu, lambda kc: hg[:, kc, hc:hc + n], [B_hg], n)
                sl, B_sl = ntf()
                act_(sl[:, 0:n], pg[:, 0:n], AF.Silu, [B_pg], [B_sl])
                tt(act[:, j, hc:hc + n], sl[:, 0:n], pu[:, 0:n], ALU.mult, [B_sl, B_pu], [B_act[j][ti]])

    def ffn_down(l, s_, main, norms=()):
        items = []
        wq = {}
        for nn in range(8):
            for ti, (hc, n, xi) in enumerate(main):
                def A(nn=nn, ti=ti, hc=hc, n=n):
                    if ti == 0:
                        wq[nn] = [wload(l, CH_WD + 3 * nn + i, 1024 if i < 2 else 768) for i in range(3)]
                    ws_ = wq[nn]
                    po, B_po = nps()
                    mm(po[:, 0:n], [(ws_[j // 8][0][:, (j % 8) * 128:(j % 8 + 1) * 128], act[:, j, hc:hc + n]) for j in range(NJ)],
                       [w_[1] for w_ in ws_] + [B_act[j][ti] for j in range(NJ)], B_po)
                    return po, B_po

                def Bf(po, B_po, nn=nn, n=n, xi=xi):
                    xo = TILES[xi][0]
                    stt(xT[:, nn, xo:xo + n], po[:, 0:n], mod_ap(l, 5, nn, s_), xT[:, nn, xo:xo + n], ALU.mult, ALU.add,
                        [B_po, B_c, B_x[xi]], [B_x[xi]])
                items.append((A, Bf))
        st["rsv7"] = True
        run_items(spread(items, norms) if norms and len(items) >= 6 * len(norms) else items)
        st["rsv7"] = False

    def ffn_down_old(l, s_, main):
        for nn in range(8):
            ws_ = [wload(l, CH_WD + 3 * nn + i, 1024 if i < 2 else 768) for i in range(3)]
            for ti, (hc, n, xi) in enumerate(main):
                po, B_po = nps()
                mm(po[:, 0:n], [(ws_[j // 8][0][:, (j % 8) * 128:(j % 8 + 1) * 128], act[:, j, hc:hc + n]) for j in range(NJ)],
                   [w_[1] for w_ in ws_] + [B_act[j][ti] for j in range(NJ)], B_po)
                xo = TILES[xi][0]
                stt(xT[:, nn, xo:xo + n], po[:, 0:n], mod_ap(l, 5, nn, s_), xT[:, nn, xo:xo + n], ALU.mult, ALU.add,
                    [B_po, B_c, B_x[xi]], [B_x[xi]])

    def load_x(q):
        for i in range(TOK // 128):
            src = ctx_d[q, i * 128:(i + 1) * 128, :] if i < 2 else x_d[q, (i - 2) * 128:(i - 1) * 128, :]
            s_ = i % 2
            stg = tf[:, 2 * s_:2 * s_ + 2, :].rearrange("p a t -> p (a t)")
            B_s = B_tf[2 * s_:2 * s_ + 2]
            P.dma("sp" if s_ == 0 else "pool", stg, src, writes=B_s)
            xi = [j for j, (o, n) in enumerate(TILES) if o <= i * 128 < o + n][0]
            for half in range(2):
                pt, B_pt = nps()
                for k4 in range(4):
                    kc = half * 4 + k4
                    P.emit("pe", lambda e, pt=pt, k4=k4, kc=kc, stg=stg: e.transpose(out=pt[:, k4 * 128:(k4 + 1) * 128], in_=stg[:, kc * 128:(kc + 1) * 128],
                                                                                    identity=identF), reads=B_s + [B_c], writes=[B_pt], inc=(k4 == 3))
                dst = xT[:, half * 4:half * 4 + 4, i * 128:(i + 1) * 128]
                srcp = pt.rearrange("p (a t) -> p a t", a=4)
                if half == 0:
                    P.emit("act", lambda e, dst=dst, srcp=srcp: e.activation(out=dst, in_=srcp, func=AF.Copy), reads=[B_pt], writes=[B_x[xi]])
                else:
                    P.emit("dve", lambda e, dst=dst, srcp=srcp: e.tensor_copy(out=dst, in_=srcp), reads=[B_pt], writes=[B_x[xi]])

    def store_tokens(q, tok0, ntok, dst_d, dst_row0, norm):
        yT = gF.rearrange("p (c t) -> p c t", c=8)
        for t0 in range(0, ntok, 512):
            n = min(512, ntok - t0)
            xo = tok0 + t0
            xi = [j for j, (o, nn) in enumerate(TILES) if o <= xo < o + nn][0]
            if norm:
                pss, B_pss = nps()
                for kc in range(KC):
                    sq, B_sq = ntb()
                    act_(sq[:, 0:n], xT[:, kc, xo:xo + n], AF.Square, [B_x[xi]], [B_sq])
                    P.emit("pe", lambda e, sq=sq, kc=kc, pss=pss, n=n: e.matmul(pss[:, 0:n], lhsT=OnesD, rhs=sq[:, 0:n], start=(kc == 0), stop=(kc == KC - 1)),
                           reads=[B_sq, B_c], writes=[B_pss], inc=True)
                rs, B_rs = rstd_from(pss, B_pss, n)
                for kc in range(KC):
                    stt(yT[:, kc, 0:n], xT[:, kc, xo:xo + n], V(V_FG + kc), rs[:, 0:n], ALU.mult, ALU.mult, [B_x[xi], B_rs, B_c], [B_gF])
                srcT, B_src = (lambda kc, a: yT[:, kc, a * 128:(a + 1) * 128]), [B_gF]
            else:
                srcT, B_src = (lambda kc, a, xo=xo: xT[:, kc, xo + a * 128:xo + (a + 1) * 128]), [B_x[xi]]
            for a in range(n // 128):
                s_ = a % 2
                stg = tf[:, 2 * s_:2 * s_ + 2, :].rearrange("p a t -> p (a t)")
                B_s = B_tf[2 * s_:2 * s_ + 2]
                for half in range(2):
                    pt, B_pt = nps()
                    for k4 in range(4):
                        kc = half * 4 + k4
                        P.emit("pe", lambda e, pt=pt, k4=k4, kc=kc, a=a, srcT=srcT: e.transpose(
                            out=pt[:, k4 * 128:(k4 + 1) * 128], in_=srcT(kc, a), identity=identF),
                            reads=B_src + [B_c], writes=[B_pt], inc=(k4 == 3))
                    if half == 0:
                        P.emit("act", lambda e, pt=pt, stg=stg: e.activation(out=stg[:, 0:512], in_=pt, func=AF.Copy), reads=[B_pt], writes=B_s)
                    else:
                        P.emit("dve", lambda e, pt=pt, stg=stg: e.tensor_copy(out=stg[:, 512:1024], in_=pt), reads=[B_pt], writes=B_s)
                r0 = dst_row0 + t0 + a * 128
                P.dma("sp" if s_ == 0 else "pool", dst_d[q, r0:r0 + 128, :], stg, reads=B_s)

    for q in range(nseq):
        load_x(q)
        if q == 0:
            mod_prologue()
            P.barrier()
        for li, l in enumerate(layers):
            last = (l == NL - 1)
            kv_phase(l, q, q)
            groups = []
            if not last:
                groups.append((2, [(0, 256, 0)], [(0, 256, 0)], "none", None, 2))
            groups.append((q, [(256, 512, 0), (768, 512, 512), (1280, 1, 1024)], [(0, 512, 1), (512, 512, 2)], "right", 0, 18))
            groups.append((q, [(1280, 512, 0), (1792, 512, 512), (1279, 1, 1024)], [(0, 512, 3), (512, 512, 4)], "left", 1024, 18))
            ng = len(groups)
            group_norm(l, groups[0][0], groups[0][1])
            for gi, (s_, pieces, main, hside, lat0, nkc) in enumerate(groups):
                group_phase(l, s_, pieces, main, hside, lat0, nkc)
                if gi + 1 < ng:
                    ns_, npieces, nmain = groups[gi + 1][0], groups[gi + 1][1], groups[gi + 1][2]
                    halo = [p_ for p_ in npieces if p_[1] == 1]
                    if halo:
                        group_norm(l, ns_, halo)
                    if len(main) * 8 >= 6 * len(nmain):
                        group_wout(l, s_, main, tile_norms(l, ns_, 0, nmain))
                    else:
                        group_norm(l, ns_, [p_ for p_ in npieces if p_[1] != 1])
                        group_wout(l, s_, main)
                else:
                    group_wout(l, s_, main, tile_norms(l, groups[0][0], 1, groups[0][2]))
            P.barrier()
            dump("xT_mid", xT.rearrange("p c t -> p (c t)"), B_x)
            for gi, (s_, pieces, main, hside, lat0, nkc) in enumerate(groups):
                ffn_gu(l, s_, main)
                if gi + 1 < ng and len(main) * 8 >= 6 * len(groups[gi + 1][2]):
                    ffn_down(l, s_, main, tile_norms(l, groups[gi + 1][0], 1, groups[gi + 1][2]))
                else:
                    if gi + 1 < ng:
                        ffn_norm(l, groups[gi + 1][0], groups[gi + 1][2])
                    ffn_down(l, s_, main)
            P.barrier()
        store_tokens(q, CTX, SEQ, out_d, 0, final_norm)
        if out_ctx:
            store_tokens(q, 0, CTX, octx_d, 0, False)
        P.barrier()
    P.finish("sp")
    P.build()
    return nc


def _fm_chunk(w, rows, cols):
    raise NotImplementedError


def _prep_weights(inp):
    w_in = np.asarray(inp["w_in"], np.float32)
    wfm = np.zeros((NL, NCHUNK, 128, 8, 128), np.float32)
    wv = np.zeros((NL, 128, 8, 640), np.float32)

    def fm(wcols):
        return wcols.reshape(8, 128, 128).transpose(1, 0, 2)

    oq, okk, ov, obq, obk, obv, ocb, occ, ocu, og = 0, 512, 1024, 1536, 2048, 2176, 2304, 2816, 3328, 3840
    for l in range(NL):
        W = w_in[l]
        for c in range(4):
            wfm[l, CH_AK + c] = fm(W[:, okk + c * 128: okk + (c + 1) * 128])
            wfm[l, CH_AQ + c] = fm(W[:, oq + c * 128: oq + (c + 1) * 128])
            cols = np.concatenate([np.arange(obq + c * 64, obq + (c + 1) * 64), np.arange(obq + (c + 4) * 64, obq + (c + 5) * 64)])
            wfm[l, CH_BQ + c] = fm(W[:, cols])
            wfm[l, CH_CC + c] = fm(W[:, occ + c * 128: occ + (c + 1) * 128])
            wfm[l, CH_CU + c] = fm(W[:, ocu + c * 128: ocu + (c + 1) * 128])
            wfm[l, CH_CB + c] = fm(W[:, ocb + c * 128: ocb + (c + 1) * 128])
        wfm[l, CH_BK] = fm(W[:, obk: obk + 128])
        for m in range(8):
            for b in range(3):
                wfm[l, CH_G + 3 * m + b] = fm(W[:, og + b * 1024 + m * 128: og + b * 1024 + (m + 1) * 128])
        wv[l, :, :, 0:512] = W[:, ov:ov + 512].reshape(8, 128, 512).transpose(1, 0, 2)
        wv[l, :, :, 512:640] = W[:, obv:obv + 128].reshape(8, 128, 128).transpose(1, 0, 2)
        wa = np.asarray(inp["w_branch_a"], np.float32)[l]
        wb = np.asarray(inp["w_branch_b"], np.float32)[l]
        wc = np.asarray(inp["w_branch_c"], np.float32)[l]
        perm = np.concatenate([np.concatenate([np.arange(c * 64, (c + 1) * 64), np.arange((c + 4) * 64, (c + 5) * 64)]) for c in range(4)])
        wbp = wb[perm]
        for m in range(8):
            wfm[l, CH_WAB + m, :, 0:4, :] = wa[:, m * 128:(m + 1) * 128].reshape(4, 128, 128).transpose(1, 0, 2)
            wfm[l, CH_WAB + m, :, 4:8, :] = wbp[:, m * 128:(m + 1) * 128].reshape(4, 128, 128).transpose(1, 0, 2)
            wfm[l, CH_WC + m, :, 0:4, :] = wc[:, m * 128:(m + 1) * 128].reshape(4, 128, 128).transpose(1, 0, 2)
            wfm[l, CH_WOUT + m] = fm(np.asarray(inp["w_out"], np.float32)[l][:, m * 128:(m + 1) * 128])
        gu = np.asarray(inp["w_ffn_gu"], np.float32)[l]
        for j in range(NJ):
            wfm[l, CH_GU + 2 * j] = fm(gu[:, j * 128:(j + 1) * 128])
            wfm[l, CH_GU + 2 * j + 1] = fm(gu[:, DFF + j * 128: DFF + (j + 1) * 128])
        wd = np.asarray(inp["w_ffn_down"], np.float32)[l]
        for nn in range(8):
            blk = wd[:, nn * 128:(nn + 1) * 128].reshape(NJ, 128, 128).transpose(1, 0, 2)
            for i in range(3):
                nt = 8 if i < 2 else 6
                wfm[l, CH_WD + 3 * nn + i, :, 0:nt, :] = blk[:, i * 8:i * 8 + nt, :]
    wmod = np.asarray(inp["w_mod"], np.float32).reshape(NL, 8, 128, 48, 128).transpose(0, 3, 2, 1, 4)
    bmod = np.asarray(inp["b_mod"], np.float32).reshape(NL, 48, 128).transpose(2, 0, 1)
    vecs = np.zeros((128, NV + NVL), np.float32)
    for l in range(NL):
        vecs[:, V_N1G + l * 8:V_N1G + l * 8 + 8] = np.asarray(inp["norm1_g"], np.float32)[l].reshape(8, 128).T
        vecs[:, V_N2G + l * 8:V_N2G + l * 8 + 8] = np.asarray(inp["norm2_g"], np.float32)[l].reshape(8, 128).T
        cw = np.asarray(inp["conv_w"], np.float32)[l]
        for k in range(3):
            vecs[:, V_CONV + l * 12 + k * 4: V_CONV + l * 12 + k * 4 + 4] = cw[k].reshape(4, 128).T
        vecs[:, V_SUB + l] = np.asarray(inp["diff_subln_g"], np.float32)[l]
        pidx = np.arange(128) % 64
        for col, name in ((V_QG, "q_norm_g"), (V_KG, "k_norm_g")):
            g = np.asarray(inp[name], np.float32)[l]
            vecs[:, col + l * 2] = g[pidx]
            vecs[:, col + l * 2 + 1] = g[pidx ^ 16]
        for a, (n1, n2) in enumerate((("lam_q1", "lam_k1"), ("lam_q2", "lam_k2"))):
            base = V_LAM + l * 256 + a * 128
            vecs[:, base:base + 64] = np.asarray(inp[n1], np.float32)[l][None, :]
            vecs[:, base + 64:base + 128] = np.asarray(inp[n2], np.float32)[l][None, :]
    vecs[:, V_FG:V_FG + 8] = np.asarray(inp["final_g"], np.float32).reshape(8, 128).T
    rows = SEQ // GRID_W
    row = np.repeat(np.arange(rows, dtype=np.float32), GRID_W)
    col = np.tile(np.arange(GRID_W, dtype=np.float32), rows)
    nf = 16
    inv_freq = (np.float32(10000.0) ** (-np.arange(nf, dtype=np.float32) / np.float32(nf))).astype(np.float32)
    rope = np.zeros((128, 2, SEQ), np.float32)
    for p in range(128):
        d = p % 64
        axis, half, f = d // 32, (d // 16) % 2, d % 16
        ang = ((row if axis == 0 else col) * inv_freq[f]).astype(np.float32)
        rope[p, 0] = np.cos(ang)
        rope[p, 1] = np.sin(ang) * (-1.0 if half == 0 else 1.0)
    cst = np.zeros((128, 5, 128), np.float32)
    cst[:, 0] = np.eye(128, dtype=np.float32)
    for p in range(128):
        cst[p ^ 16, 1, p] = 1.0
    cst[0:64, 2, 0:64] = 1.0 / 64
    cst[64:128, 2, 64:128] = 1.0 / 64
    cst[:, 3] = 1.0 / 1024
    cst[:, 4] = 1.0 / 128
    return dict(wfm=np.ascontiguousarray(wfm.reshape(NL, NCHUNK, 128, 1024)),
                wv=np.ascontiguousarray(wv.reshape(NL, 128, 5120)),
                wmod=np.ascontiguousarray(wmod.reshape(NL, 48, 128, 1024)),
                bmod=np.ascontiguousarray(bmod.reshape(128, NL * 48)),
                vecs=vecs, rope=np.ascontiguousarray(rope.reshape(128, 2 * SEQ)),
                cst=np.ascontiguousarray(cst.reshape(128, 640)))


def _core_maps(shared, x, ctx, c, c_ctx, nseq, ncores):
    maps = []
    for i in range(ncores):
        cT = np.zeros((128, 8, 4), np.float32)
        for s_ in range(nseq):
            cT[:, :, s_] = c[i * nseq + s_].reshape(8, 128).T
        cT[:, :, 2] = c_ctx.reshape(8, 128).T
        m = dict(shared)
        m["x"] = np.ascontiguousarray(x[i * nseq:(i + 1) * nseq])
        m["ctx"] = np.ascontiguousarray(ctx[i * nseq:(i + 1) * nseq])
        m["cT"] = np.ascontiguousarray(cT.reshape(128, 32))
        maps.append(m)
    return maps


FUSED = True


def kernel(**inp):
    x = np.asarray(inp["x"], np.float32)
    ctx = np.asarray(inp["ctx"], np.float32)
    c = np.asarray(inp["c"], np.float32)
    c_ctx = np.asarray(inp["c_ctx"], np.float32)
    shared = _prep_weights(inp)
    cores = list(range(NCORES))
    if FUSED:
        nc = build_program([0, 1], True, False)
        res = run_bass_kernel_spmd(nc, _core_maps(shared, x, ctx, c, c_ctx, NSEQ, NCORES), core_ids=cores)
        return np.concatenate([np.asarray(r["out"], np.float32) for r in res.results], axis=0)
    nc0 = build_program([0], False, True)
    res = run_bass_kernel_spmd(nc0, _core_maps(shared, x, ctx, c, c_ctx, NSEQ, NCORES), core_ids=cores)
    x1 = np.concatenate([np.asarray(r["out"], np.float32) for r in res.results], axis=0)
    ctx1 = np.concatenate([np.asarray(r["ctx_out"], np.float32) for r in res.results], axis=0)
    nc1 = build_program([1], True, False)
    res = run_bass_kernel_spmd(nc1, _core_maps(shared, x1, ctx1, c, c_ctx, NSEQ, NCORES), core_ids=cores)
    return np.concatenate([np.asarray(r["out"], np.float32) for r in res.results], axis=0)
```

```python
import math
import contextlib
import numpy as np
import concourse.bass as bass
import concourse.mybir as mybir
from concourse.bass_utils import run_bass_kernel_spmd

F32 = mybir.dt.float32
BF16 = mybir.dt.bfloat16
AF = mybir.ActivationFunctionType
ALU = mybir.AluOpType
AX = mybir.AxisListType

D = 1024
KC = 8
SEQ = 2048
CTX = 256
TOK = CTX + SEQ
NL = 2
NCORES = 8
NSEQ = 2
EPS = 1e-6
DFF = 2816
NJ = 22
GRID_W = 64

CH_AK, CH_BK, CH_AQ, CH_BQ, CH_CC, CH_CU, CH_CB, CH_G, CH_WAB, CH_WC, CH_WOUT, CH_GU, CH_WD = (
    0, 4, 5, 9, 13, 17, 21, 25, 49, 57, 65, 73, 117)
NCHUNK = 141
V_N1G, V_N2G, V_FG, V_CONV, V_SUB, V_QG, V_KG, V_LAM = 0, 16, 32, 40, 64, 66, 70, 74
NV = 74
NVL = 512


class Buf:
    __slots__ = ("w", "r", "excl")

    def __init__(self, excl=False):
        self.w = None
        self.r = []
        self.excl = excl


class Prog:
    ENGS = ("pe", "act", "dve", "pool", "sp")

    def __init__(self, nc, n_dma_sems=32):
        self.nc = nc
        self.ops = {e: [] for e in self.ENGS}
        self.cnt = {e: 0 for e in self.ENGS}
        self.waited = {e: {} for e in self.ENGS}
        self.n_dma_sems = n_dma_sems
        self.dcum = [0] * n_dma_sems
        self.dnext = 0
        self.dnext2 = [0, 0]
        self.self_wait = {"pe": False, "act": True, "dve": True, "pool": True, "sp": False}

    def _need(self, eng, tok):
        if tok is None:
            return
        sid, val = tok
        if sid == eng and not self.self_wait[eng]:
            return
        if self.waited[eng].get(sid, 0) >= val:
            return
        if sid in self.cnt and val > self.cnt[sid]:
            raise RuntimeError("wait on pending (future) token %s %d > %d from %s" % (sid, val, self.cnt[sid], eng))
        self.waited[eng][sid] = val
        self.ops[eng].append(("wait", sid, val))

    def _deps(self, eng, reads, writes):
        for b in reads:
            self._need(eng, b.w)
            if b.excl:
                for t in b.r:
                    if t[0] != eng:
                        self._need(eng, t)
        for b in writes:
            self._need(eng, b.w)
            for t in b.r:
                self._need(eng, t)

    def _upd(self, tok, reads, writes):
        for b in writes:
            b.w = tok
            b.r = []
        for b in reads:
            r = b.r
            for i in range(len(r)):
                if r[i][0] == tok[0]:
                    r[i] = tok
                    break
            else:
                r.append(tok)

    def emit(self, eng, fn, reads=(), writes=(), inc=True):
        self._deps(eng, reads, writes)
        tok = (eng, self.cnt[eng] + 1)
        if inc:
            self.cnt[eng] += 1
        self.ops[eng].append(("op", fn, inc))
        self._upd(tok, reads, writes)
        return tok

    def dma(self, q, out, in_, reads=(), writes=()):
        half = self.n_dma_sems // 2
        qi = 1 if q == "pool" else 0
        k = qi * half + self.dnext2[qi]
        self.dnext2[qi] = (self.dnext2[qi] + 1) % half
        sid = ("d", k)
        if self.dcum[k] > 0:
            self._need(q, (sid, self.dcum[k]))
        self._deps(q, reads, writes)
        self.dcum[k] += 16
        tok = (sid, self.dcum[k])
        self.ops[q].append(("dma", out, in_, k))
        self._upd(tok, reads, writes)
        return tok

    def barrier(self, engs=("pe", "act", "dve")):
        for e in engs:
            for f in engs:
                if f != e and self.cnt[f] > 0:
                    self._need(e, (f, self.cnt[f]))

    def finish(self, q="sp"):
        for k in range(self.n_dma_sems):
            if self.dcum[k] > 0:
                self._need(q, (("d", k), self.dcum[k]))

    def build(self):
        nc = self.nc
        with contextlib.ExitStack() as st:
            esem = {e: st.enter_context(nc.semaphore("s_" + e)) for e in self.ENGS}
            dsem = [st.enter_context(nc.semaphore("d_%d" % i)) for i in range(self.n_dma_sems)]

            def getsem(sid):
                return dsem[sid[1]] if isinstance(sid, tuple) else esem[sid]

            block = st.enter_context(nc.Block())

            def run(ename):
                def f(e):
                    for op in self.ops[ename]:
                        if op[0] == "wait":
                            e.wait_ge(getsem(op[1]), op[2])
                        elif op[0] == "op":
                            ins = op[1](e)
                            if op[2]:
                                ins.then_inc(esem[ename], 1)
                        else:
                            e.dma_start(out=op[1], in_=op[2]).then_inc(dsem[op[3]], 16)
                return f

            block.tensor(run("pe"))
            block.scalar(run("act"))
            block.vector(run("dve"))
            block.gpsimd(run("pool"))
            block.sync(run("sp"))


def lam_init_of(l):
    return 0.8 - 0.6 * math.exp(-0.3 * l)


def build_program(layers, final_norm, out_ctx, nseq=NSEQ, dbg=False):
    nc = bass.Bass("TRN2", target_bir_lowering=False, dynamic_dma_scratch_size=8192)
    nl = len(layers)
    x_d = nc.dram_tensor("x", [nseq, SEQ, D], F32, kind="ExternalInput").ap()
    ctx_d = nc.dram_tensor("ctx", [nseq, CTX, D], F32, kind="ExternalInput").ap()
    cT_d = nc.dram_tensor("cT", [128, KC * 4], F32, kind="ExternalInput").ap()
    wmod_d = nc.dram_tensor("wmod", [NL, 48, 128, 1024], F32, kind="ExternalInput").ap()
    bmod_d = nc.dram_tensor("bmod", [128, NL * 48], F32, kind="ExternalInput").ap()
    vecs_d = nc.dram_tensor("vecs", [128, NV + NVL], F32, kind="ExternalInput").ap()
    rope_d = nc.dram_tensor("rope", [128, 2 * SEQ], F32, kind="ExternalInput").ap()
    cst_d = nc.dram_tensor("cst", [128, 5 * 128], F32, kind="ExternalInput").ap()
    wfm_d = nc.dram_tensor("wfm", [NL, NCHUNK, 128, 1024], F32, kind="ExternalInput").ap()
    wv_d = nc.dram_tensor("wv", [NL, 128, KC * 640], F32, kind="ExternalInput").ap()
    out_d = nc.dram_tensor("out", [nseq, SEQ, D], F32, kind="ExternalOutput").ap()
    if out_ctx:
        octx_d = nc.dram_tensor("ctx_out", [nseq, CTX, D], F32, kind="ExternalOutput").ap()

    P = Prog(nc)
    dbg_list = []

    def dump(key, ap, bufs):
        if not dbg or key in dbg_list:
            return
        dbg_list.append(key)
        shp = list(ap.shape)
        d = nc.dram_tensor("dbg_" + key, shp, ap.dtype, kind="ExternalOutput").ap()
        P.dma("sp", d, ap, reads=bufs)

    def sb(name, shape, dt):
        return nc.alloc_sbuf_tensor("sb_" + name, shape, dt).ap()

    xT = sb("xT", [128, KC, TOK], F32)
    kvreg = sb("kvreg", [128, 24192], BF16)
    gscr = sb("gscr", [128, 8208 + 8192 + 4096], BF16)
    mreg = sb("mreg", [128, 8192], BF16)
    NWS = 7
    wsl = [sb("wsl%d" % i, [128, 1024], BF16) for i in range(NWS)]
    tf = sb("tf", [128, 6, 512], F32)
    tb = sb("tb", [128, 6, 512], BF16)
    sqd = sb("sqd", [128, 4, 512], BF16)
    identF = sb("identF", [128, 128], F32)
    cstb = sb("cstb", [128, 4, 128], BF16)
    onesb = sb("onesb", [128, 128], BF16)
    modt = sb("modt", [128, NL, 48, 4], F32)
    drv = sb("drv", [128, NL, 3, 2, KC], F32)
    vecs = sb("vecs", [128, NV], F32)
    lamt = sb("lamt", [128, NL, 4], F32)
    subs = sb("subs", [128, NL], F32)
    scT = sb("scT", [128, KC, 4], F32)
    bmod = sb("bmod", [128, NL, 48], F32)

    ak = kvreg[:, 0:9216].rearrange("p (c t) -> p c t", c=4)
    bk = kvreg[:, 9216:11520]
    av = kvreg[:, 11520:20736].rearrange("p (k c) -> p k c", k=18)
    bv = kvreg[:, 20736:24192].rearrange("p (k c) -> p k c", k=18)
    act = kvreg[:, 0:22528].rearrange("p (j t) -> p j t", j=NJ)
    hg = gscr[:, 0:8200].rearrange("p (c t) -> p c t", c=8)
    hs = [gscr[:, 0:4096].rearrange("p (c t) -> p c t", c=8), gscr[:, 4096:8192].rearrange("p (c t) -> p c t", c=8)]
    qy = gscr[:, 8208:16400].rearrange("p (c t) -> p c t", c=8)
    yc = gscr[:, 16400:20496].rearrange("p (c t) -> p c t", c=4)
    kw = gscr[:, 8208:13328].rearrange("p (i c) -> p i c", i=5)
    vw = gscr[:, 13328:18448].rearrange("p (k c) -> p k c", k=8)
    merged = mreg.rearrange("p (c t) -> p c t", c=8)
    ropeT = mreg.bitcast(F32).rearrange("p (a t) -> p a t", a=2)
    gF = gscr[:, 0:16384].bitcast(F32)
    Pm, Blk, OnesD, OnesV = cstb[:, 0, :], cstb[:, 1, :], cstb[:, 2, :], cstb[:, 3, :]

    psall = nc.alloc_psum_tensor("psall", [128, 8, 512], F32).ap()
    ps = [psall[:, i, :] for i in range(8)]
    B_ps = [Buf(True) for _ in range(8)]

    B_x = [Buf() for _ in range(5)]
    B_ak, B_bk, B_av, B_bv = Buf(), Buf(), Buf(), Buf()
    B_hs = [Buf(), Buf()]
    B_hg = Buf()
    B_qy = [[Buf(), Buf()] for _ in range(8)]
    B_yc = [Buf() for _ in range(4)]
    B_kw = [Buf() for _ in range(5)]
    B_vw = Buf()
    B_mg = [[Buf(), Buf()] for _ in range(8)]
    B_rope = Buf()
    B_ws = [Buf() for _ in range(NWS)]
    B_tf = [Buf() for _ in range(6)]
    B_tb = [Buf() for _ in range(6)]
    B_sqd = [Buf() for _ in range(4)]
    B_act = [[Buf(), Buf()] for _ in range(NJ)]
    B_c = Buf()
    B_gF = Buf()
    NWM = 7
    B_wm = [Buf() for _ in range(NWM)]
    st = {"ws": 0, "tf": 0, "tb": 0, "ps": 0, "tbp": 0, "wsf": 0, "ffn_fence": False}
    NWX = 12
    wslx = [gscr[:, 8208 + 1024 * i:8208 + 1024 * (i + 1)] for i in range(NWX)]
    B_wsx = [Buf() for _ in range(NWX)]

    def all_qy():
        return [b for r in B_qy for b in r]

    def all_mg():
        return [b for r in B_mg for b in r]

    def ntf():
        i = 2 + st["tf"] % 4
        st["tf"] = (st["tf"] + 1) % 4
        return tf[:, i, :], B_tf[i]

    def ntb():
        i = st["tb"]
        st["tb"] = (i + 1) % 6
        return tb[:, i, :], B_tb[i]

    def nps():
        nb = 7 if st.get("rsv7") else 8
        i = st["ps"] % nb
        st["ps"] = (i + 1) % nb
        return ps[i], B_ps[i]

    def wload(l, ch, ncols=1024, ffn=False):
        if ffn:
            i = st["wsf"]
            st["wsf"] = (i + 1) % (NWS + NWX)
            if i >= NWS:
                k = i - NWS
                extra = []
                if st["ffn_fence"]:
                    extra = all_qy() + B_yc + B_kw + [B_vw]
                    st["ffn_fence"] = False
                P.dma("pool", wslx[k][:, 0:ncols], wfm_d[l, ch, :, 0:ncols], writes=[B_wsx[k]] + extra)
                return wslx[k], B_wsx[k]
        else:
            i = st["ws"]
            st["ws"] = (i + 1) % NWS
        P.dma("pool", wsl[i][:, 0:ncols], wfm_d[l, ch, :, 0:ncols], writes=[B_ws[i]])
        return wsl[i], B_ws[i]

    def mm(out, pairs, reads, wbuf):
        n = len(pairs)
        for i, (l, r) in enumerate(pairs):
            P.emit("pe", lambda e, l=l, r=r, i=i: e.matmul(out, lhsT=l, rhs=r, start=(i == 0), stop=(i == n - 1)),
                   reads=reads, writes=[wbuf], inc=(i == n - 1))

    def V(col):
        return vecs[:, col:col + 1]

    def act_(out, in_, func, reads, writes, **kw_):
        P.emit("act", lambda e: e.activation(out=out, in_=in_, func=func, **kw_), reads=reads, writes=writes)

    def tt(out, in0, in1, op, reads, writes):
        P.emit("dve", lambda e: e.tensor_tensor(out=out, in0=in0, in1=in1, op=op), reads=reads, writes=writes)

    def stt(out, in0, scalar, in1, op0, op1, reads, writes):
        P.emit("dve", lambda e: e.scalar_tensor_tensor(out=out, in0=in0, scalar=scalar, in1=in1, op0=op0, op1=op1),
               reads=reads, writes=writes)

    def recip(out, in_, reads, writes):
        P.emit("dve", lambda e: e.reciprocal(out=out, in_=in_), reads=reads, writes=writes)

    def rstd_from(pss, B_pss, n, reads_extra=()):
        sd, B_sd = tf[:, 0, :], B_tf[0]
        act_(sd[:, 0:n], pss[:, 0:n], AF.Ln, [B_pss] + list(reads_extra), [B_sd], bias=epsc[:, 0:1])
        rs, B_rs = tf[:, 1, :], B_tf[1]
        act_(rs[:, 0:n], sd[:, 0:n], AF.Exp, [B_sd], [B_rs], scale=-0.5)
        return rs, B_rs

    if dbg:
        P.emit("dve", lambda e: e.memset(modt.rearrange("p l j s -> p (l j s)"), 0.0), writes=[B_c, B_gF] + B_wm)
        P.emit("dve", lambda e: e.memset(drv.rearrange("p l s a k -> p (l s a k)"), 0.0), writes=[B_c, B_gF] + B_wm)
        P.emit("dve", lambda e: e.memset(lamt.rearrange("p l a -> p (l a)"), 0.0), writes=[B_c, B_gF] + B_wm)
        P.emit("dve", lambda e: e.memset(gscr, 0.0), writes=[B_c, B_gF] + B_wm)
        P.emit("dve", lambda e: e.memset(mreg, 0.0), writes=[B_c, B_gF] + B_wm)
    P.dma("sp", vecs, vecs_d[:, 0:NV], writes=[B_c])
    lamv = tf[:, 0, :]
    P.dma("sp", bmod.rearrange("p l j -> p (l j)"), bmod_d, writes=[B_c])
    P.dma("sp", scT.rearrange("p k s -> p (k s)"), cT_d, writes=[B_c])
    cstF = gF[:, 0:640]
    P.dma("sp", cstF, cst_d, writes=[B_gF])
    P.emit("dve", lambda e: e.tensor_copy(out=identF, in_=cstF[:, 0:128]), reads=[B_gF], writes=[B_c])
    P.emit("dve", lambda e: e.tensor_copy(out=cstb.rearrange("p a c -> p (a c)"), in_=cstF[:, 128:640]), reads=[B_gF], writes=[B_c])
    P.emit("dve", lambda e: e.memset(onesb, 1.0), writes=[B_c])
    epsc = sb("epsc", [128, 1], F32)
    P.emit("dve", lambda e: e.memset(epsc, EPS), writes=[B_c])
    act_(scT.rearrange("p k s -> p (k s)"), scT.rearrange("p k s -> p (k s)"), AF.Silu, [B_c], [B_c])
    def mod_prologue():
      if True:
        P.dma("sp", lamv, vecs_d[:, NV:NV + NVL], writes=[B_tf[0]])
        wmF = [gF[:, 1024 * (1 + i):1024 * (2 + i)] for i in range(NWM)]
        for l in layers:
            pm, B_pm = nps()
            for j in range(48):
                s_ = (l * 48 + j) % NWM
                P.dma("sp" if j % 2 == 0 else "pool", wmF[s_], wmod_d[l, j], writes=[B_wm[s_]])
                mm(pm[:, j * 4:(j + 1) * 4], [(wmF[s_][:, kc * 128:(kc + 1) * 128], scT[:, kc, :]) for kc in range(KC)],
                   [B_wm[s_], B_c], B_pm)
            for s_ in range(4):
                tt(modt[:, l, :, s_], pm[:, 0:192].rearrange("p (j s) -> p j s", s=4)[:, :, s_], bmod[:, l, :], ALU.add,
                   [B_pm, B_c], [B_c])
            for s_ in range(3):
                for a, (msc, vg) in enumerate(((8, V_N1G), (32, V_N2G))):
                    stt(drv[:, l, s_, a, :], modt[:, l, msc:msc + 8, s_], 1.0, vecs[:, vg + l * 8:vg + l * 8 + 8], ALU.add, ALU.mult,
                        [B_c], [B_c])
            for a in range(2):
                t_, B_t = ntf()
                base = l * 256 + a * 128
                tt(t_[:, 0:64], lamv[:, base:base + 64], lamv[:, base + 64:base + 128], ALU.mult, [B_tf[0]], [B_t])
                P.emit("dve", lambda e, t_=t_, a=a, l=l: e.tensor_reduce(out=lamt[:, l, 2 + a:3 + a], in_=t_[:, 0:64], axis=AX.X, op=ALU.add),
                       reads=[B_t], writes=[B_c])
            act_(lamt[:, l, 2:4], lamt[:, l, 2:4], AF.Exp, [B_c], [B_c])
            li = lam_init_of(l)
            stt(lamt[:, l, 1:2], lamt[:, l, 3:4], -li, lamt[:, l, 2:3], ALU.add, ALU.subtract, [B_c], [B_c])
            P.emit("dve", lambda e, l=l, li=li: e.tensor_scalar(out=subs[:, l:l + 1], in0=vecs[:, V_SUB + l:V_SUB + l + 1], scalar1=1.0 - li,
                                                        scalar2=None, op0=ALU.mult), reads=[B_c], writes=[B_c])


    def mod_ap(l, m, kc, s_):
        return modt[:, l, m * 8 + kc, s_:s_ + 1]

    def norm_parts(l, s_, which, xoff, n, dst, B_dst, B_xs, sqbufs=None, bank=None):
        state = {}

        def get_bank():
            if "pss" not in state:
                state["pss"] = (ps[bank], B_ps[bank]) if bank is not None else nps()
            return state["pss"]

        def sq(k):
            if sqbufs is None:
                b, B_b = ntb()
            else:
                b, B_b = sqbufs[k % len(sqbufs)]
            state[k] = (b, B_b)
            act_(b[:, 0:n], xT[:, k, xoff:xoff + n], AF.Square, B_xs, [B_b])

        def mmk(k):
            pss, B_pss = get_bank()
            b, B_b = state[k]
            P.emit("pe", lambda e: e.matmul(pss[:, 0:n], lhsT=OnesD, rhs=b[:, 0:n], start=(k == 0), stop=(k == KC - 1)),
                   reads=[B_b, B_c], writes=[B_pss], inc=True)

        def fin():
            pss, B_pss = get_bank()
            rs, B_rs = rstd_from(pss, B_pss, n)
            msh = 0 if which == 0 else 3
            for kc in range(KC):
                t_, B_t = ntf()
                stt(t_[:, 0:n], xT[:, kc, xoff:xoff + n], drv[:, l, s_, which, kc:kc + 1], rs[:, 0:n], ALU.mult, ALU.mult,
                    B_xs + [B_rs, B_c], [B_t])
                act_(dst[:, kc, 0:n], t_[:, 0:n], AF.Identity, [B_t, B_c], [B_dst], bias=mod_ap(l, msh, kc, s_), scale=1.0)
        return sq, mmk, fin

    def norm_tile(l, s_, which, xoff, n, dst, B_dst, B_xs):
        sq, mmk, fin = norm_parts(l, s_, which, xoff, n, dst, B_dst, B_xs)
        for kc in range(KC):
            sq(kc)
            mmk(kc)
        fin()

    def proj_fm(w, B_w, src, B_src, n, ntiles=8):
        pz, B_pz = nps()
        mm(pz[:, 0:n], [(w[:, kc * 128:(kc + 1) * 128], src(kc)) for kc in range(ntiles)], [B_w] + B_src, B_pz)
        return pz, B_pz

    def qk_post(pz, B_pz, n, lat_off, gcol, normed, dst, B_dst):
        rs = None
        roped = lat_off is not None
        if normed:
            z2, B_z2 = ntb()
            act_(z2[:, 0:n], pz[:, 0:n], AF.Square, [B_pz], [B_z2])
        if roped:
            zb, B_zb = ntb()
            act_(zb[:, 0:n], pz[:, 0:n], AF.Copy, [B_pz], [B_zb])
        if normed:
            pss, B_pss = nps()
            mm(pss[:, 0:n], [(Blk, z2[:, 0:n])], [B_z2, B_c], B_pss)
        if roped:
            psw, B_psw = nps()
            mm(psw[:, 0:n], [(Pm, zb[:, 0:n])], [B_zb, B_c], B_psw)
        if normed:
            rs, B_rs = rstd_from(pss, B_pss, n)
        if not roped:
            if normed:
                stt(dst, pz[:, 0:n], V(gcol), rs[:, 0:n], ALU.mult, ALU.mult, [B_pz, B_rs, B_c], [B_dst])
            else:
                act_(dst, pz[:, 0:n], AF.Copy, [B_pz], [B_dst])
            return
        C = ropeT[:, 0, lat_off:lat_off + n]
        S = ropeT[:, 1, lat_off:lat_off + n]
        t1, B_t1 = ntf()
        t2, B_t2 = ntf()
        if normed:
            stt(t1[:, 0:n], pz[:, 0:n], V(gcol), C, ALU.mult, ALU.mult, [B_pz, B_rope, B_c], [B_t1])
            stt(t2[:, 0:n], psw[:, 0:n], V(gcol + 1), S, ALU.mult, ALU.mult, [B_psw, B_rope, B_c], [B_t2])
            tt(t1[:, 0:n], t1[:, 0:n], t2[:, 0:n], ALU.add, [B_t1, B_t2], [B_t1])
            tt(dst, t1[:, 0:n], rs[:, 0:n], ALU.mult, [B_t1, B_rs], [B_dst])
        else:
            tt(t1[:, 0:n], pz[:, 0:n], C, ALU.mult, [B_pz, B_rope], [B_t1])
            tt(t2[:, 0:n], psw[:, 0:n], S, ALU.mult, [B_psw, B_rope], [B_t2])
            tt(dst, t1[:, 0:n], t2[:, 0:n], ALU.add, [B_t1, B_t2], [B_dst])

    def run_items(items):
        pend = None
        for (A, Bf) in items:
            r = A()
            if pend is not None:
                pend[0](*pend[1])
            pend = (Bf, r)
        if pend is not None:
            pend[0](*pend[1])

    def load_rope(extra_writes, lo=0, nn=SEQ):
        P.dma("sp", ropeT[:, :, lo:lo + nn], rope_d.rearrange("p (a t) -> p a t", a=2)[:, :, lo:lo + nn],
              writes=[B_rope] + all_mg() + extra_writes)

    TILES = [(0, 256), (256, 512), (768, 512), (1280, 512), (1792, 512)]

    def kv_phase(l, q, s_lat):
        for i in range(5):
            P.dma("pool", kw[:, i, :], wfm_d[l, CH_AK + i], writes=[B_kw[i]] + (all_qy() + B_yc + [B_gF] + B_wm + B_wsx if i == 0 else []))
        P.dma("pool", vw, wv_d[l].rearrange("p (k c) -> p k c", k=8), writes=[B_vw])
        load_rope([])
        P.emit("dve", lambda e: e.memset(bv[:, :, 64:128], 1.0), writes=[B_bv])
        st["rsv7"] = True

        def kv_norm(ti):
            off, n = TILES[ti]
            norm_tile(l, 2 if ti == 0 else s_lat, 0, off, n, hs[ti % 2], B_hs[ti % 2], [B_x[ti]])
        kv_norm(0)
        for ti, (off, n) in enumerate(TILES):
            h, B_h = hs[ti % 2], B_hs[ti % 2]
            lat_off = None if ti == 0 else off - CTX
            items = []
            for c in range(5):
                def A(c=c, h=h, B_h=B_h, n=n):
                    return proj_fm(kw[:, c, :], B_kw[c], lambda kc: h[:, kc, 0:n], [B_h], n)

                def Bf(pz, B_pz, c=c, n=n, off=off, lat_off=lat_off):
                    if c < 4:
                        qk_post(pz, B_pz, n, lat_off, 0, False, ak[:, c, off:off + n], B_ak)
                    else:
                        qk_post(pz, B_pz, n, lat_off, V_KG + l * 2, True, bk[:, off:off + n], B_bk)
                items.append((A, Bf))
            for sub in range(n // 128):
                kci = off // 128 + sub

                def A(sub=sub, h=h, B_h=B_h):
                    pv, B_pv = nps()
                    mm(pv, [(h[:, kc, sub * 128:(sub + 1) * 128], vw[:, kc, 0:512]) for kc in range(KC)], [B_h, B_vw], B_pv)
                    pv2, B_pv2 = nps()
                    mm(pv2[:, 0:128], [(h[:, kc, sub * 128:(sub + 1) * 128], vw[:, kc, 512:640]) for kc in range(KC)], [B_h, B_vw], B_pv2)
                    return (pv, pv2), (B_pv, B_pv2)

                def Bf(pvs, Bs, kci=kci):
                    pv, pv2 = pvs
                    B_pv, B_pv2 = Bs
                    act_(av[:, kci, :], pv, AF.Copy, [B_pv], [B_av])
                    P.emit("dve", lambda e: e.tensor_copy(out=bv[:, kci, 0:64], in_=pv2[:, 0:64]), reads=[B_pv2], writes=[B_bv])
                    P.emit("dve", lambda e: e.tensor_copy(out=bv[:, kci, 128:192], in_=pv2[:, 64:128]), reads=[B_pv2], writes=[B_bv])
                items.append((A, Bf))
            if ti + 1 < len(TILES):
                off2, n2 = TILES[ti + 1]
                nsq, nmm, nfin = norm_parts(l, s_lat, 0, off2, n2, hs[(ti + 1) % 2], B_hs[(ti + 1) % 2], [B_x[ti + 1]],
                                            sqbufs=[(sqd[:, i, :], B_sqd[i]) for i in range(4)], bank=7)
                wrapped = []
                for j, (A, Bf) in enumerate(items):
                    def A2(A=A, j=j):
                        if 1 <= j <= 4:
                            nmm(2 * j - 2)
                            nmm(2 * j - 1)
                        if j <= 3:
                            nsq(2 * j)
                            nsq(2 * j + 1)
                        if j == 5:
                            nfin()
                        return A()
                    wrapped.append((A2, Bf))
                items = wrapped
            run_items(items)
        st["rsv7"] = False
        P.barrier()
        dump("xT", xT.rearrange("p c t -> p (c t)"), B_x)
        dump("kv", kvreg, [B_ak, B_bk, B_av, B_bv])

    def attention(l, main, nkc):
        for ti, (qo, n) in enumerate(main):
            for h in range(8):
                diff = h < 4
                SA = [(ps[0], B_ps[0]), (ps[2], B_ps[2])]
                SB = [(ps[1], B_ps[1]), (ps[3], B_ps[3])]
                kT = (lambda kc, h=h: ak[:, h, kc * 128:(kc + 1) * 128]) if diff else (lambda kc: bk[:, kc * 128:(kc + 1) * 128])
                B_k = B_ak if diff else B_bk
                qT = qy[:, h, qo:qo + n]
                B_q = B_qy[h][ti]
                gA, gB = (4, 5) if h % 2 == 0 else (6, 7)

                def smm(kc):
                    a, B_a = SA[kc % 2]
                    b, B_b = SB[kc % 2]
                    kt = kT(kc)
                    mm(a[:, 0:n], [(kt[0:64, :], qT[0:64, :])], [B_k, B_q], B_a)
                    mm(b[:, 0:n], [(kt[64:128, :], qT[64:128, :])], [B_k, B_q], B_b)

                smm(0)
                for kc in range(nkc):
                    if kc + 1 < nkc:
                        smm(kc + 1)
                    a, B_a = SA[kc % 2]
                    b, B_b = SB[kc % 2]
                    pi = 2 * (st["tbp"] % 3)
                    st["tbp"] += 1
                    p1, B_p1, p2, B_p2 = tb[:, pi, :], B_tb[pi], tb[:, pi + 1, :], B_tb[pi + 1]
                    act_(p1[:, 0:n], a[:, 0:n], AF.Exp, [B_a], [B_p1], scale=0.125)
                    act_(p2[:, 0:n], b[:, 0:n], AF.Exp, [B_b], [B_p2], scale=0.125)
                    f, la = (kc == 0), (kc == nkc - 1)

                    def acc(bank, lhsT, rhs, rd, f=f, la=la):
                        P.emit("pe", lambda e: e.matmul(ps[bank][:, 0:n], lhsT=lhsT, rhs=rhs, start=f, stop=la),
                               reads=rd, writes=[B_ps[bank]], inc=la)
                    if diff:
                        vt = av[:, kc, h * 128:(h + 1) * 128]
                        acc(4, vt, p1[:, 0:n], [B_av, B_p1])
                        acc(5, onesb, p1[:, 0:n], [B_c, B_p1])
                        acc(6, vt, p2[:, 0:n], [B_av, B_p2])
                        acc(7, onesb, p2[:, 0:n], [B_c, B_p2])
                    else:
                        acc(gA, bv[:, kc, 0:128], p1[:, 0:n], [B_bv, B_p1])
                        acc(gB, bv[:, kc, 64:192], p2[:, 0:n], [B_bv, B_p2])
                if diff:
                    sa, B_sa = ntf()
                    sb2, B_sb2 = ntf()
                    t1, B_t1 = ntf()
                    t2, B_t2 = ntf()
                    P.emit("dve", lambda e, sa=sa: e.tensor_copy(out=sa[:, 0:n], in_=ps[5][:, 0:n]), reads=[B_ps[5]], writes=[B_sa])
                    P.emit("dve", lambda e, sb2=sb2: e.tensor_copy(out=sb2[:, 0:n], in_=ps[7][:, 0:n]), reads=[B_ps[7]], writes=[B_sb2])
                    tt(t1[:, 0:n], ps[4][:, 0:n], sb2[:, 0:n], ALU.mult, [B_ps[4], B_sb2], [B_t1])
                    tt(t2[:, 0:n], ps[6][:, 0:n], sa[:, 0:n], ALU.mult, [B_ps[6], B_sa], [B_t2])
                    stt(t1[:, 0:n], t2[:, 0:n], lamt[:, l, 1:2], t1[:, 0:n], ALU.mult, ALU.add, [B_t1, B_t2, B_c], [B_t1])
                    tt(sa[:, 0:n], sa[:, 0:n], sb2[:, 0:n], ALU.mult, [B_sa, B_sb2], [B_sa])
                    pi = 2 * (st["tbp"] % 3)
                    st["tbp"] += 1
                    d2, B_d2 = tb[:, pi, :], B_tb[pi]
                    tt(d2[:, 0:n], t1[:, 0:n], t1[:, 0:n], ALU.mult, [B_t1], [B_d2])
                    mm(ps[5][:, 0:n], [(OnesV, d2[:, 0:n])], [B_d2, B_c], B_ps[5])
                    stt(sb2[:, 0:n], sa[:, 0:n], EPS, sa[:, 0:n], ALU.mult, ALU.mult, [B_sa], [B_sb2])
                    tt(sb2[:, 0:n], ps[5][:, 0:n], sb2[:, 0:n], ALU.add, [B_ps[5], B_sb2], [B_sb2])
                    act_(sb2[:, 0:n], sb2[:, 0:n], AF.Ln, [B_sb2], [B_sb2])
                    act_(sb2[:, 0:n], sb2[:, 0:n], AF.Exp, [B_sb2], [B_sb2], scale=-0.5)
                    stt(qT, t1[:, 0:n], subs[:, l:l + 1], sb2[:, 0:n], ALU.mult, ALU.mult, [B_t1, B_sb2, B_c], [B_q])
                else:
                    r1, B_r1 = ntf()
                    recip(r1[0:64, 0:n], ps[gA][64:128, 0:n], [B_ps[gA]], [B_r1])
                    recip(r1[64:128, 0:n], ps[gB][0:64, 0:n], [B_ps[gB]], [B_r1])
                    tt(qT[0:64, :], ps[gA][0:64, 0:n], r1[0:64, 0:n], ALU.mult, [B_ps[gA], B_r1], [B_q])
                    tt(qT[64:128, :], ps[gB][64:128, 0:n], r1[64:128, 0:n], ALU.mult, [B_ps[gB], B_r1], [B_q])

    def group_norm(l, s_, pieces):
        for (off, n, hc) in pieces:
            xt = [B_x[i] for i, (o2, n2) in enumerate(TILES) if o2 < off + n and off < o2 + n2]
            norm_tile(l, s_, 0, off, n, hg[:, :, hc:hc + n], B_hg, xt)

    def group_phase(l, s_, pieces, main, halo_side, lat0, nkc):
        is_lat = lat0 is not None
        if is_lat:
            load_rope([], lat0, 1024)
        items = []
        for c in range(8):
            for ti, (hc, n, xi) in enumerate(main):
                def A(c=c, ti=ti, hc=hc, n=n):
                    if ti == 0:
                        qw[c] = wload(l, CH_AQ + c)
                    w, B_w = qw[c]
                    return proj_fm(w, B_w, lambda kc: hg[:, kc, hc:hc + n], [B_hg], n)

                def Bf(pz, B_pz, c=c, ti=ti, hc=hc, n=n):
                    lo = (lat0 + hc) if is_lat else None
                    qk_post(pz, B_pz, n, lo, V_QG + l * 2, c >= 4, qy[:, c, hc:hc + n], B_qy[c][ti])
                items.append((A, Bf))
        qw = {}
        run_items(items)
        dump("q_%s" % halo_side, gscr[:, 8208:16400], all_qy())
        attention(l, [(hc, n) for (hc, n, xi) in main], nkc)
        dump("y_%s" % halo_side, gscr[:, 8208:16400], all_qy())
        st["ps"] = 0
        ntot = sum(n for (hc, n, xi) in main)
        for j in range(4):
            wc_, B_wc = wload(l, CH_CC + j)
            wu_, B_wu = wload(l, CH_CU + j)
            pext = tf[:, 0:3, :].rearrange("p a t -> p (a t)")
            B_pe = B_tf[0:3]
            for (off, n, hc) in pieces:
                pc, B_pc = proj_fm(wc_, B_wc, lambda kc: hg[:, kc, hc:hc + n], [B_hg], n)
                pu, B_pu = proj_fm(wu_, B_wu, lambda kc: hg[:, kc, hc:hc + n], [B_hg], n)
                if hc >= ntot:
                    dcol = 0 if halo_side == "left" else ntot + 1
                else:
                    dcol = 1 + hc
                cc_, B_cc = tf[:, 3, :], B_tf[3]
                act_(cc_[:, 0:n], pc[:, 0:n], AF.Copy, [B_pc], [B_cc])
                tt(pext[:, dcol:dcol + n], cc_[:, 0:n], pu[:, 0:n], ALU.mult, [B_cc, B_pu], B_pe)
            zc = []
            if halo_side != "left":
                zc.append(0)
            if halo_side != "right":
                zc.append(ntot + 1)
            for z in zc:
                P.emit("dve", lambda e, z=z: e.memset(pext[:, z:z + 1], 0.0), writes=B_pe)
            wb_, B_wb = wload(l, CH_CB + j)
            for ti, (hc, n, xi) in enumerate(main):
                pb, B_pb = proj_fm(wb_, B_wb, lambda kc: hg[:, kc, hc:hc + n], [B_hg], n)
                q_, B_q_ = tf[:, 4 + ti % 2, :], B_tf[4 + ti % 2]
                cw = V_CONV + l * 12 + j
                act_(q_[:, 0:n], pext[:, 1 + hc:1 + hc + n], AF.Identity, B_pe + [B_c], [B_q_], scale=V(cw + 4))
                stt(q_[:, 0:n], pext[:, hc:hc + n], V(cw), q_[:, 0:n], ALU.mult, ALU.add, B_pe + [B_c, B_q_], [B_q_])
                stt(q_[:, 0:n], pext[:, 2 + hc:2 + hc + n], V(cw + 8), q_[:, 0:n], ALU.mult, ALU.add, B_pe + [B_c, B_q_], [B_q_])
                tt(yc[:, j, hc:hc + n], pb[:, 0:n], q_[:, 0:n], ALU.mult, [B_pb, B_q_], [B_yc[j]])
        dump("yc_%s" % halo_side, gscr[:, 16400:20496], B_yc)
        st["tf"] = 0
        for m in range(8):
            wg = [wload(l, CH_G + 3 * m + b) for b in range(3)]
            wab, B_wab = wload(l, CH_WAB + m)
            for ti, (hc, n, xi) in enumerate(main):
                sg = []
                for b in range(3):
                    pg, B_pg = proj_fm(wg[b][0], wg[b][1], lambda kc: hg[:, kc, hc:hc + n], [B_hg], n)
                    s__, B_s = tf[:, b, :], B_tf[b]
                    act_(s__[:, 0:n], pg[:, 0:n], AF.Sigmoid, [B_pg], [B_s])
                    sg.append((s__, B_s))
                if ti == 0:
                    wcm, B_wcm = wload(l, CH_WC + m, 512)
                pa, B_pa = nps()
                mm(pa[:, 0:n], [(wab[:, kc * 128:(kc + 1) * 128], qy[:, kc, hc:hc + n]) for kc in range(4)],
                   [B_wab] + [B_qy[kc][ti] for kc in range(4)], B_pa)
                pb2, B_pb2 = nps()
                mm(pb2[:, 0:n], [(wab[:, (4 + kc) * 128:(5 + kc) * 128], qy[:, 4 + kc, hc:hc + n]) for kc in range(4)],
                   [B_wab] + [B_qy[4 + kc][ti] for kc in range(4)], B_pb2)
                pc2, B_pc2 = nps()
                mm(pc2[:, 0:n], [(wcm[:, kc * 128:(kc + 1) * 128], yc[:, kc, hc:hc + n]) for kc in range(4)],
                   [B_wcm] + B_yc, B_pc2)
                t1, B_t1 = tf[:, 3, :], B_tf[3]
                t2, B_t2 = tf[:, 4, :], B_tf[4]
                tt(t1[:, 0:n], pa[:, 0:n], sg[0][0][:, 0:n], ALU.mult, [B_pa, sg[0][1]], [B_t1])
                tt(t2[:, 0:n], pb2[:, 0:n], sg[1][0][:, 0:n], ALU.mult, [B_pb2, sg[1][1]], [B_t2])
                tt(t1[:, 0:n], t1[:, 0:n], t2[:, 0:n], ALU.add, [B_t1, B_t2], [B_t1])
                tt(t2[:, 0:n], pc2[:, 0:n], sg[2][0][:, 0:n], ALU.mult, [B_pc2, sg[2][1]], [B_t2])
                tt(merged[:, m, hc:hc + n], t1[:, 0:n], t2[:, 0:n], ALU.add, [B_t1, B_t2], [B_mg[m][ti], B_rope])
        dump("mg_%s" % halo_side, mreg, all_mg())

    def spread(items, norms):
        sched = {}
        pos = 0
        for (nsq, nmm, nfin) in norms:
            for j in range(4):
                sched.setdefault(pos + j, []).append((nsq, 2 * j))
                sched.setdefault(pos + j, []).append((nsq, 2 * j + 1))
                sched.setdefault(pos + j + 1, []).append((nmm, 2 * j))
                sched.setdefault(pos + j + 1, []).append((nmm, 2 * j + 1))
            sched.setdefault(pos + 5, []).append((nfin, None))
            pos += 6
        assert pos <= len(items), (pos, len(items))
        out = []
        for j, (A, Bf) in enumerate(items):
            def A2(A=A, j=j):
                for f, k in sorted(sched.get(j, []), key=lambda fk: 0 if fk[0].__name__ == "mmk" else 1):
                    f(k) if k is not None else f()
                return A()
            out.append((A2, Bf))
        return out

    def tile_norms(l, s_, which, main):
        return [norm_parts(l, s_, which, TILES[xi][0], n, hg[:, :, hc:hc + n], B_hg, [B_x[xi]],
                           sqbufs=[(sqd[:, i, :], B_sqd[i]) for i in range(4)], bank=7) for (hc, n, xi) in main]

    def group_wout(l, s_, main, norms=()):
        items = []
        wq = {}
        for nn in range(8):
            for ti, (hc, n, xi) in enumerate(main):
                def A(nn=nn, ti=ti, hc=hc, n=n):
                    if ti == 0:
                        wq[nn] = wload(l, CH_WOUT + nn)
                    w, B_w = wq[nn]
                    po, B_po = nps()
                    mm(po[:, 0:n], [(w[:, m * 128:(m + 1) * 128], merged[:, m, hc:hc + n]) for m in range(8)],
                       [B_w] + [B_mg[m][ti] for m in range(8)], B_po)
                    return po, B_po

                def Bf(po, B_po, nn=nn, n=n, xi=xi):
                    xo = TILES[xi][0]
                    stt(xT[:, nn, xo:xo + n], po[:, 0:n], mod_ap(l, 2, nn, s_), xT[:, nn, xo:xo + n], ALU.mult, ALU.add,
                        [B_po, B_c, B_x[xi]], [B_x[xi]])
                items.append((A, Bf))
        st["rsv7"] = True
        run_items(spread(items, norms) if norms and len(items) >= 6 * len(norms) else items)
        st["rsv7"] = False

    def group_wout_old(l, s_, main):
        for nn in range(8):
            w, B_w = wload(l, CH_WOUT + nn)
            for ti, (hc, n, xi) in enumerate(main):
                po, B_po = nps()
                mm(po[:, 0:n], [(w[:, m * 128:(m + 1) * 128], merged[:, m, hc:hc + n]) for m in range(8)],
                   [B_w] + [B_mg[m][ti] for m in range(8)], B_po)
                xo = TILES[xi][0]
                stt(xT[:, nn, xo:xo + n], po[:, 0:n], mod_ap(l, 2, nn, s_), xT[:, nn, xo:xo + n], ALU.mult, ALU.add,
                    [B_po, B_c, B_x[xi]], [B_x[xi]])

    def ffn_norm(l, s_, main):
        for ti, (hc, n, xi) in enumerate(main):
            norm_tile(l, s_, 1, TILES[xi][0], n, hg[:, :, hc:hc + n], B_hg, [B_x[xi]])

    def ffn_gu(l, s_, main):
        for j in range(NJ):
            wg_, B_wg = wload(l, CH_GU + 2 * j, ffn=True)
            wu_, B_wu = wload(l, CH_GU + 2 * j + 1, ffn=True)
            for ti, (hc, n, xi) in enumerate(main):
                pg, B_pg = proj_fm(wg_, B_wg, lambda kc: hg[:, kc, hc:hc + n], [B_hg], n)
                pu, B_pu = proj_fm(wu_, B_wu, lambda kc: hg[:, kc, hc:hc + n], [B_hg], n)
                sl, B_sl = ntf()
                act_(sl[:, 0:n], pg[:, 0:n], AF.Silu, [B_pg], [B_sl])
                tt(act[:, j, hc:hc + n], sl[:, 0:n], pu[:, 0:n], ALU.mult, [B_sl, B_pu], [B_act[j][ti]])

    def ffn_down(l, s_, main, norms=()):
        items = []
        wq = {}
        for nn in range(8):
            for ti, (hc, n, xi) in enumerate(main):
                def A(nn=nn, ti=ti, hc=hc, n=n):
                    if ti == 0:
                        wq[nn] = [wload(l, CH_WD + 3 * nn + i, 1024 if i < 2 else 768, ffn=True) for i in range(3)]
                    ws_ = wq[nn]
                    po, B_po = nps()
                    mm(po[:, 0:n], [(ws_[j // 8][0][:, (j % 8) * 128:(j % 8 + 1) * 128], act[:, j, hc:hc + n]) for j in range(NJ)],
                       [w_[1] for w_ in ws_] + [B_act[j][ti] for j in range(NJ)], B_po)
                    return po, B_po

                def Bf(po, B_po, nn=nn, n=n, xi=xi):
                    xo = TILES[xi][0]
                    stt(xT[:, nn, xo:xo + n], po[:, 0:n], mod_ap(l, 5, nn, s_), xT[:, nn, xo:xo + n], ALU.mult, ALU.add,
                        [B_po, B_c, B_x[xi]], [B_x[xi]])
                items.append((A, Bf))
        st["rsv7"] = True
        run_items(spread(items, norms) if norms and len(items) >= 6 * len(norms) else items)
        st["rsv7"] = False

    def ffn_down_old(l, s_, main):
        for nn in range(8):
            ws_ = [wload(l, CH_WD + 3 * nn + i, 1024 if i < 2 else 768) for i in range(3)]
            for ti, (hc, n, xi) in enumerate(main):
                po, B_po = nps()
                mm(po[:, 0:n], [(ws_[j // 8][0][:, (j % 8) * 128:(j % 8 + 1) * 128], act[:, j, hc:hc + n]) for j in range(NJ)],
                   [w_[1] for w_ in ws_] + [B_act[j][ti] for j in range(NJ)], B_po)
                xo = TILES[xi][0]
                stt(xT[:, nn, xo:xo + n], po[:, 0:n], mod_ap(l, 5, nn, s_), xT[:, nn, xo:xo + n], ALU.mult, ALU.add,
                    [B_po, B_c, B_x[xi]], [B_x[xi]])

    def load_x(q):
        for i in range(TOK // 128):
            src = ctx_d[q, i * 128:(i + 1) * 128, :] if i < 2 else x_d[q, (i - 2) * 128:(i - 1) * 128, :]
            s_ = i % 2
            stg = tf[:, 2 * s_:2 * s_ + 2, :].rearrange("p a t -> p (a t)")
            B_s = B_tf[2 * s_:2 * s_ + 2]
            P.dma("sp" if s_ == 0 else "pool", stg, src, writes=B_s)
            xi = [j for j, (o, n) in enumerate(TILES) if o <= i * 128 < o + n][0]
            for half in range(2):
                pt, B_pt = nps()
                for k4 in range(4):
                    kc = half * 4 + k4
                    P.emit("pe", lambda e, pt=pt, k4=k4, kc=kc, stg=stg: e.transpose(out=pt[:, k4 * 128:(k4 + 1) * 128], in_=stg[:, kc * 128:(kc + 1) * 128],
                                                                                    identity=identF), reads=B_s + [B_c], writes=[B_pt], inc=(k4 == 3))
                dst = xT[:, half * 4:half * 4 + 4, i * 128:(i + 1) * 128]
                srcp = pt.rearrange("p (a t) -> p a t", a=4)
                if half == 0:
                    P.emit("act", lambda e, dst=dst, srcp=srcp: e.activation(out=dst, in_=srcp, func=AF.Copy), reads=[B_pt], writes=[B_x[xi]])
                else:
                    P.emit("dve", lambda e, dst=dst, srcp=srcp: e.tensor_copy(out=dst, in_=srcp), reads=[B_pt], writes=[B_x[xi]])

    def store_tokens(q, tok0, ntok, dst_d, dst_row0, norm):
        yT = gF.rearrange("p (c t) -> p c t", c=8)
        for t0 in range(0, ntok, 512):
            n = min(512, ntok - t0)
            xo = tok0 + t0
            xi = [j for j, (o, nn) in enumerate(TILES) if o <= xo < o + nn][0]
            if norm:
                pss, B_pss = nps()
                for kc in range(KC):
                    sq, B_sq = ntb()
                    act_(sq[:, 0:n], xT[:, kc, xo:xo + n], AF.Square, [B_x[xi]], [B_sq])
                    P.emit("pe", lambda e, sq=sq, kc=kc, pss=pss, n=n: e.matmul(pss[:, 0:n], lhsT=OnesD, rhs=sq[:, 0:n], start=(kc == 0), stop=(kc == KC - 1)),
                           reads=[B_sq, B_c], writes=[B_pss], inc=True)
                rs, B_rs = rstd_from(pss, B_pss, n)
                for kc in range(KC):
                    stt(yT[:, kc, 0:n], xT[:, kc, xo:xo + n], V(V_FG + kc), rs[:, 0:n], ALU.mult, ALU.mult, [B_x[xi], B_rs, B_c], [B_gF] + B_wsx)
                srcT, B_src = (lambda kc, a: yT[:, kc, a * 128:(a + 1) * 128]), [B_gF]
            else:
                srcT, B_src = (lambda kc, a, xo=xo: xT[:, kc, xo + a * 128:xo + (a + 1) * 128]), [B_x[xi]]
            for a in range(n // 128):
                s_ = a % 2
                stg = tf[:, 2 * s_:2 * s_ + 2, :].rearrange("p a t -> p (a t)")
                B_s = B_tf[2 * s_:2 * s_ + 2]
                for half in range(2):
                    pt, B_pt = nps()
                    for k4 in range(4):
                        kc = half * 4 + k4
                        P.emit("pe", lambda e, pt=pt, k4=k4, kc=kc, a=a, srcT=srcT: e.transpose(
                            out=pt[:, k4 * 128:(k4 + 1) * 128], in_=srcT(kc, a), identity=identF),
                            reads=B_src + [B_c], writes=[B_pt], inc=(k4 == 3))
                    if half == 0:
                        P.emit("act", lambda e, pt=pt, stg=stg: e.activation(out=stg[:, 0:512], in_=pt, func=AF.Copy), reads=[B_pt], writes=B_s)
                    else:
                        P.emit("dve", lambda e, pt=pt, stg=stg: e.tensor_copy(out=stg[:, 512:1024], in_=pt), reads=[B_pt], writes=B_s)
                r0 = dst_row0 + t0 + a * 128
                P.dma("sp" if s_ == 0 else "pool", dst_d[q, r0:r0 + 128, :], stg, reads=B_s)

    for q in range(nseq):
        load_x(q)
        if q == 0:
            mod_prologue()
            P.barrier()
        for li, l in enumerate(layers):
            last = (l == NL - 1)
            kv_phase(l, q, q)
            groups = []
            if not last:
                groups.append((2, [(0, 256, 0)], [(0, 256, 0)], "none", None, 2))
            groups.append((q, [(256, 512, 0), (768, 512, 512), (1280, 1, 1024)], [(0, 512, 1), (512, 512, 2)], "right", 0, 18))
            groups.append((q, [(1280, 512, 0), (1792, 512, 512), (1279, 1, 1024)], [(0, 512, 3), (512, 512, 4)], "left", 1024, 18))
            ng = len(groups)
            group_norm(l, groups[0][0], groups[0][1])
            for gi, (s_, pieces, main, hside, lat0, nkc) in enumerate(groups):
                group_phase(l, s_, pieces, main, hside, lat0, nkc)
                if gi + 1 < ng:
                    ns_, npieces, nmain = groups[gi + 1][0], groups[gi + 1][1], groups[gi + 1][2]
                    halo = [p_ for p_ in npieces if p_[1] == 1]
                    if halo:
                        group_norm(l, ns_, halo)
                    if len(main) * 8 >= 6 * len(nmain):
                        group_wout(l, s_, main, tile_norms(l, ns_, 0, nmain))
                    else:
                        group_norm(l, ns_, [p_ for p_ in npieces if p_[1] != 1])
                        group_wout(l, s_, main)
                else:
                    group_wout(l, s_, main, tile_norms(l, groups[0][0], 1, groups[0][2]))
            P.barrier()
            dump("xT_mid", xT.rearrange("p c t -> p (c t)"), B_x)
            st["ffn_fence"] = True
            for gi, (s_, pieces, main, hside, lat0, nkc) in enumerate(groups):
                ffn_gu(l, s_, main)
                if gi + 1 < ng and len(main) * 8 >= 6 * len(groups[gi + 1][2]):
                    ffn_down(l, s_, main, tile_norms(l, groups[gi + 1][0], 1, groups[gi + 1][2]))
                else:
                    if gi + 1 < ng:
                        ffn_norm(l, groups[gi + 1][0], groups[gi + 1][2])
                    ffn_down(l, s_, main)
            P.barrier()
        store_tokens(q, CTX, SEQ, out_d, 0, final_norm)
        if out_ctx:
            store_tokens(q, 0, CTX, octx_d, 0, False)
        P.barrier()
    P.finish("sp")
    P.build()
    return nc


def _fm_chunk(w, rows, cols):
    raise NotImplementedError


def _prep_weights(inp):
    w_in = np.asarray(inp["w_in"], np.float32)
    wfm = np.zeros((NL, NCHUNK, 128, 8, 128), np.float32)
    wv = np.zeros((NL, 128, 8, 640), np.float32)

    def fm(wcols):
        return wcols.reshape(8, 128, 128).transpose(1, 0, 2)

    oq, okk, ov, obq, obk, obv, ocb, occ, ocu, og = 0, 512, 1024, 1536, 2048, 2176, 2304, 2816, 3328, 3840
    for l in range(NL):
        W = w_in[l]
        for c in range(4):
            wfm[l, CH_AK + c] = fm(W[:, okk + c * 128: okk + (c + 1) * 128])
            wfm[l, CH_AQ + c] = fm(W[:, oq + c * 128: oq + (c + 1) * 128])
            cols = np.concatenate([np.arange(obq + c * 64, obq + (c + 1) * 64), np.arange(obq + (c + 4) * 64, obq + (c + 5) * 64)])
            wfm[l, CH_BQ + c] = fm(W[:, cols])
            wfm[l, CH_CC + c] = fm(W[:, occ + c * 128: occ + (c + 1) * 128])
            wfm[l, CH_CU + c] = fm(W[:, ocu + c * 128: ocu + (c + 1) * 128])
            wfm[l, CH_CB + c] = fm(W[:, ocb + c * 128: ocb + (c + 1) * 128])
        wfm[l, CH_BK] = fm(W[:, obk: obk + 128])
        for m in range(8):
            for b in range(3):
                wfm[l, CH_G + 3 * m + b] = fm(W[:, og + b * 1024 + m * 128: og + b * 1024 + (m + 1) * 128])
        wv[l, :, :, 0:512] = W[:, ov:ov + 512].reshape(8, 128, 512).transpose(1, 0, 2)
        wv[l, :, :, 512:640] = W[:, obv:obv + 128].reshape(8, 128, 128).transpose(1, 0, 2)
        wa = np.asarray(inp["w_branch_a"], np.float32)[l]
        wb = np.asarray(inp["w_branch_b"], np.float32)[l]
        wc = np.asarray(inp["w_branch_c"], np.float32)[l]
        perm = np.concatenate([np.concatenate([np.arange(c * 64, (c + 1) * 64), np.arange((c + 4) * 64, (c + 5) * 64)]) for c in range(4)])
        wbp = wb[perm]
        for m in range(8):
            wfm[l, CH_WAB + m, :, 0:4, :] = wa[:, m * 128:(m + 1) * 128].reshape(4, 128, 128).transpose(1, 0, 2)
            wfm[l, CH_WAB + m, :, 4:8, :] = wbp[:, m * 128:(m + 1) * 128].reshape(4, 128, 128).transpose(1, 0, 2)
            wfm[l, CH_WC + m, :, 0:4, :] = wc[:, m * 128:(m + 1) * 128].reshape(4, 128, 128).transpose(1, 0, 2)
            wfm[l, CH_WOUT + m] = fm(np.asarray(inp["w_out"], np.float32)[l][:, m * 128:(m + 1) * 128])
        gu = np.asarray(inp["w_ffn_gu"], np.float32)[l]
        for j in range(NJ):
            wfm[l, CH_GU + 2 * j] = fm(gu[:, j * 128:(j + 1) * 128])
            wfm[l, CH_GU + 2 * j + 1] = fm(gu[:, DFF + j * 128: DFF + (j + 1) * 128])
        wd = np.asarray(inp["w_ffn_down"], np.float32)[l]
        for nn in range(8):
            blk = wd[:, nn * 128:(nn + 1) * 128].reshape(NJ, 128, 128).transpose(1, 0, 2)
            for i in range(3):
                nt = 8 if i < 2 else 6
                wfm[l, CH_WD + 3 * nn + i, :, 0:nt, :] = blk[:, i * 8:i * 8 + nt, :]
    wmod = np.asarray(inp["w_mod"], np.float32).reshape(NL, 8, 128, 48, 128).transpose(0, 3, 2, 1, 4)
    bmod = np.asarray(inp["b_mod"], np.float32).reshape(NL, 48, 128).transpose(2, 0, 1)
    vecs = np.zeros((128, NV + NVL), np.float32)
    for l in range(NL):
        vecs[:, V_N1G + l * 8:V_N1G + l * 8 + 8] = np.asarray(inp["norm1_g"], np.float32)[l].reshape(8, 128).T
        vecs[:, V_N2G + l * 8:V_N2G + l * 8 + 8] = np.asarray(inp["norm2_g"], np.float32)[l].reshape(8, 128).T
        cw = np.asarray(inp["conv_w"], np.float32)[l]
        for k in range(3):
            vecs[:, V_CONV + l * 12 + k * 4: V_CONV + l * 12 + k * 4 + 4] = cw[k].reshape(4, 128).T
        vecs[:, V_SUB + l] = np.asarray(inp["diff_subln_g"], np.float32)[l]
        pidx = np.arange(128) % 64
        for col, name in ((V_QG, "q_norm_g"), (V_KG, "k_norm_g")):
            g = np.asarray(inp[name], np.float32)[l]
            vecs[:, col + l * 2] = g[pidx]
            vecs[:, col + l * 2 + 1] = g[pidx ^ 16]
        for a, (n1, n2) in enumerate((("lam_q1", "lam_k1"), ("lam_q2", "lam_k2"))):
            base = V_LAM + l * 256 + a * 128
            vecs[:, base:base + 64] = np.asarray(inp[n1], np.float32)[l][None, :]
            vecs[:, base + 64:base + 128] = np.asarray(inp[n2], np.float32)[l][None, :]
    vecs[:, V_FG:V_FG + 8] = np.asarray(inp["final_g"], np.float32).reshape(8, 128).T
    rows = SEQ // GRID_W
    row = np.repeat(np.arange(rows, dtype=np.float32), GRID_W)
    col = np.tile(np.arange(GRID_W, dtype=np.float32), rows)
    nf = 16
    inv_freq = (np.float32(10000.0) ** (-np.arange(nf, dtype=np.float32) / np.float32(nf))).astype(np.float32)
    rope = np.zeros((128, 2, SEQ), np.float32)
    for p in range(128):
        d = p % 64
        axis, half, f = d // 32, (d // 16) % 2, d % 16
        ang = ((row if axis == 0 else col) * inv_freq[f]).astype(np.float32)
        rope[p, 0] = np.cos(ang)
        rope[p, 1] = np.sin(ang) * (-1.0 if half == 0 else 1.0)
    cst = np.zeros((128, 5, 128), np.float32)
    cst[:, 0] = np.eye(128, dtype=np.float32)
    for p in range(128):
        cst[p ^ 16, 1, p] = 1.0
    cst[0:64, 2, 0:64] = 1.0 / 64
    cst[64:128, 2, 64:128] = 1.0 / 64
    cst[:, 3] = 1.0 / 1024
    cst[:, 4] = 1.0 / 128
    return dict(wfm=np.ascontiguousarray(wfm.reshape(NL, NCHUNK, 128, 1024)),
                wv=np.ascontiguousarray(wv.reshape(NL, 128, 5120)),
                wmod=np.ascontiguousarray(wmod.reshape(NL, 48, 128, 1024)),
                bmod=np.ascontiguousarray(bmod.reshape(128, NL * 48)),
                vecs=vecs, rope=np.ascontiguousarray(rope.reshape(128, 2 * SEQ)),
                cst=np.ascontiguousarray(cst.reshape(128, 640)))


def _core_maps(shared, x, ctx, c, c_ctx, nseq, ncores):
    maps = []
    for i in range(ncores):
        cT = np.zeros((128, 8, 4), np.float32)
        for s_ in range(nseq):
            cT[:, :, s_] = c[i * nseq + s_].reshape(8, 128).T
        cT[:, :, 2] = c_ctx.reshape(8, 128).T
        m = dict(shared)
        m["x"] = np.ascontiguousarray(x[i * nseq:(i + 1) * nseq])
        m["ctx"] = np.ascontiguousarray(ctx[i * nseq:(i + 1) * nseq])
        m["cT"] = np.ascontiguousarray(cT.reshape(128, 32))
        maps.append(m)
    return maps


FUSED = True


def kernel(**inp):
    x = np.asarray(inp["x"], np.float32)
    ctx = np.asarray(inp["ctx"], np.float32)
    c = np.asarray(inp["c"], np.float32)
    c_ctx = np.asarray(inp["c_ctx"], np.float32)
    shared = _prep_weights(inp)
    cores = list(range(NCORES))
    if FUSED:
        nc = build_program([0, 1], True, False)
        res = run_bass_kernel_spmd(nc, _core_maps(shared, x, ctx, c, c_ctx, NSEQ, NCORES), core_ids=cores)
        return np.concatenate([np.asarray(r["out"], np.float32) for r in res.results], axis=0)
    nc0 = build_program([0], False, True)
    res = run_bass_kernel_spmd(nc0, _core_maps(shared, x, ctx, c, c_ctx, NSEQ, NCORES), core_ids=cores)
    x1 = np.concatenate([np.asarray(r["out"], np.float32) for r in res.results], axis=0)
    ctx1 = np.concatenate([np.asarray(r["ctx_out"], np.float32) for r in res.results], axis=0)
    nc1 = build_program([1], True, False)
    res = run_bass_kernel_spmd(nc1, _core_maps(shared, x1, ctx1, c, c_ctx, NSEQ, NCORES), core_ids=cores)
    return np.concatenate([np.asarray(r["out"], np.float32) for r in res.results], axis=0)
```

```python
import math
import contextlib
import numpy as np
import concourse.bass as bass
import concourse.mybir as mybir
from concourse.bass_utils import run_bass_kernel_spmd

F32 = mybir.dt.float32
BF16 = mybir.dt.bfloat16
AF = mybir.ActivationFunctionType
ALU = mybir.AluOpType
AX = mybir.AxisListType

D = 1024
KC = 8
SEQ = 2048
CTX = 256
TOK = CTX + SEQ
NL = 2
NCORES = 8
NSEQ = 2
EPS = 1e-6
DFF = 2816
NJ = 22
GRID_W = 64

CH_AK, CH_BK, CH_AQ, CH_BQ, CH_CC, CH_CU, CH_CB, CH_G, CH_WAB, CH_WC, CH_WOUT, CH_GU, CH_WD = (
    0, 4, 5, 9, 13, 17, 21, 25, 49, 57, 65, 73, 117)
NCHUNK = 141
V_N1G, V_N2G, V_FG, V_CONV, V_SUB, V_QG, V_KG, V_LAM = 0, 16, 32, 40, 64, 66, 70, 74
NV = 74
NVL = 512


class Buf:
    __slots__ = ("w", "r", "excl")

    def __init__(self, excl=False):
        self.w = None
        self.r = []
        self.excl = excl


class Prog:
    ENGS = ("pe", "act", "dve", "pool", "sp")

    def __init__(self, nc, n_dma_sems=32):
        self.nc = nc
        self.ops = {e: [] for e in self.ENGS}
        self.cnt = {e: 0 for e in self.ENGS}
        self.waited = {e: {} for e in self.ENGS}
        self.n_dma_sems = n_dma_sems
        self.dcum = [0] * n_dma_sems
        self.dnext = 0
        self.dnext2 = [0, 0]
        self.self_wait = {"pe": False, "act": True, "dve": True, "pool": True, "sp": False}

    def _need(self, eng, tok):
        if tok is None:
            return
        sid, val = tok
        if sid == eng and not self.self_wait[eng]:
            return
        if self.waited[eng].get(sid, 0) >= val:
            return
        if sid in self.cnt and val > self.cnt[sid]:
            raise RuntimeError("wait on pending (future) token %s %d > %d from %s" % (sid, val, self.cnt[sid], eng))
        self.waited[eng][sid] = val
        self.ops[eng].append(("wait", sid, val))

    def _deps(self, eng, reads, writes):
        for b in reads:
            self._need(eng, b.w)
            if b.excl:
                for t in b.r:
                    if t[0] != eng:
                        self._need(eng, t)
        for b in writes:
            self._need(eng, b.w)
            for t in b.r:
                self._need(eng, t)

    def _upd(self, tok, reads, writes):
        for b in writes:
            b.w = tok
            b.r = []
        for b in reads:
            r = b.r
            for i in range(len(r)):
                if r[i][0] == tok[0]:
                    r[i] = tok
                    break
            else:
                r.append(tok)

    def emit(self, eng, fn, reads=(), writes=(), inc=True):
        self._deps(eng, reads, writes)
        tok = (eng, self.cnt[eng] + 1)
        if inc:
            self.cnt[eng] += 1
        self.ops[eng].append(("op", fn, inc))
        self._upd(tok, reads, writes)
        return tok

    def dma(self, q, out, in_, reads=(), writes=()):
        half = self.n_dma_sems // 2
        qi = 1 if q == "pool" else 0
        k = qi * half + self.dnext2[qi]
        self.dnext2[qi] = (self.dnext2[qi] + 1) % half
        sid = ("d", k)
        if self.dcum[k] > 0:
            self._need(q, (sid, self.dcum[k]))
        self._deps(q, reads, writes)
        self.dcum[k] += 16
        tok = (sid, self.dcum[k])
        self.ops[q].append(("dma", out, in_, k))
        self._upd(tok, reads, writes)
        return tok

    def barrier(self, engs=("pe", "act", "dve")):
        for e in engs:
            for f in engs:
                if f != e and self.cnt[f] > 0:
                    self._need(e, (f, self.cnt[f]))

    def finish(self, q="sp"):
        for k in range(self.n_dma_sems):
            if self.dcum[k] > 0:
                self._need(q, (("d", k), self.dcum[k]))

    def build(self):
        nc = self.nc
        with contextlib.ExitStack() as st:
            esem = {e: st.enter_context(nc.semaphore("s_" + e)) for e in self.ENGS}
            dsem = [st.enter_context(nc.semaphore("d_%d" % i)) for i in range(self.n_dma_sems)]

            def getsem(sid):
                return dsem[sid[1]] if isinstance(sid, tuple) else esem[sid]

            block = st.enter_context(nc.Block())

            def run(ename):
                def f(e):
                    for op in self.ops[ename]:
                        if op[0] == "wait":
                            e.wait_ge(getsem(op[1]), op[2])
                        elif op[0] == "op":
                            ins = op[1](e)
                            if op[2]:
                                ins.then_inc(esem[ename], 1)
                        else:
                            e.dma_start(out=op[1], in_=op[2]).then_inc(dsem[op[3]], 16)
                return f

            block.tensor(run("pe"))
            block.scalar(run("act"))
            block.vector(run("dve"))
            block.gpsimd(run("pool"))
            block.sync(run("sp"))


def lam_init_of(l):
    return 0.8 - 0.6 * math.exp(-0.3 * l)


def build_program(layers, final_norm, out_ctx, nseq=NSEQ, dbg=False):
    nc = bass.Bass("TRN2", target_bir_lowering=False, dynamic_dma_scratch_size=8192)
    nl = len(layers)
    x_d = nc.dram_tensor("x", [nseq, SEQ, D], F32, kind="ExternalInput").ap()
    ctx_d = nc.dram_tensor("ctx", [nseq, CTX, D], F32, kind="ExternalInput").ap()
    cT_d = nc.dram_tensor("cT", [128, KC * 4], F32, kind="ExternalInput").ap()
    wmod_d = nc.dram_tensor("wmod", [NL, 48, 128, 1024], F32, kind="ExternalInput").ap()
    bmod_d = nc.dram_tensor("bmod", [128, NL * 48], F32, kind="ExternalInput").ap()
    vecs_d = nc.dram_tensor("vecs", [128, NV + NVL], F32, kind="ExternalInput").ap()
    rope_d = nc.dram_tensor("rope", [128, 2 * SEQ], F32, kind="ExternalInput").ap()
    cst_d = nc.dram_tensor("cst", [128, 5 * 128], F32, kind="ExternalInput").ap()
    wfm_d = nc.dram_tensor("wfm", [NL, NCHUNK, 128, 1024], F32, kind="ExternalInput").ap()
    wv_d = nc.dram_tensor("wv", [NL, 128, KC * 640], F32, kind="ExternalInput").ap()
    out_d = nc.dram_tensor("out", [nseq, SEQ, D], F32, kind="ExternalOutput").ap()
    if out_ctx:
        octx_d = nc.dram_tensor("ctx_out", [nseq, CTX, D], F32, kind="ExternalOutput").ap()

    P = Prog(nc)
    dbg_list = []

    def dump(key, ap, bufs):
        if not dbg or key in dbg_list:
            return
        dbg_list.append(key)
        shp = list(ap.shape)
        d = nc.dram_tensor("dbg_" + key, shp, ap.dtype, kind="ExternalOutput").ap()
        P.dma("sp", d, ap, reads=bufs)

    def sb(name, shape, dt):
        return nc.alloc_sbuf_tensor("sb_" + name, shape, dt).ap()

    xT = sb("xT", [128, KC, TOK], F32)
    kvreg = sb("kvreg", [128, 24192], BF16)
    gscr = sb("gscr", [128, 8208 + 8192 + 4096], BF16)
    mreg = sb("mreg", [128, 8192], BF16)
    NWS = 8
    wsl = [sb("wsl%d" % i, [128, 1024], BF16) for i in range(NWS)]
    tf = sb("tf", [128, 6, 512], F32)
    tb = sb("tb", [128, 6, 512], BF16)
    sqd = sb("sqd", [128, 2, 512], BF16)
    identF = sb("identF", [128, 128], F32)
    cstb = sb("cstb", [128, 4, 128], BF16)
    onesb = sb("onesb", [128, 128], BF16)
    modt = sb("modt", [128, NL, 48, 4], F32)
    drv = sb("drv", [128, NL, 3, 2, KC], F32)
    vecs = sb("vecs", [128, NV], F32)
    lamt = sb("lamt", [128, NL, 4], F32)
    subs = sb("subs", [128, NL], F32)
    scT = sb("scT", [128, KC, 4], F32)
    bmod = sb("bmod", [128, NL, 48], F32)

    ak = kvreg[:, 0:9216].rearrange("p (c t) -> p c t", c=4)
    bk = kvreg[:, 9216:11520]
    av = kvreg[:, 11520:20736].rearrange("p (k c) -> p k c", k=18)
    bv = kvreg[:, 20736:24192].rearrange("p (k c) -> p k c", k=18)
    act = kvreg[:, 0:22528].rearrange("p (j t) -> p j t", j=NJ)
    hg = gscr[:, 0:8200].rearrange("p (c t) -> p c t", c=8)
    hs = [gscr[:, 0:4096].rearrange("p (c t) -> p c t", c=8), gscr[:, 4096:8192].rearrange("p (c t) -> p c t", c=8)]
    qy = gscr[:, 8208:16400].rearrange("p (c t) -> p c t", c=8)
    yc = gscr[:, 16400:20496].rearrange("p (c t) -> p c t", c=4)
    kw = gscr[:, 8208:13328].rearrange("p (i c) -> p i c", i=5)
    vw = gscr[:, 13328:18448].rearrange("p (k c) -> p k c", k=8)
    merged = mreg.rearrange("p (c t) -> p c t", c=8)
    ropeT = mreg.bitcast(F32).rearrange("p (a t) -> p a t", a=2)
    gF = gscr[:, 0:16384].bitcast(F32)
    Pm, Blk, OnesD, OnesV = cstb[:, 0, :], cstb[:, 1, :], cstb[:, 2, :], cstb[:, 3, :]

    psall = nc.alloc_psum_tensor("psall", [128, 8, 512], F32).ap()
    ps = [psall[:, i, :] for i in range(8)]
    B_ps = [Buf(True) for _ in range(8)]

    B_x = [Buf() for _ in range(5)]
    B_ak, B_bk, B_av, B_bv = Buf(), Buf(), Buf(), Buf()
    B_hs = [Buf(), Buf()]
    B_hg = Buf()
    B_qy = [[Buf(), Buf()] for _ in range(8)]
    B_yc = [Buf() for _ in range(4)]
    B_kw = [Buf() for _ in range(5)]
    B_vw = Buf()
    B_mg = [[Buf(), Buf()] for _ in range(8)]
    B_rope = Buf()
    B_ws = [Buf() for _ in range(NWS)]
    B_tf = [Buf() for _ in range(6)]
    B_tb = [Buf() for _ in range(6)]
    B_sqd = [Buf() for _ in range(2)]
    B_act = [[Buf(), Buf()] for _ in range(NJ)]
    B_c = Buf()
    B_gF = Buf()
    NWM = 7
    B_wm = [Buf() for _ in range(NWM)]
    st = {"ws": 0, "tf": 0, "tb": 0, "ps": 0, "tbp": 0}

    def all_qy():
        return [b for r in B_qy for b in r]

    def all_mg():
        return [b for r in B_mg for b in r]

    def ntf():
        i = 2 + st["tf"] % 4
        st["tf"] = (st["tf"] + 1) % 4
        return tf[:, i, :], B_tf[i]

    def ntb():
        i = st["tb"]
        st["tb"] = (i + 1) % 6
        return tb[:, i, :], B_tb[i]

    def nps():
        nb = 7 if st.get("rsv7") else 8
        i = st["ps"] % nb
        st["ps"] = (i + 1) % nb
        return ps[i], B_ps[i]

    def wload(l, ch, ncols=1024):
        i = st["ws"]
        st["ws"] = (i + 1) % NWS
        P.dma("pool", wsl[i][:, 0:ncols], wfm_d[l, ch, :, 0:ncols], writes=[B_ws[i]])
        return wsl[i], B_ws[i]

    def mm(out, pairs, reads, wbuf):
        n = len(pairs)
        for i, (l, r) in enumerate(pairs):
            P.emit("pe", lambda e, l=l, r=r, i=i: e.matmul(out, lhsT=l, rhs=r, start=(i == 0), stop=(i == n - 1)),
                   reads=reads, writes=[wbuf], inc=(i == n - 1))

    def V(col):
        return vecs[:, col:col + 1]

    def act_(out, in_, func, reads, writes, **kw_):
        P.emit("act", lambda e: e.activation(out=out, in_=in_, func=func, **kw_), reads=reads, writes=writes)

    def tt(out, in0, in1, op, reads, writes):
        P.emit("dve", lambda e: e.tensor_tensor(out=out, in0=in0, in1=in1, op=op), reads=reads, writes=writes)

    def stt(out, in0, scalar, in1, op0, op1, reads, writes):
        P.emit("dve", lambda e: e.scalar_tensor_tensor(out=out, in0=in0, scalar=scalar, in1=in1, op0=op0, op1=op1),
               reads=reads, writes=writes)

    def recip(out, in_, reads, writes):
        P.emit("dve", lambda e: e.reciprocal(out=out, in_=in_), reads=reads, writes=writes)

    def rstd_from(pss, B_pss, n, reads_extra=()):
        sd, B_sd = tf[:, 0, :], B_tf[0]
        act_(sd[:, 0:n], pss[:, 0:n], AF.Ln, [B_pss] + list(reads_extra), [B_sd], bias=epsc[:, 0:1])
        rs, B_rs = tf[:, 1, :], B_tf[1]
        act_(rs[:, 0:n], sd[:, 0:n], AF.Exp, [B_sd], [B_rs], scale=-0.5)
        return rs, B_rs

    if dbg:
        P.emit("dve", lambda e: e.memset(modt.rearrange("p l j s -> p (l j s)"), 0.0), writes=[B_c, B_gF] + B_wm)
        P.emit("dve", lambda e: e.memset(drv.rearrange("p l s a k -> p (l s a k)"), 0.0), writes=[B_c, B_gF] + B_wm)
        P.emit("dve", lambda e: e.memset(lamt.rearrange("p l a -> p (l a)"), 0.0), writes=[B_c, B_gF] + B_wm)
        P.emit("dve", lambda e: e.memset(gscr, 0.0), writes=[B_c, B_gF] + B_wm)
        P.emit("dve", lambda e: e.memset(mreg, 0.0), writes=[B_c, B_gF] + B_wm)
    P.dma("sp", vecs, vecs_d[:, 0:NV], writes=[B_c])
    lamv = tf[:, 0, :]
    P.dma("sp", bmod.rearrange("p l j -> p (l j)"), bmod_d, writes=[B_c])
    P.dma("sp", scT.rearrange("p k s -> p (k s)"), cT_d, writes=[B_c])
    cstF = gF[:, 0:640]
    P.dma("sp", cstF, cst_d, writes=[B_gF])
    P.emit("dve", lambda e: e.tensor_copy(out=identF, in_=cstF[:, 0:128]), reads=[B_gF], writes=[B_c])
    P.emit("dve", lambda e: e.tensor_copy(out=cstb.rearrange("p a c -> p (a c)"), in_=cstF[:, 128:640]), reads=[B_gF], writes=[B_c])
    P.emit("dve", lambda e: e.memset(onesb, 1.0), writes=[B_c])
    epsc = sb("epsc", [128, 1], F32)
    P.emit("dve", lambda e: e.memset(epsc, EPS), writes=[B_c])
    act_(scT.rearrange("p k s -> p (k s)"), scT.rearrange("p k s -> p (k s)"), AF.Silu, [B_c], [B_c])
    def mod_prologue():
      if True:
        P.dma("sp", lamv, vecs_d[:, NV:NV + NVL], writes=[B_tf[0]])
        wmF = [gF[:, 1024 * (1 + i):1024 * (2 + i)] for i in range(NWM)]
        for l in layers:
            pm, B_pm = nps()
            for j in range(48):
                s_ = (l * 48 + j) % NWM
                P.dma("sp" if j % 2 == 0 else "pool", wmF[s_], wmod_d[l, j], writes=[B_wm[s_]])
                mm(pm[:, j * 4:(j + 1) * 4], [(wmF[s_][:, kc * 128:(kc + 1) * 128], scT[:, kc, :]) for kc in range(KC)],
                   [B_wm[s_], B_c], B_pm)
            for s_ in range(4):
                tt(modt[:, l, :, s_], pm[:, 0:192].rearrange("p (j s) -> p j s", s=4)[:, :, s_], bmod[:, l, :], ALU.add,
                   [B_pm, B_c], [B_c])
            for s_ in range(3):
                for a, (msc, vg) in enumerate(((8, V_N1G), (32, V_N2G))):
                    stt(drv[:, l, s_, a, :], modt[:, l, msc:msc + 8, s_], 1.0, vecs[:, vg + l * 8:vg + l * 8 + 8], ALU.add, ALU.mult,
                        [B_c], [B_c])
            for a in range(2):
                t_, B_t = ntf()
                base = l * 256 + a * 128
                tt(t_[:, 0:64], lamv[:, base:base + 64], lamv[:, base + 64:base + 128], ALU.mult, [B_tf[0]], [B_t])
                P.emit("dve", lambda e, t_=t_, a=a, l=l: e.tensor_reduce(out=lamt[:, l, 2 + a:3 + a], in_=t_[:, 0:64], axis=AX.X, op=ALU.add),
                       reads=[B_t], writes=[B_c])
            act_(lamt[:, l, 2:4], lamt[:, l, 2:4], AF.Exp, [B_c], [B_c])
            li = lam_init_of(l)
            stt(lamt[:, l, 1:2], lamt[:, l, 3:4], -li, lamt[:, l, 2:3], ALU.add, ALU.subtract, [B_c], [B_c])
            P.emit("dve", lambda e, l=l, li=li: e.tensor_scalar(out=subs[:, l:l + 1], in0=vecs[:, V_SUB + l:V_SUB + l + 1], scalar1=1.0 - li,
                                                        scalar2=None, op0=ALU.mult), reads=[B_c], writes=[B_c])


    def mod_ap(l, m, kc, s_):
        return modt[:, l, m * 8 + kc, s_:s_ + 1]

    def norm_parts(l, s_, which, xoff, n, dst, B_dst, B_xs, sqbufs=None, bank=None):
        state = {}

        def get_bank():
            if "pss" not in state:
                state["pss"] = (ps[bank], B_ps[bank]) if bank is not None else nps()
            return state["pss"]

        def sq(k):
            if sqbufs is None:
                b, B_b = ntb()
            else:
                b, B_b = sqbufs[k % len(sqbufs)]
            state[k] = (b, B_b)
            act_(b[:, 0:n], xT[:, k, xoff:xoff + n], AF.Square, B_xs, [B_b])

        def mmk(k):
            pss, B_pss = get_bank()
            b, B_b = state[k]
            P.emit("pe", lambda e: e.matmul(pss[:, 0:n], lhsT=OnesD, rhs=b[:, 0:n], start=(k == 0), stop=(k == KC - 1)),
                   reads=[B_b, B_c], writes=[B_pss], inc=True)

        def fin():
            pss, B_pss = get_bank()
            rs, B_rs = rstd_from(pss, B_pss, n)
            msh = 0 if which == 0 else 3
            for kc in range(KC):
                t_, B_t = ntf()
                stt(t_[:, 0:n], xT[:, kc, xoff:xoff + n], drv[:, l, s_, which, kc:kc + 1], rs[:, 0:n], ALU.mult, ALU.mult,
                    B_xs + [B_rs, B_c], [B_t])
                act_(dst[:, kc, 0:n], t_[:, 0:n], AF.Identity, [B_t, B_c], [B_dst], bias=mod_ap(l, msh, kc, s_), scale=1.0)
        return sq, mmk, fin

    def norm_tile(l, s_, which, xoff, n, dst, B_dst, B_xs):
        sq, mmk, fin = norm_parts(l, s_, which, xoff, n, dst, B_dst, B_xs)
        for kc in range(KC):
            sq(kc)
            mmk(kc)
        fin()

    def proj_fm(w, B_w, src, B_src, n, ntiles=8):
        pz, B_pz = nps()
        mm(pz[:, 0:n], [(w[:, kc * 128:(kc + 1) * 128], src(kc)) for kc in range(ntiles)], [B_w] + B_src, B_pz)
        return pz, B_pz

    def qk_post(pz, B_pz, n, lat_off, gcol, normed, dst, B_dst):
        rs = None
        roped = lat_off is not None
        if normed:
            z2, B_z2 = ntb()
            act_(z2[:, 0:n], pz[:, 0:n], AF.Square, [B_pz], [B_z2])
        if roped:
            zb, B_zb = ntb()
            act_(zb[:, 0:n], pz[:, 0:n], AF.Copy, [B_pz], [B_zb])
        if normed:
            pss, B_pss = nps()
            mm(pss[:, 0:n], [(Blk, z2[:, 0:n])], [B_z2, B_c], B_pss)
        if roped:
            psw, B_psw = nps()
            mm(psw[:, 0:n], [(Pm, zb[:, 0:n])], [B_zb, B_c], B_psw)
        if normed:
            rs, B_rs = rstd_from(pss, B_pss, n)
        if not roped:
            if normed:
                stt(dst, pz[:, 0:n], V(gcol), rs[:, 0:n], ALU.mult, ALU.mult, [B_pz, B_rs, B_c], [B_dst])
            else:
                act_(dst, pz[:, 0:n], AF.Copy, [B_pz], [B_dst])
            return
        C = ropeT[:, 0, lat_off:lat_off + n]
        S = ropeT[:, 1, lat_off:lat_off + n]
        t1, B_t1 = ntf()
        t2, B_t2 = ntf()
        if normed:
            stt(t1[:, 0:n], pz[:, 0:n], V(gcol), C, ALU.mult, ALU.mult, [B_pz, B_rope, B_c], [B_t1])
            stt(t2[:, 0:n], psw[:, 0:n], V(gcol + 1), S, ALU.mult, ALU.mult, [B_psw, B_rope, B_c], [B_t2])
            tt(t1[:, 0:n], t1[:, 0:n], t2[:, 0:n], ALU.add, [B_t1, B_t2], [B_t1])
            tt(dst, t1[:, 0:n], rs[:, 0:n], ALU.mult, [B_t1, B_rs], [B_dst])
        else:
            tt(t1[:, 0:n], pz[:, 0:n], C, ALU.mult, [B_pz, B_rope], [B_t1])
            tt(t2[:, 0:n], psw[:, 0:n], S, ALU.mult, [B_psw, B_rope], [B_t2])
            tt(dst, t1[:, 0:n], t2[:, 0:n], ALU.add, [B_t1, B_t2], [B_dst])

    def run_items(items):
        pend = None
        for (A, Bf) in items:
            r = A()
            if pend is not None:
                pend[0](*pend[1])
            pend = (Bf, r)
        if pend is not None:
            pend[0](*pend[1])

    def load_rope(extra_writes, lo=0, nn=SEQ):
        P.dma("sp", ropeT[:, :, lo:lo + nn], rope_d.rearrange("p (a t) -> p a t", a=2)[:, :, lo:lo + nn],
              writes=[B_rope] + all_mg() + extra_writes)

    TILES = [(0, 256), (256, 512), (768, 512), (1280, 512), (1792, 512)]

    def kv_phase(l, q, s_lat):
        for i in range(5):
            P.dma("pool", kw[:, i, :], wfm_d[l, CH_AK + i], writes=[B_kw[i]] + (all_qy() + B_yc + [B_gF] + B_wm if i == 0 else []))
        P.dma("pool", vw, wv_d[l].rearrange("p (k c) -> p k c", k=8), writes=[B_vw])
        load_rope([])
        P.emit("dve", lambda e: e.memset(bv[:, :, 64:128], 1.0), writes=[B_bv])
        st["rsv7"] = True

        def kv_norm(ti):
            off, n = TILES[ti]
            norm_tile(l, 2 if ti == 0 else s_lat, 0, off, n, hs[ti % 2], B_hs[ti % 2], [B_x[ti]])
        kv_norm(0)
        for ti, (off, n) in enumerate(TILES):
            h, B_h = hs[ti % 2], B_hs[ti % 2]
            lat_off = None if ti == 0 else off - CTX
            items = []
            for c in range(5):
                def A(c=c, h=h, B_h=B_h, n=n):
                    return proj_fm(kw[:, c, :], B_kw[c], lambda kc: h[:, kc, 0:n], [B_h], n)

                def Bf(pz, B_pz, c=c, n=n, off=off, lat_off=lat_off):
                    if c < 4:
                        qk_post(pz, B_pz, n, lat_off, 0, False, ak[:, c, off:off + n], B_ak)
                    else:
                        qk_post(pz, B_pz, n, lat_off, V_KG + l * 2, True, bk[:, off:off + n], B_bk)
                items.append((A, Bf))
            for sub in range(n // 128):
                kci = off // 128 + sub

                def A(sub=sub, h=h, B_h=B_h):
                    pv, B_pv = nps()
                    mm(pv, [(h[:, kc, sub * 128:(sub + 1) * 128], vw[:, kc, 0:512]) for kc in range(KC)], [B_h, B_vw], B_pv)
                    pv2, B_pv2 = nps()
                    mm(pv2[:, 0:128], [(h[:, kc, sub * 128:(sub + 1) * 128], vw[:, kc, 512:640]) for kc in range(KC)], [B_h, B_vw], B_pv2)
                    return (pv, pv2), (B_pv, B_pv2)

                def Bf(pvs, Bs, kci=kci):
                    pv, pv2 = pvs
                    B_pv, B_pv2 = Bs
                    act_(av[:, kci, :], pv, AF.Copy, [B_pv], [B_av])
                    P.emit("dve", lambda e: e.tensor_copy(out=bv[:, kci, 0:64], in_=pv2[:, 0:64]), reads=[B_pv2], writes=[B_bv])
                    P.emit("dve", lambda e: e.tensor_copy(out=bv[:, kci, 128:192], in_=pv2[:, 64:128]), reads=[B_pv2], writes=[B_bv])
                items.append((A, Bf))
            if ti + 1 < len(TILES):
                off2, n2 = TILES[ti + 1]
                nsq, nmm, nfin = norm_parts(l, s_lat, 0, off2, n2, hs[(ti + 1) % 2], B_hs[(ti + 1) % 2], [B_x[ti + 1]],
                                            sqbufs=[(sqd[:, i, :], B_sqd[i]) for i in range(2)], bank=7)
                wrapped = []
                for j, (A, Bf) in enumerate(items):
                    def A2(A=A, j=j):
                        if 1 <= j <= 4:
                            nmm(2 * j - 2)
                            nmm(2 * j - 1)
                        if j <= 3:
                            nsq(2 * j)
                            nsq(2 * j + 1)
                        if j == 5:
                            nfin()
                        return A()
                    wrapped.append((A2, Bf))
                items = wrapped
            run_items(items)
        st["rsv7"] = False
        P.barrier()
        dump("xT", xT.rearrange("p c t -> p (c t)"), B_x)
        dump("kv", kvreg, [B_ak, B_bk, B_av, B_bv])

    def attention(l, main, nkc):
        for ti, (qo, n) in enumerate(main):
            for h in range(8):
                diff = h < 4
                SA = [(ps[0], B_ps[0]), (ps[2], B_ps[2])]
                SB = [(ps[1], B_ps[1]), (ps[3], B_ps[3])]
                kT = (lambda kc, h=h: ak[:, h, kc * 128:(kc + 1) * 128]) if diff else (lambda kc: bk[:, kc * 128:(kc + 1) * 128])
                B_k = B_ak if diff else B_bk
                qT = qy[:, h, qo:qo + n]
                B_q = B_qy[h][ti]
                gA, gB = (4, 5) if h % 2 == 0 else (6, 7)

                def smm(kc):
                    a, B_a = SA[kc % 2]
                    b, B_b = SB[kc % 2]
                    kt = kT(kc)
                    mm(a[:, 0:n], [(kt[0:64, :], qT[0:64, :])], [B_k, B_q], B_a)
                    mm(b[:, 0:n], [(kt[64:128, :], qT[64:128, :])], [B_k, B_q], B_b)

                smm(0)
                for kc in range(nkc):
                    if kc + 1 < nkc:
                        smm(kc + 1)
                    a, B_a = SA[kc % 2]
                    b, B_b = SB[kc % 2]
                    pi = 2 * (st["tbp"] % 3)
                    st["tbp"] += 1
                    p1, B_p1, p2, B_p2 = tb[:, pi, :], B_tb[pi], tb[:, pi + 1, :], B_tb[pi + 1]
                    act_(p1[:, 0:n], a[:, 0:n], AF.Exp, [B_a], [B_p1], scale=0.125)
                    act_(p2[:, 0:n], b[:, 0:n], AF.Exp, [B_b], [B_p2], scale=0.125)
                    f, la = (kc == 0), (kc == nkc - 1)

                    def acc(bank, lhsT, rhs, rd, f=f, la=la):
                        P.emit("pe", lambda e: e.matmul(ps[bank][:, 0:n], lhsT=lhsT, rhs=rhs, start=f, stop=la),
                               reads=rd, writes=[B_ps[bank]], inc=la)
                    if diff:
                        vt = av[:, kc, h * 128:(h + 1) * 128]
                        acc(4, vt, p1[:, 0:n], [B_av, B_p1])
                        acc(5, onesb, p1[:, 0:n], [B_c, B_p1])
                        acc(6, vt, p2[:, 0:n], [B_av, B_p2])
                        acc(7, onesb, p2[:, 0:n], [B_c, B_p2])
                    else:
                        acc(gA, bv[:, kc, 0:128], p1[:, 0:n], [B_bv, B_p1])
                        acc(gB, bv[:, kc, 64:192], p2[:, 0:n], [B_bv, B_p2])
                if diff:
                    sa, B_sa = ntf()
                    sb2, B_sb2 = ntf()
                    t1, B_t1 = ntf()
                    t2, B_t2 = ntf()
                    P.emit("dve", lambda e, sa=sa: e.tensor_copy(out=sa[:, 0:n], in_=ps[5][:, 0:n]), reads=[B_ps[5]], writes=[B_sa])
                    P.emit("dve", lambda e, sb2=sb2: e.tensor_copy(out=sb2[:, 0:n], in_=ps[7][:, 0:n]), reads=[B_ps[7]], writes=[B_sb2])
                    tt(t1[:, 0:n], ps[4][:, 0:n], sb2[:, 0:n], ALU.mult, [B_ps[4], B_sb2], [B_t1])
                    tt(t2[:, 0:n], ps[6][:, 0:n], sa[:, 0:n], ALU.mult, [B_ps[6], B_sa], [B_t2])
                    stt(t1[:, 0:n], t2[:, 0:n], lamt[:, l, 1:2], t1[:, 0:n], ALU.mult, ALU.add, [B_t1, B_t2, B_c], [B_t1])
                    tt(sa[:, 0:n], sa[:, 0:n], sb2[:, 0:n], ALU.mult, [B_sa, B_sb2], [B_sa])
                    pi = 2 * (st["tbp"] % 3)
                    st["tbp"] += 1
                    d2, B_d2 = tb[:, pi, :], B_tb[pi]
                    tt(d2[:, 0:n], t1[:, 0:n], t1[:, 0:n], ALU.mult, [B_t1], [B_d2])
                    mm(ps[5][:, 0:n], [(OnesV, d2[:, 0:n])], [B_d2, B_c], B_ps[5])
                    stt(sb2[:, 0:n], sa[:, 0:n], EPS, sa[:, 0:n], ALU.mult, ALU.mult, [B_sa], [B_sb2])
                    tt(sb2[:, 0:n], ps[5][:, 0:n], sb2[:, 0:n], ALU.add, [B_ps[5], B_sb2], [B_sb2])
                    act_(sb2[:, 0:n], sb2[:, 0:n], AF.Ln, [B_sb2], [B_sb2])
                    act_(sb2[:, 0:n], sb2[:, 0:n], AF.Exp, [B_sb2], [B_sb2], scale=-0.5)
                    stt(qT, t1[:, 0:n], subs[:, l:l + 1], sb2[:, 0:n], ALU.mult, ALU.mult, [B_t1, B_sb2, B_c], [B_q])
                else:
                    r1, B_r1 = ntf()
                    recip(r1[0:64, 0:n], ps[gA][64:128, 0:n], [B_ps[gA]], [B_r1])
                    recip(r1[64:128, 0:n], ps[gB][0:64, 0:n], [B_ps[gB]], [B_r1])
                    tt(qT[0:64, :], ps[gA][0:64, 0:n], r1[0:64, 0:n], ALU.mult, [B_ps[gA], B_r1], [B_q])
                    tt(qT[64:128, :], ps[gB][64:128, 0:n], r1[64:128, 0:n], ALU.mult, [B_ps[gB], B_r1], [B_q])

    def group_norm(l, s_, pieces):
        for (off, n, hc) in pieces:
            xt = [B_x[i] for i, (o2, n2) in enumerate(TILES) if o2 < off + n and off < o2 + n2]
            norm_tile(l, s_, 0, off, n, hg[:, :, hc:hc + n], B_hg, xt)

    def group_phase(l, s_, pieces, main, halo_side, lat0, nkc):
        is_lat = lat0 is not None
        if is_lat:
            load_rope([], lat0, 1024)
        items = []
        for c in range(8):
            for ti, (hc, n, xi) in enumerate(main):
                def A(c=c, ti=ti, hc=hc, n=n):
                    if ti == 0:
                        qw[c] = wload(l, CH_AQ + c)
                    w, B_w = qw[c]
                    return proj_fm(w, B_w, lambda kc: hg[:, kc, hc:hc + n], [B_hg], n)

                def Bf(pz, B_pz, c=c, ti=ti, hc=hc, n=n):
                    lo = (lat0 + hc) if is_lat else None
                    qk_post(pz, B_pz, n, lo, V_QG + l * 2, c >= 4, qy[:, c, hc:hc + n], B_qy[c][ti])
                items.append((A, Bf))
        qw = {}
        run_items(items)
        dump("q_%s" % halo_side, gscr[:, 8208:16400], all_qy())
        attention(l, [(hc, n) for (hc, n, xi) in main], nkc)
        dump("y_%s" % halo_side, gscr[:, 8208:16400], all_qy())
        st["ps"] = 0
        ntot = sum(n for (hc, n, xi) in main)
        for j in range(4):
            wc_, B_wc = wload(l, CH_CC + j)
            wu_, B_wu = wload(l, CH_CU + j)
            pext = tf[:, 0:3, :].rearrange("p a t -> p (a t)")
            B_pe = B_tf[0:3]
            for (off, n, hc) in pieces:
                pc, B_pc = proj_fm(wc_, B_wc, lambda kc: hg[:, kc, hc:hc + n], [B_hg], n)
                pu, B_pu = proj_fm(wu_, B_wu, lambda kc: hg[:, kc, hc:hc + n], [B_hg], n)
                if hc >= ntot:
                    dcol = 0 if halo_side == "left" else ntot + 1
                else:
                    dcol = 1 + hc
                cc_, B_cc = tf[:, 3, :], B_tf[3]
                act_(cc_[:, 0:n], pc[:, 0:n], AF.Copy, [B_pc], [B_cc])
                tt(pext[:, dcol:dcol + n], cc_[:, 0:n], pu[:, 0:n], ALU.mult, [B_cc, B_pu], B_pe)
            zc = []
            if halo_side != "left":
                zc.append(0)
            if halo_side != "right":
                zc.append(ntot + 1)
            for z in zc:
                P.emit("dve", lambda e, z=z: e.memset(pext[:, z:z + 1], 0.0), writes=B_pe)
            wb_, B_wb = wload(l, CH_CB + j)
            for ti, (hc, n, xi) in enumerate(main):
                pb, B_pb = proj_fm(wb_, B_wb, lambda kc: hg[:, kc, hc:hc + n], [B_hg], n)
                q_, B_q_ = tf[:, 4 + ti % 2, :], B_tf[4 + ti % 2]
                cw = V_CONV + l * 12 + j
                act_(q_[:, 0:n], pext[:, 1 + hc:1 + hc + n], AF.Identity, B_pe + [B_c], [B_q_], scale=V(cw + 4))
                stt(q_[:, 0:n], pext[:, hc:hc + n], V(cw), q_[:, 0:n], ALU.mult, ALU.add, B_pe + [B_c, B_q_], [B_q_])
                stt(q_[:, 0:n], pext[:, 2 + hc:2 + hc + n], V(cw + 8), q_[:, 0:n], ALU.mult, ALU.add, B_pe + [B_c, B_q_], [B_q_])
                tt(yc[:, j, hc:hc + n], pb[:, 0:n], q_[:, 0:n], ALU.mult, [B_pb, B_q_], [B_yc[j]])
        dump("yc_%s" % halo_side, gscr[:, 16400:20496], B_yc)
        st["tf"] = 0
        for m in range(8):
            wg = [wload(l, CH_G + 3 * m + b) for b in range(3)]
            wab, B_wab = wload(l, CH_WAB + m)
            for ti, (hc, n, xi) in enumerate(main):
                sg = []
                for b in range(3):
                    pg, B_pg = proj_fm(wg[b][0], wg[b][1], lambda kc: hg[:, kc, hc:hc + n], [B_hg], n)
                    s__, B_s = tf[:, b, :], B_tf[b]
                    act_(s__[:, 0:n], pg[:, 0:n], AF.Sigmoid, [B_pg], [B_s])
                    sg.append((s__, B_s))
                if ti == 0:
                    wcm, B_wcm = wload(l, CH_WC + m, 512)
                pa, B_pa = nps()
                mm(pa[:, 0:n], [(wab[:, kc * 128:(kc + 1) * 128], qy[:, kc, hc:hc + n]) for kc in range(4)],
                   [B_wab] + [B_qy[kc][ti] for kc in range(4)], B_pa)
                pb2, B_pb2 = nps()
                mm(pb2[:, 0:n], [(wab[:, (4 + kc) * 128:(5 + kc) * 128], qy[:, 4 + kc, hc:hc + n]) for kc in range(4)],
                   [B_wab] + [B_qy[4 + kc][ti] for kc in range(4)], B_pb2)
                pc2, B_pc2 = nps()
                mm(pc2[:, 0:n], [(wcm[:, kc * 128:(kc + 1) * 128], yc[:, kc, hc:hc + n]) for kc in range(4)],
                   [B_wcm] + B_yc, B_pc2)
                t1, B_t1 = tf[:, 3, :], B_tf[3]
                t2, B_t2 = tf[:, 4, :], B_tf[4]
                tt(t1[:, 0:n], pa[:, 0:n], sg[0][0][:, 0:n], ALU.mult, [B_pa, sg[0][1]], [B_t1])
                tt(t2[:, 0:n], pb2[:, 0:n], sg[1][0][:, 0:n], ALU.mult, [B_pb2, sg[1][1]], [B_t2])
                tt(t1[:, 0:n], t1[:, 0:n], t2[:, 0:n], ALU.add, [B_t1, B_t2], [B_t1])
                tt(t2[:, 0:n], pc2[:, 0:n], sg[2][0][:, 0:n], ALU.mult, [B_pc2, sg[2][1]], [B_t2])
                tt(merged[:, m, hc:hc + n], t1[:, 0:n], t2[:, 0:n], ALU.add, [B_t1, B_t2], [B_mg[m][ti], B_rope])
        dump("mg_%s" % halo_side, mreg, all_mg())

    def spread(items, norms):
        sched = {}
        pos = 0
        for (nsq, nmm, nfin) in norms:
            for j in range(4):
                sched.setdefault(pos + j, []).append((nsq, 2 * j))
                sched.setdefault(pos + j, []).append((nsq, 2 * j + 1))
                sched.setdefault(pos + j + 1, []).append((nmm, 2 * j))
                sched.setdefault(pos + j + 1, []).append((nmm, 2 * j + 1))
            sched.setdefault(pos + 5, []).append((nfin, None))
            pos += 6
        assert pos <= len(items), (pos, len(items))
        out = []
        for j, (A, Bf) in enumerate(items):
            def A2(A=A, j=j):
                for f, k in sorted(sched.get(j, []), key=lambda fk: 0 if fk[0].__name__ == "mmk" else 1):
                    f(k) if k is not None else f()
                return A()
            out.append((A2, Bf))
        return out

    def tile_norms(l, s_, which, main):
        return [norm_parts(l, s_, which, TILES[xi][0], n, hg[:, :, hc:hc + n], B_hg, [B_x[xi]],
                           sqbufs=[(sqd[:, i, :], B_sqd[i]) for i in range(2)], bank=7) for (hc, n, xi) in main]

    def group_wout(l, s_, main, norms=()):
        items = []
        wq = {}
        for nn in range(8):
            for ti, (hc, n, xi) in enumerate(main):
                def A(nn=nn, ti=ti, hc=hc, n=n):
                    if ti == 0:
                        wq[nn] = wload(l, CH_WOUT + nn)
                    w, B_w = wq[nn]
                    po, B_po = nps()
                    mm(po[:, 0:n], [(w[:, m * 128:(m + 1) * 128], merged[:, m, hc:hc + n]) for m in range(8)],
                       [B_w] + [B_mg[m][ti] for m in range(8)], B_po)
                    return po, B_po

                def Bf(po, B_po, nn=nn, n=n, xi=xi):
                    xo = TILES[xi][0]
                    stt(xT[:, nn, xo:xo + n], po[:, 0:n], mod_ap(l, 2, nn, s_), xT[:, nn, xo:xo + n], ALU.mult, ALU.add,
                        [B_po, B_c, B_x[xi]], [B_x[xi]])
                items.append((A, Bf))
        st["rsv7"] = True
        run_items(spread(items, norms) if norms and len(items) >= 6 * len(norms) else items)
        st["rsv7"] = False

    def group_wout_old(l, s_, main):
        for nn in range(8):
            w, B_w = wload(l, CH_WOUT + nn)
            for ti, (hc, n, xi) in enumerate(main):
                po, B_po = nps()
                mm(po[:, 0:n], [(w[:, m * 128:(m + 1) * 128], merged[:, m, hc:hc + n]) for m in range(8)],
                   [B_w] + [B_mg[m][ti] for m in range(8)], B_po)
                xo = TILES[xi][0]
                stt(xT[:, nn, xo:xo + n], po[:, 0:n], mod_ap(l, 2, nn, s_), xT[:, nn, xo:xo + n], ALU.mult, ALU.add,
                    [B_po, B_c, B_x[xi]], [B_x[xi]])

    def ffn_norm(l, s_, main):
        for ti, (hc, n, xi) in enumerate(main):
            norm_tile(l, s_, 1, TILES[xi][0], n, hg[:, :, hc:hc + n], B_hg, [B_x[xi]])

    def ffn_gu(l, s_, main):
        for j in range(NJ):
            wg_, B_wg = wload(l, CH_GU + 2 * j)
            wu_, B_wu = wload(l, CH_GU + 2 * j + 1)
            for ti, (hc, n, xi) in enumerate(main):
                pg, B_pg = proj_fm(wg_, B_wg, lambda kc: hg[:, kc, hc:hc + n], [B_hg], n)
                pu, B_pu = proj_fm(wu_, B_wu, lambda kc: hg[:, kc, hc:hc + n], [B_hg], n)
                sl, B_sl = ntf()
                act_(sl[:, 0:n], pg[:, 0:n], AF.Silu, [B_pg], [B_sl])
                tt(act[:, j, hc:hc + n], sl[:, 0:n], pu[:, 0:n], ALU.mult, [B_sl, B_pu], [B_act[j][ti]])

    def ffn_down(l, s_, main, norms=()):
        items = []
        wq = {}
        for nn in range(8):
            for ti, (hc, n, xi) in enumerate(main):
                def A(nn=nn, ti=ti, hc=hc, n=n):
                    if ti == 0:
                        wq[nn] = [wload(l, CH_WD + 3 * nn + i, 1024 if i < 2 else 768) for i in range(3)]
                    ws_ = wq[nn]
                    po, B_po = nps()
                    mm(po[:, 0:n], [(ws_[j // 8][0][:, (j % 8) * 128:(j % 8 + 1) * 128], act[:, j, hc:hc + n]) for j in range(NJ)],
                       [w_[1] for w_ in ws_] + [B_act[j][ti] for j in range(NJ)], B_po)
                    return po, B_po

                def Bf(po, B_po, nn=nn, n=n, xi=xi):
                    xo = TILES[xi][0]
                    stt(xT[:, nn, xo:xo + n], po[:, 0:n], mod_ap(l, 5, nn, s_), xT[:, nn, xo:xo + n], ALU.mult, ALU.add,
                        [B_po, B_c, B_x[xi]], [B_x[xi]])
                items.append((A, Bf))
        st["rsv7"] = True
        run_items(spread(items, norms) if norms and len(items) >= 6 * len(norms) else items)
        st["rsv7"] = False

    def ffn_down_old(l, s_, main):
        for nn in range(8):
            ws_ = [wload(l, CH_WD + 3 * nn + i, 1024 if i < 2 else 768) for i in range(3)]
            for ti, (hc, n, xi) in enumerate(main):
                po, B_po = nps()
                mm(po[:, 0:n], [(ws_[j // 8][0][:, (j % 8) * 128:(j % 8 + 1) * 128], act[:, j, hc:hc + n]) for j in range(NJ)],
                   [w_[1] for w_ in ws_] + [B_act[j][ti] for j in range(NJ)], B_po)
                xo = TILES[xi][0]
                stt(xT[:, nn, xo:xo + n], po[:, 0:n], mod_ap(l, 5, nn, s_), xT[:, nn, xo:xo + n], ALU.mult, ALU.add,
                    [B_po, B_c, B_x[xi]], [B_x[xi]])

    def load_x(q):
        for i in range(TOK // 128):
            src = ctx_d[q, i * 128:(i + 1) * 128, :] if i < 2 else x_d[q, (i - 2) * 128:(i - 1) * 128, :]
            s_ = i % 2
            stg = tf[:, 2 * s_:2 * s_ + 2, :].rearrange("p a t -> p (a t)")
            B_s = B_tf[2 * s_:2 * s_ + 2]
            P.dma("sp" if s_ == 0 else "pool", stg, src, writes=B_s)
            xi = [j for j, (o, n) in enumerate(TILES) if o <= i * 128 < o + n][0]
            for half in range(2):
                pt, B_pt = nps()
                for k4 in range(4):
                    kc = half * 4 + k4
                    P.emit("pe", lambda e, pt=pt, k4=k4, kc=kc, stg=stg: e.transpose(out=pt[:, k4 * 128:(k4 + 1) * 128], in_=stg[:, kc * 128:(kc + 1) * 128],
                                                                                    identity=identF), reads=B_s + [B_c], writes=[B_pt], inc=(k4 == 3))
                dst = xT[:, half * 4:half * 4 + 4, i * 128:(i + 1) * 128]
                srcp = pt.rearrange("p (a t) -> p a t", a=4)
                if half == 0:
                    P.emit("act", lambda e, dst=dst, srcp=srcp: e.activation(out=dst, in_=srcp, func=AF.Copy), reads=[B_pt], writes=[B_x[xi]])
                else:
                    P.emit("dve", lambda e, dst=dst, srcp=srcp: e.tensor_copy(out=dst, in_=srcp), reads=[B_pt], writes=[B_x[xi]])

    def store_tokens(q, tok0, ntok, dst_d, dst_row0, norm):
        yT = gF.rearrange("p (c t) -> p c t", c=8)
        for t0 in range(0, ntok, 512):
            n = min(512, ntok - t0)
            xo = tok0 + t0
            xi = [j for j, (o, nn) in enumerate(TILES) if o <= xo < o + nn][0]
            if norm:
                pss, B_pss = nps()
                for kc in range(KC):
                    sq, B_sq = ntb()
                    act_(sq[:, 0:n], xT[:, kc, xo:xo + n], AF.Square, [B_x[xi]], [B_sq])
                    P.emit("pe", lambda e, sq=sq, kc=kc, pss=pss, n=n: e.matmul(pss[:, 0:n], lhsT=OnesD, rhs=sq[:, 0:n], start=(kc == 0), stop=(kc == KC - 1)),
                           reads=[B_sq, B_c], writes=[B_pss], inc=True)
                rs, B_rs = rstd_from(pss, B_pss, n)
                for kc in range(KC):
                    stt(yT[:, kc, 0:n], xT[:, kc, xo:xo + n], V(V_FG + kc), rs[:, 0:n], ALU.mult, ALU.mult, [B_x[xi], B_rs, B_c], [B_gF])
                srcT, B_src = (lambda kc, a: yT[:, kc, a * 128:(a + 1) * 128]), [B_gF]
            else:
                srcT, B_src = (lambda kc, a, xo=xo: xT[:, kc, xo + a * 128:xo + (a + 1) * 128]), [B_x[xi]]
            for a in range(n // 128):
                s_ = a % 2
                stg = tf[:, 2 * s_:2 * s_ + 2, :].rearrange("p a t -> p (a t)")
                B_s = B_tf[2 * s_:2 * s_ + 2]
                for half in range(2):
                    pt, B_pt = nps()
                    for k4 in range(4):
                        kc = half * 4 + k4
                        P.emit("pe", lambda e, pt=pt, k4=k4, kc=kc, a=a, srcT=srcT: e.transpose(
                            out=pt[:, k4 * 128:(k4 + 1) * 128], in_=srcT(kc, a), identity=identF),
                            reads=B_src + [B_c], writes=[B_pt], inc=(k4 == 3))
                    if half == 0:
                        P.emit("act", lambda e, pt=pt, stg=stg: e.activation(out=stg[:, 0:512], in_=pt, func=AF.Copy), reads=[B_pt], writes=B_s)
                    else:
                        P.emit("dve", lambda e, pt=pt, stg=stg: e.tensor_copy(out=stg[:, 512:1024], in_=pt), reads=[B_pt], writes=B_s)
                r0 = dst_row0 + t0 + a * 128
                P.dma("sp" if s_ == 0 else "pool", dst_d[q, r0:r0 + 128, :], stg, reads=B_s)

    for q in range(nseq):
        load_x(q)
        if q == 0:
            mod_prologue()
            P.barrier()
        for li, l in enumerate(layers):
            last = (l == NL - 1)
            kv_phase(l, q, q)
            groups = []
            if not last:
                groups.append((2, [(0, 256, 0)], [(0, 256, 0)], "none", None, 2))
            groups.append((q, [(256, 512, 0), (768, 512, 512), (1280, 1, 1024)], [(0, 512, 1), (512, 512, 2)], "right", 0, 18))
            groups.append((q, [(1280, 512, 0), (1792, 512, 512), (1279, 1, 1024)], [(0, 512, 3), (512, 512, 4)], "left", 1024, 18))
            ng = len(groups)
            group_norm(l, groups[0][0], groups[0][1])
            for gi, (s_, pieces, main, hside, lat0, nkc) in enumerate(groups):
                group_phase(l, s_, pieces, main, hside, lat0, nkc)
                if gi + 1 < ng:
                    ns_, npieces, nmain = groups[gi + 1][0], groups[gi + 1][1], groups[gi + 1][2]
                    halo = [p_ for p_ in npieces if p_[1] == 1]
                    if halo:
                        group_norm(l, ns_, halo)
                    if len(main) * 8 >= 6 * len(nmain):
                        group_wout(l, s_, main, tile_norms(l, ns_, 0, nmain))
                    else:
                        group_norm(l, ns_, [p_ for p_ in npieces if p_[1] != 1])
                        group_wout(l, s_, main)
                else:
                    group_wout(l, s_, main, tile_norms(l, groups[0][0], 1, groups[0][2]))
            P.barrier()
            dump("xT_mid", xT.rearrange("p c t -> p (c t)"), B_x)
            for gi, (s_, pieces, main, hside, lat0, nkc) in enumerate(groups):
                ffn_gu(l, s_, main)
                if gi + 1 < ng and len(main) * 8 >= 6 * len(groups[gi + 1][2]):
                    ffn_down(l, s_, main, tile_norms(l, groups[gi + 1][0], 1, groups[gi + 1][2]))
                else:
                    if gi + 1 < ng:
                        ffn_norm(l, groups[gi + 1][0], groups[gi + 1][2])
                    ffn_down(l, s_, main)
            P.barrier()
        store_tokens(q, CTX, SEQ, out_d, 0, final_norm)
        if out_ctx:
            store_tokens(q, 0, CTX, octx_d, 0, False)
        P.barrier()
    P.finish("sp")
    P.build()
    return nc


def _fm_chunk(w, rows, cols):
    raise NotImplementedError


def _prep_weights(inp):
    w_in = np.asarray(inp["w_in"], np.float32)
    wfm = np.zeros((NL, NCHUNK, 128, 8, 128), np.float32)
    wv = np.zeros((NL, 128, 8, 640), np.float32)

    def fm(wcols):
        return wcols.reshape(8, 128, 128).transpose(1, 0, 2)

    oq, okk, ov, obq, obk, obv, ocb, occ, ocu, og = 0, 512, 1024, 1536, 2048, 2176, 2304, 2816, 3328, 3840
    for l in range(NL):
        W = w_in[l]
        for c in range(4):
            wfm[l, CH_AK + c] = fm(W[:, okk + c * 128: okk + (c + 1) * 128])
            wfm[l, CH_AQ + c] = fm(W[:, oq + c * 128: oq + (c + 1) * 128])
            cols = np.concatenate([np.arange(obq + c * 64, obq + (c + 1) * 64), np.arange(obq + (c + 4) * 64, obq + (c + 5) * 64)])
            wfm[l, CH_BQ + c] = fm(W[:, cols])
            wfm[l, CH_CC + c] = fm(W[:, occ + c * 128: occ + (c + 1) * 128])
            wfm[l, CH_CU + c] = fm(W[:, ocu + c * 128: ocu + (c + 1) * 128])
            wfm[l, CH_CB + c] = fm(W[:, ocb + c * 128: ocb + (c + 1) * 128])
        wfm[l, CH_BK] = fm(W[:, obk: obk + 128])
        for m in range(8):
            for b in range(3):
                wfm[l, CH_G + 3 * m + b] = fm(W[:, og + b * 1024 + m * 128: og + b * 1024 + (m + 1) * 128])
        wv[l, :, :, 0:512] = W[:, ov:ov + 512].reshape(8, 128, 512).transpose(1, 0, 2)
        wv[l, :, :, 512:640] = W[:, obv:obv + 128].reshape(8, 128, 128).transpose(1, 0, 2)
        wa = np.asarray(inp["w_branch_a"], np.float32)[l]
        wb = np.asarray(inp["w_branch_b"], np.float32)[l]
        wc = np.asarray(inp["w_branch_c"], np.float32)[l]
        perm = np.concatenate([np.concatenate([np.arange(c * 64, (c + 1) * 64), np.arange((c + 4) * 64, (c + 5) * 64)]) for c in range(4)])
        wbp = wb[perm]
        for m in range(8):
            wfm[l, CH_WAB + m, :, 0:4, :] = wa[:, m * 128:(m + 1) * 128].reshape(4, 128, 128).transpose(1, 0, 2)
            wfm[l, CH_WAB + m, :, 4:8, :] = wbp[:, m * 128:(m + 1) * 128].reshape(4, 128, 128).transpose(1, 0, 2)
            wfm[l, CH_WC + m, :, 0:4, :] = wc[:, m * 128:(m + 1) * 128].reshape(4, 128, 128).transpose(1, 0, 2)
            wfm[l, CH_WOUT + m] = fm(np.asarray(inp["w_out"], np.float32)[l][:, m * 128:(m + 1) * 128])
        gu = np.asarray(inp["w_ffn_gu"], np.float32)[l]
        for j in range(NJ):
            wfm[l, CH_GU + 2 * j] = fm(gu[:, j * 128:(j + 1) * 128])
            wfm[l, CH_GU + 2 * j + 1] = fm(gu[:, DFF + j * 128: DFF + (j + 1) * 128])
        wd = np.asarray(inp["w_ffn_down"], np.float32)[l]
        for nn in range(8):
            blk = wd[:, nn * 128:(nn + 1) * 128].reshape(NJ, 128, 128).transpose(1, 0, 2)
            for i in range(3):
                nt = 8 if i < 2 else 6
                wfm[l, CH_WD + 3 * nn + i, :, 0:nt, :] = blk[:, i * 8:i * 8 + nt, :]
    wmod = np.asarray(inp["w_mod"], np.float32).reshape(NL, 8, 128, 48, 128).transpose(0, 3, 2, 1, 4)
    bmod = np.asarray(inp["b_mod"], np.float32).reshape(NL, 48, 128).transpose(2, 0, 1)
    vecs = np.zeros((128, NV + NVL), np.float32)
    for l in range(NL):
        vecs[:, V_N1G + l * 8:V_N1G + l * 8 + 8] = np.asarray(inp["norm1_g"], np.float32)[l].reshape(8, 128).T
        vecs[:, V_N2G + l * 8:V_N2G + l * 8 + 8] = np.asarray(inp["norm2_g"], np.float32)[l].reshape(8, 128).T
        cw = np.asarray(inp["conv_w"], np.float32)[l]
        for k in range(3):
            vecs[:, V_CONV + l * 12 + k * 4: V_CONV + l * 12 + k * 4 + 4] = cw[k].reshape(4, 128).T
        vecs[:, V_SUB + l] = np.asarray(inp["diff_subln_g"], np.float32)[l]
        pidx = np.arange(128) % 64
        for col, name in ((V_QG, "q_norm_g"), (V_KG, "k_norm_g")):
            g = np.asarray(inp[name], np.float32)[l]
            vecs[:, col + l * 2] = g[pidx]
            vecs[:, col + l * 2 + 1] = g[pidx ^ 16]
        for a, (n1, n2) in enumerate((("lam_q1", "lam_k1"), ("lam_q2", "lam_k2"))):
            base = V_LAM + l * 256 + a * 128
            vecs[:, base:base + 64] = np.asarray(inp[n1], np.float32)[l][None, :]
            vecs[:, base + 64:base + 128] = np.asarray(inp[n2], np.float32)[l][None, :]
    vecs[:, V_FG:V_FG + 8] = np.asarray(inp["final_g"], np.float32).reshape(8, 128).T
    rows = SEQ // GRID_W
    row = np.repeat(np.arange(rows, dtype=np.float32), GRID_W)
    col = np.tile(np.arange(GRID_W, dtype=np.float32), rows)
    nf = 16
    inv_freq = (np.float32(10000.0) ** (-np.arange(nf, dtype=np.float32) / np.float32(nf))).astype(np.float32)
    rope = np.zeros((128, 2, SEQ), np.float32)
    for p in range(128):
        d = p % 64
        axis, half, f = d // 32, (d // 16) % 2, d % 16
        ang = ((row if axis == 0 else col) * inv_freq[f]).astype(np.float32)
        rope[p, 0] = np.cos(ang)
        rope[p, 1] = np.sin(ang) * (-1.0 if half == 0 else 1.0)
    cst = np.zeros((128, 5, 128), np.float32)
    cst[:, 0] = np.eye(128, dtype=np.float32)
    for p in range(128):
        cst[p ^ 16, 1, p] = 1.0
    cst[0:64, 2, 0:64] = 1.0 / 64
    cst[64:128, 2, 64:128] = 1.0 / 64
    cst[:, 3] = 1.0 / 1024
    cst[:, 4] = 1.0 / 128
    return dict(wfm=np.ascontiguousarray(wfm.reshape(NL, NCHUNK, 128, 1024)),
                wv=np.ascontiguousarray(wv.reshape(NL, 128, 5120)),
                wmod=np.ascontiguousarray(wmod.reshape(NL, 48, 128, 1024)),
                bmod=np.ascontiguousarray(bmod.reshape(128, NL * 48)),
                vecs=vecs, rope=np.ascontiguousarray(rope.reshape(128, 2 * SEQ)),
                cst=np.ascontiguousarray(cst.reshape(128, 640)))


def _core_maps(shared, x, ctx, c, c_ctx, nseq, ncores):
    maps = []
    for i in range(ncores):
        cT = np.zeros((128, 8, 4), np.float32)
        for s_ in range(nseq):
            cT[:, :, s_] = c[i * nseq + s_].reshape(8, 128).T
        cT[:, :, 2] = c_ctx.reshape(8, 128).T
        m = dict(shared)
        m["x"] = np.ascontiguousarray(x[i * nseq:(i + 1) * nseq])
        m["ctx"] = np.ascontiguousarray(ctx[i * nseq:(i + 1) * nseq])
        m["cT"] = np.ascontiguousarray(cT.reshape(128, 32))
        maps.append(m)
    return maps


FUSED = True


def kernel(**inp):
    x = np.asarray(inp["x"], np.float32)
    ctx = np.asarray(inp["ctx"], np.float32)
    c = np.asarray(inp["c"], np.float32)
    c_ctx = np.asarray(inp["c_ctx"], np.float32)
    shared = _prep_weights(inp)
    cores = list(range(NCORES))
    if FUSED:
        nc = build_program([0, 1], True, False)
        res = run_bass_kernel_spmd(nc, _core_maps(shared, x, ctx, c, c_ctx, NSEQ, NCORES), core_ids=cores)
        return np.concatenate([np.asarray(r["out"], np.float32) for r in res.results], axis=0)
    nc0 = build_program([0], False, True)
    res = run_bass_kernel_spmd(nc0, _core_maps(shared, x, ctx, c, c_ctx, NSEQ, NCORES), core_ids=cores)
    x1 = np.concatenate([np.asarray(r["out"], np.float32) for r in res.results], axis=0)
    ctx1 = np.concatenate([np.asarray(r["ctx_out"], np.float32) for r in res.results], axis=0)
    nc1 = build_program([1], True, False)
    res = run_bass_kernel_spmd(nc1, _core_maps(shared, x1, ctx1, c, c_ctx, NSEQ, NCORES), core_ids=cores)
    return np.concatenate([np.asarray(r["out"], np.float32) for r in res.results], axis=0)
```
